# Optimizing a Trainium2 kernel written in Bass

```python
import math
import jax, jax.numpy as jnp
from jax import lax
import numpy as np

D_MODEL = 1024
BATCH = 1
SEQ = 16384
DEPTH = 1

MEM_LEN = 256
HEAD_DIM = 64
MIX_WIDTH = D_MODEL
DA_WIDTH = MIX_WIDTH // 2
SB_WIDTH = MIX_WIDTH - DA_WIDTH
DA_HEADS = DA_WIDTH // (2 * HEAD_DIM)
DA_V = 2 * HEAD_DIM
SB_HEADS = SB_WIDTH // HEAD_DIM
DA_Q_COLS = DA_HEADS * 2 * HEAD_DIM
DA_K_COLS = DA_HEADS * 2 * HEAD_DIM
DA_V_COLS = DA_HEADS * DA_V
SB_COLS = SB_HEADS * HEAD_DIM
IN_COLS = DA_Q_COLS + DA_K_COLS + DA_V_COLS + 3 * SB_COLS
ROT_DIM = HEAD_DIM // 4
ROPE_THETA = 500000.0
X_HEADS = 4
X_HEAD_DIM = D_MODEL // X_HEADS
D_FF = 4 * D_MODEL
Q_BLOCK = 128
EPS = 1e-6

kernel_name = "hybrid_diffattn_stickbreaking_block"


def rms_norm(x, g):
    xf = x.astype(jnp.float32)
    y = xf * lax.rsqrt(jnp.mean(xf * xf, axis=-1, keepdims=True) + EPS)
    return (y * g.astype(jnp.float32)).astype(x.dtype)


def rope_tables(positions, dtype):
    inv_freq = ROPE_THETA ** (-jnp.arange(0, ROT_DIM, 2, dtype=jnp.float32) / ROT_DIM)
    ang = positions.astype(jnp.float32)[..., None] * inv_freq
    return jnp.cos(ang).astype(dtype), jnp.sin(ang).astype(dtype)


def apply_partial_rope(x, cos, sin):
    extra = x.ndim - 3
    c = cos.reshape(cos.shape[:2] + (1,) * extra + cos.shape[-1:])
    s = sin.reshape(sin.shape[:2] + (1,) * extra + sin.shape[-1:])
    half = ROT_DIM // 2
    x1, x2, xp = x[..., :half], x[..., half:ROT_DIM], x[..., ROT_DIM:]
    return jnp.concatenate([x1 * c - x2 * s, x2 * c + x1 * s, xp], axis=-1)


def diff_attention(q, k, v, lam, g_subln, lam_init):
    B, S, H = q.shape[0], q.shape[1], q.shape[2]
    nb = S // Q_BLOCK
    scale = 1.0 / math.sqrt(HEAD_DIM)
    kh = k.transpose(0, 2, 3, 1, 4)
    vh = v.transpose(0, 2, 1, 3).astype(jnp.float32)
    qb = q.transpose(0, 2, 3, 1, 4).reshape(B, H, 2, nb, Q_BLOCK, HEAD_DIM)
    qb = jnp.moveaxis(qb, 3, 0)
    qidx = jnp.arange(S, dtype=jnp.int32).reshape(nb, Q_BLOCK)
    kidx = jnp.arange(S, dtype=jnp.int32)

    def block(args):
        qblk, qi = args
        s = jnp.einsum('bhcqd,bhckd->bhcqk', qblk, kh).astype(jnp.float32) * scale
        mask = kidx[None, :] <= qi[:, None]
        p = jax.nn.softmax(jnp.where(mask, s, -jnp.inf), axis=-1)
        w = p[:, :, 0] - lam * p[:, :, 1]
        return jnp.einsum('bhqk,bhkv->bhqv', w, vh)

    o = lax.map(block, (qb, qidx))
    o = o.transpose(1, 0, 3, 2, 4).reshape(B, S, H, DA_V)
    o = rms_norm(o, g_subln) * (1.0 - lam_init)
    return o.reshape(B, S, H * DA_V)


def stick_breaking_attention(q, k, v):
    B, S, H = q.shape[0], q.shape[1], q.shape[2]
    nb = S // Q_BLOCK
    scale = 1.0 / math.sqrt(HEAD_DIM)
    kh = k.transpose(0, 2, 1, 3)
    vh = v.transpose(0, 2, 1, 3).astype(jnp.float32)
    qb = q.transpose(0, 2, 1, 3).reshape(B, H, nb, Q_BLOCK, HEAD_DIM)
    qb = jnp.moveaxis(qb, 2, 0)
    qidx = jnp.arange(S, dtype=jnp.int32).reshape(nb, Q_BLOCK)
    kidx = jnp.arange(S, dtype=jnp.int32)

    def block(args):
        qblk, qi = args
        z = jnp.einsum('bhqd,bhkd->bhqk', qblk, kh).astype(jnp.float32) * scale
        strict = kidx[None, :] < qi[:, None]
        log_beta = jax.nn.log_sigmoid(z)
        log_1mb = jnp.where(strict, jax.nn.log_sigmoid(-z), 0.0)
        between = lax.cumsum(log_1mb, axis=3, reverse=True) - log_1mb
        a = jnp.exp(jnp.where(strict, log_beta + between, -jnp.inf))
        return jnp.einsum('bhqk,bhkd->bhqd', a, vh)

    o = lax.map(block, (qb, qidx))
    return o.transpose(1, 0, 3, 2, 4).reshape(B, S, H * HEAD_DIM)


def cross_attention(h, memn, w_xq, w_xkv, w_xo):
    B, S, _ = h.shape
    q = (h @ w_xq).reshape(B, S, X_HEADS, X_HEAD_DIM)
    kv = memn @ w_xkv
    k = kv[..., :D_MODEL].reshape(B, MEM_LEN, X_HEADS, X_HEAD_DIM)
    v = kv[..., D_MODEL:].reshape(B, MEM_LEN, X_HEADS, X_HEAD_DIM).astype(jnp.float32)
    s = jnp.einsum('bqhd,bkhd->bhqk', q, k).astype(jnp.float32) / math.sqrt(X_HEAD_DIM)
    p = jax.nn.softmax(s, axis=-1)
    o = jnp.einsum('bhqk,bkhd->bqhd', p, v).reshape(B, S, D_MODEL).astype(h.dtype)
    return o @ w_xo


def setup_inputs(seed: int = 0) -> dict:
    key = jax.random.key(seed)
    ks = jax.random.split(key, 24)
    f32 = jnp.float32

    def w(k, shape, fan_in):
        return jax.random.normal(k, shape, f32) * (fan_in ** -0.5)

    def gain(k, shape):
        return 1.0 + 0.02 * jax.random.normal(k, shape, f32)

    L = DEPTH
    return {
        "x": jax.random.normal(ks[0], (BATCH, SEQ, D_MODEL), f32),
        "mem": jax.random.normal(ks[1], (BATCH, MEM_LEN, D_MODEL), f32),
        "positions": jnp.broadcast_to(jnp.arange(SEQ, dtype=jnp.int32), (BATCH, SEQ)),
        "g_mix": gain(ks[2], (L, D_MODEL)),
        "w_in": w(ks[3], (L, D_MODEL, IN_COLS), D_MODEL),
        "lambda_q1": 0.1 * jax.random.normal(ks[4], (L, HEAD_DIM), f32),
        "lambda_k1": 0.1 * jax.random.normal(ks[5], (L, HEAD_DIM), f32),
        "lambda_q2": 0.1 * jax.random.normal(ks[6], (L, HEAD_DIM), f32),
        "lambda_k2": 0.1 * jax.random.normal(ks[7], (L, HEAD_DIM), f32),
        "g_subln": gain(ks[8], (L, DA_V)),
        "w_out": w(ks[9], (L, MIX_WIDTH, D_MODEL), MIX_WIDTH),
        "g_cross": gain(ks[10], (L, D_MODEL)),
        "g_mem": gain(ks[11], (L, D_MODEL)),
        "w_xq": w(ks[12], (L, D_MODEL, D_MODEL), D_MODEL),
        "w_xkv": w(ks[13], (L, D_MODEL, 2 * D_MODEL), D_MODEL),
        "w_xo": w(ks[14], (L, D_MODEL, D_MODEL), D_MODEL),
        "g_mlp": gain(ks[15], (L, D_MODEL)),
        "w_up": w(ks[16], (L, D_MODEL, D_FF), D_MODEL),
        "w_down": w(ks[17], (L, D_FF, D_MODEL), D_FF),
        "g_final": gain(ks[18], (D_MODEL,)),
    }


def reference(x, mem, positions, g_mix, w_in, lambda_q1, lambda_k1, lambda_q2, lambda_k2,
              g_subln, w_out, g_cross, g_mem, w_xq, w_xkv, w_xo, g_mlp, w_up, w_down,
              g_final):
    B, S, _ = x.shape
    cos, sin = rope_tables(positions, x.dtype)
    h = x
    for l in range(DEPTH):
        lam_init = 0.8 - 0.6 * math.exp(-0.3 * l)
        u = rms_norm(h, g_mix[l]) @ w_in[l]
        o0 = 0
        qa = u[..., o0:o0 + DA_Q_COLS].reshape(B, S, DA_HEADS, 2, HEAD_DIM); o0 += DA_Q_COLS
        ka = u[..., o0:o0 + DA_K_COLS].reshape(B, S, DA_HEADS, 2, HEAD_DIM); o0 += DA_K_COLS
        va = u[..., o0:o0 + DA_V_COLS].reshape(B, S, DA_HEADS, DA_V); o0 += DA_V_COLS
        qs = u[..., o0:o0 + SB_COLS].reshape(B, S, SB_HEADS, HEAD_DIM); o0 += SB_COLS
        ksb = u[..., o0:o0 + SB_COLS].reshape(B, S, SB_HEADS, HEAD_DIM); o0 += SB_COLS
        vs = u[..., o0:o0 + SB_COLS].reshape(B, S, SB_HEADS, HEAD_DIM)
        qa = apply_partial_rope(qa, cos, sin)
        ka = apply_partial_rope(ka, cos, sin)
        lam = (jnp.exp(jnp.sum(lambda_q1[l].astype(jnp.float32) * lambda_k1[l].astype(jnp.float32)))
               - jnp.exp(jnp.sum(lambda_q2[l].astype(jnp.float32) * lambda_k2[l].astype(jnp.float32)))
               + lam_init)
        ya = diff_attention(qa, ka, va, lam, g_subln[l], lam_init)
        yb = stick_breaking_attention(qs, ksb, vs)
        y = jnp.concatenate([ya.astype(h.dtype), yb.astype(h.dtype)], axis=-1)
        h = h + y @ w_out[l]
        h = h + cross_attention(rms_norm(h, g_cross[l]), rms_norm(mem, g_mem[l]),
                                w_xq[l], w_xkv[l], w_xo[l])
        a = jnp.square(jax.nn.relu(rms_norm(h, g_mlp[l]) @ w_up[l]))
        h = h + a @ w_down[l]
    return rms_norm(h, g_final)
```

```python
import math
import numpy as np
import ml_dtypes
import concourse.bass as bass
import concourse.mybir as mybir
from concourse.bass_utils import run_bass_kernel_spmd

F32 = mybir.dt.float32
BF16 = mybir.dt.bfloat16
I32 = mybir.dt.int32
AF = mybir.ActivationFunctionType
ALU = mybir.AluOpType
AX = mybir.AxisListType

NCORES = 8
S = 16384
D = 1024
OWN = S // NCORES
NCH_ALL = S // 512
NCH_OWN = OWN // 512
EPS = 1e-6
TWO_PI = 2.0 * math.pi

DEBUG = False
import os
MAXPH = int(os.environ.get('KMAXPH', '4'))
NUNITS = int(os.environ.get('KNUNITS', '8'))


class Buf:
    def __init__(self, name, multi=False):
        self.name = name
        self.multi = multi
        self.writes = {}
        self.reads = {}
        self.sem = None
        self.cnt = 0


class Prog:
    COMPUTE = ("pe", "act", "dve", "pool")
    ALL = ("sp", "pe", "act", "dve", "pool")

    def __init__(self):
        self.ops = {e: [] for e in self.ALL}
        self.seen = {e: {} for e in self.ALL}
        self.bufs = []
        self.dma_owners = []

    def buf(self, name, multi=False):
        b = Buf(name, multi)
        self.bufs.append(b)
        return b

    @staticmethod
    def _merge(dst, src):
        for k, v in src.items():
            if dst.get(k, -1) < v:
                dst[k] = v

    def _waits(self, eng, reads, writes):
        need = {}
        for b in reads:
            self._merge(need, b.writes)
        for b in writes:
            self._merge(need, b.reads)
            if not b.multi:
                self._merge(need, b.writes)
        out = []
        seen = self.seen[eng]
        for k, v in need.items():
            if k[0] == "c" and k[1] == eng and eng in ("pe", "sp"):
                continue
            if seen.get(k, -1) >= v:
                continue
            seen[k] = v
            out.append((k, v))
        return out

    def _commit(self, tokkey, tokval, reads, writes):
        for b in reads:
            if b.reads.get(tokkey, -1) < tokval:
                b.reads[tokkey] = tokval
        for b in writes:
            if b.multi:
                if b.writes.get(tokkey, -1) < tokval:
                    b.writes[tokkey] = tokval
            else:
                b.reads = {}
                b.writes = {tokkey: tokval}

    def op(self, eng, fn, reads=(), writes=()):
        waits = self._waits(eng, reads, writes)
        idx = len(self.ops[eng])
        self.ops[eng].append(dict(fn=fn, waits=waits, dma=None))
        self._commit(("c", eng), idx, reads, writes)

    def dma(self, q, fn, owner, reads=(), writes=()):
        if q == "pool":
            if getattr(owner, "twin", None) is None:
                owner.twin = Buf(owner.name + "_sw")
            owner = owner.twin
        waits = self._waits(q, reads, writes)
        if owner.cnt == 0 and owner not in self.dma_owners:
            self.dma_owners.append(owner)
        owner.cnt += 16
        self.ops[q].append(dict(fn=fn, waits=waits, dma=owner))
        self._commit(("d", id(owner), owner), owner.cnt, reads, writes)

    def barrier(self):
        toks = {}
        for e in self.COMPUTE:
            real = [i for i, o in enumerate(self.ops[e]) if o["fn"] is not None and o["dma"] is None]
            if real:
                toks[("c", e)] = real[-1]
        for b in self.dma_owners:
            toks[("d", id(b), b)] = b.cnt
        for e in self.ALL:
            waits = []
            seen = self.seen[e]
            for k, v in toks.items():
                if k[0] == "c" and k[1] == e and e in ("pe", "sp"):
                    continue
                if seen.get(k, -1) >= v:
                    continue
                seen[k] = v
                waits.append((k, v))
            if waits:
                self.ops[e].append(dict(fn=None, waits=waits, dma=None))
        for b in self.bufs:
            b.reads = {}
            b.writes = {}

    def emit(self, nc, block, es):
        need_inc = {e: set() for e in self.COMPUTE}
        for e in self.ALL:
            for o in self.ops[e]:
                for k, v in o["waits"]:
                    if k[0] == "c":
                        need_inc[k[1]].add(v)
        semval = {}
        for e in self.COMPUTE:
            cnt = 0
            vals = []
            for i in range(len(self.ops[e])):
                if i in need_inc[e]:
                    cnt += 1
                vals.append(cnt)
            semval[e] = vals
        sems = {e: es.enter_context(nc.semaphore("sem_" + e)) for e in self.COMPUTE}
        print("n dma semaphores", len(self.dma_owners))
        for b in self.dma_owners:
            b.sem = es.enter_context(nc.semaphore("dsem_" + b.name))

        def run(ename):
            def body(eng):
                for i, o in enumerate(self.ops[ename]):
                    for k, v in o["waits"]:
                        if k[0] == "c":
                            eng.wait_ge(sems[k[1]], semval[k[1]][v])
                        else:
                            eng.wait_ge(k[2].sem, v)
                    if o["fn"] is None:
                        continue
                    ins = o["fn"](eng)
                    if o["dma"] is not None:
                        ins.then_inc(o["dma"].sem, 16)
                    elif i in need_inc.get(ename, ()):
                        ins.then_inc(sems[ename], 1)
            return body

        block.sync(run("sp"))
        block.tensor(run("pe"))
        block.scalar(run("act"))
        block.vector(run("dve"))
        block.gpsimd(run("pool"))


def chain(P, eng, fns, reads, writes, link):
    for f in fns:
        P.op(eng, f, reads=list(reads) + [link], writes=list(writes) + [link])


class Ring:
    def __init__(self, P, name, n):
        self.bufs = [P.buf(f"{name}{i}") for i in range(n)]
        self.n = n
        self.i = 0

    def next(self):
        k = self.i % self.n
        self.i += 1
        return k, self.bufs[k]


def build_program(debug=False):
    nc = bass.Bass("TRN2", target_bir_lowering=False)
    P = Prog()

    def din(name, shape, dt=F32):
        return nc.dram_tensor(name, list(shape), dt, kind="ExternalInput").ap()

    x_all = din("x_all", [S, D])
    x_own = din("x_own", [OWN, D])
    pos_all = din("pos_all", [1, S], I32)
    pos_own = din("pos_own", [1, OWN], I32)
    wk_d = din("wk", [D, 2560])
    wq_d = din("wq", [D, 1536])
    wout_d = din("w_out", [D, D])
    wxq_d = din("w_xq", [D, D])
    wxkv_d = din("w_xkv", [D, 2 * D])
    wxo_d = din("w_xo", [D, D])
    wup_d = din("w_up", [D, 4 * D])
    wdn_d = din("w_down", [4 * D, D])
    mem_d = din("mem", [256, D])
    gcols_d = din("gcols", [128, 32])
    gfin_d = din("gfin", [128, D])
    gsub_d = din("gsub", [128, 128])
    lam_d = din("lam", [128, 256])
    ropec_d = din("ropec", [128, 2])
    masks_d = din("masks", [128, 32])
    cbf_d = din("cbf", [128, 512], BF16)
    out_d = nc.dram_tensor("out", [OWN, D], F32, kind="ExternalOutput").ap()

    kT_scr = nc.dram_tensor("kT_scr", [8, 128, S], BF16).ap()
    v_scr = nc.dram_tensor("v_scr", [8, 128, S], BF16).ap()
    q_scr = nc.dram_tensor("q_scr", [8, 128, OWN], BF16).ap()
    y_scr = nc.dram_tensor("y_scr", [8, 128, OWN], BF16).ap()
    h_scr = nc.dram_tensor("h_scr", [OWN, D], F32).ap()
    wbf = {
        "wo": nc.dram_tensor("wbf_wo", [128, 8, 1024], BF16).ap(),
        "wxq": nc.dram_tensor("wbf_wxq", [128, 8, 1024], BF16).ap(),
        "wxo": nc.dram_tensor("wbf_wxo", [128, 8, 1024], BF16).ap(),
        "wkv": nc.dram_tensor("wbf_wkv", [128, 8, 2048], BF16).ap(),
        "wup": nc.dram_tensor("wbf_wup", [128, 8, 4096], BF16).ap(),
        "wdn": nc.dram_tensor("wbf_wdn", [128, 32, 1024], BF16).ap(),
    }
    dbg = {}
    if debug:
        dbg["kT"] = nc.dram_tensor("dbg_kT", [8, 128, 1024], BF16, kind="ExternalOutput").ap()
        dbg["v"] = nc.dram_tensor("dbg_v", [8, 128, 1024], BF16, kind="ExternalOutput").ap()
        dbg["q"] = nc.dram_tensor("dbg_q", [8, 128, OWN], BF16, kind="ExternalOutput").ap()
        dbg["y"] = nc.dram_tensor("dbg_y", [8, 128, OWN], BF16, kind="ExternalOutput").ap()
        dbg["h"] = nc.dram_tensor("dbg_h", [OWN, D], F32, kind="ExternalOutput").ap()

    B_kT = P.buf("kT_scr", multi=True)
    B_v = P.buf("v_scr", multi=True)
    B_q = P.buf("q_scr", multi=True)
    B_y = P.buf("y_scr", multi=True)
    B_h = P.buf("h_scr", multi=True)
    B_out = P.buf("out", multi=True)
    B_wbf = P.buf("wbf", multi=True)

    import contextlib
    es = contextlib.ExitStack()
    with es:
        NB = 95 * 1024
        big = es.enter_context(nc.sbuf_tensor("big", [128, NB], BF16))
        pA = es.enter_context(nc.psum_tensor("pA", [128, 1024], F32))
        pB = es.enter_context(nc.psum_tensor("pB", [128, 1024], F32))
        psum = [pA[:, 0:512], pA[:, 512:1024], pB[:, 0:512], pB[:, 512:1024]]
        psum2 = [es.enter_context(nc.psum_tensor(f"pq{i}", [128, 1024], F32)) for i in range(2)]
        psT = pB[:, 512:1024].bitcast(BF16).rearrange("p (k t) -> p k t", k=8)
        psT2 = pB[:, 0:512].bitcast(BF16).rearrange("p (k t) -> p k t", k=8)

        class Alloc:
            def __init__(self):
                self.o = 0

            def get(self, size, shape=None, dt=BF16):
                mul = 1 if dt == BF16 else 2
                n = size * mul
                self.o = (self.o + 1) // 2 * 2
                assert self.o + n <= NB, ("sbuf overflow", self.o, n, NB)
                ap = big[:, self.o:self.o + n]
                self.o += n
                if dt != BF16:
                    ap = ap.bitcast(dt)
                if shape is not None:
                    names = " ".join(f"d{i}" for i in range(len(shape)))
                    kw = {f"d{i}": shape[i] for i in range(len(shape))}
                    ap = ap.rearrange(f"p ({names}) -> p {names}", **kw)
                return ap

            def mark(self):
                return self.o

            def reset(self, o):
                self.o = o

        abf = Alloc()

        class AF32:
            def get(self, size, shape=None):
                return abf.get(size, shape, F32)

            def mark(self):
                return 0

            def reset(self, o):
                pass
        af = AF32()

        cbf = abf.get(512)
        ident = cbf[:, 0:128]
        tri = cbf[:, 128:256]
        slt = cbf[:, 256:384]
        ones_bf = cbf[:, 384:512]
        gcols = af.get(32)
        gsub = af.get(128)
        lamt = af.get(256)
        ropec = af.get(2)
        masks = af.get(32)
        small = af.get(64)
        nlam = small[:, 0:1]
        epsc = small[:, 17:18]
        B_const = P.buf("const")
        B_small = P.buf("small")

        def ld_const(dst, src):
            P.dma("sp", lambda e, dst=dst, src=src: e.dma_start(out=dst, in_=src), B_const, writes=[B_const])

        P.op("pool", lambda e: e.memset(epsc, EPS), writes=[B_small])
        ld_const(cbf, cbf_d[:, :])
        ld_const(gcols, gcols_d[:, :])
        ld_const(gsub, gsub_d[:, :])
        ld_const(lamt, lam_d[:, :])
        ld_const(ropec, ropec_d[:, :])
        ld_const(masks, masks_d[:, :])
        mask_sb = masks[:, 0:16]
        mask_da = masks[:, 16:32]

        lam_init = 0.8 - 0.6 * math.exp(-0.3 * 0)
        tmp64 = af.get(128)

        chain(P, "dve", [
            lambda e: e.tensor_tensor(out=tmp64[:, 0:64], in0=lamt[:, 0:64], in1=lamt[:, 64:128], op=ALU.mult),
            lambda e: e.tensor_tensor(out=tmp64[:, 64:128], in0=lamt[:, 128:192], in1=lamt[:, 192:256], op=ALU.mult),
            lambda e: e.tensor_reduce(out=small[:, 1:2], in_=tmp64[:, 0:64], axis=AX.X, op=ALU.add),
            lambda e: e.tensor_reduce(out=small[:, 2:3], in_=tmp64[:, 64:128], axis=AX.X, op=ALU.add),
        ], [B_const], [], B_small)
        P.op("act", lambda e: e.activation(out=small[:, 3:5], in_=small[:, 1:3], func=AF.Exp),
             reads=[B_small], writes=[B_small])
        chain(P, "dve", [
            lambda e: e.tensor_tensor(out=small[:, 5:6], in0=small[:, 4:5], in1=small[:, 3:4], op=ALU.subtract),
            lambda e: e.tensor_scalar(out=nlam, in0=small[:, 5:6], scalar1=-lam_init, scalar2=None, op0=ALU.add),
            lambda e: e.tensor_scalar(out=gsub, in0=gsub, scalar1=(1.0 - lam_init), scalar2=None, op0=ALU.mult),
        ], [], [B_const], B_small)

        def load_weight(dst, src, K, N, gcol0, stage, stage_ring):
            nk = K // 128
            t = 0
            PW = stage[0].shape[1]
            for k in range(nk):
                for n0 in range(0, N, PW):
                    n1 = min(N, n0 + PW)
                    si, sb = stage_ring.next()
                    st = stage[si][:, 0:n1 - n0]
                    P.dma("sp", lambda e, st=st, k=k, n0=n0, n1=n1: e.dma_start(
                        out=st, in_=src[k * 128:(k + 1) * 128, n0:n1]), sb, writes=[sb])
                    eng = "dve" if (t % 2 == 0) else "pool"
                    t += 1
                    if gcol0 is None:
                        P.op(eng, lambda e, st=st, k=k, n0=n0, n1=n1: e.tensor_copy(out=dst[:, k, n0:n1], in_=st),
                             reads=[sb], writes=[B_w])
                    else:
                        P.op(eng, lambda e, st=st, k=k, n0=n0, n1=n1: e.tensor_scalar(
                            out=dst[:, k, n0:n1], in0=st, scalar1=gcols[:, gcol0 + k:gcol0 + k + 1],
                            scalar2=None, op0=ALU.mult), reads=[sb, B_const], writes=[B_w])

        B_w = P.buf("weights", multi=True)

        def norm_transpose(xt_ap, xt_buf, dstT, dst_buf, col0, scr, ps_t, ps_buf, tag):
            junk, ss, rr, xs, B_s = scr
            P.op("pool", lambda e: e.memset(ss, 0.0), writes=[B_s])
            chain(P, "act", [
                lambda e: e.activation(out=junk, in_=xt_ap, func=AF.Square, accum_out=ss),
                lambda e: e.activation(out=rr, in_=ss, func=AF.Ln, bias=epsc, scale=1.0 / D),
                lambda e: e.activation(out=rr, in_=rr, func=AF.Exp, scale=-0.5),
            ], [xt_buf, B_small], [], B_s)
            P.op("dve", lambda e: e.tensor_scalar(out=xs, in0=xt_ap, scalar1=rr, scalar2=None, op0=ALU.mult),
                 reads=[xt_buf, B_s], writes=[B_s])

            def ft(e):
                ins = None
                for k in range(8):
                    ins = e.transpose(out=ps_t[:, k, :], in_=xs[:, k * 128:(k + 1) * 128], identity=ident)
                return ins
            P.op("pe", ft, reads=[B_s, B_const], writes=[ps_buf])
            P.op("act", lambda e: e.activation(out=dstT[:, :, col0:col0 + 128], in_=ps_t, func=AF.Copy),
                 reads=[ps_buf], writes=[dst_buf])


        m_bf = abf.mark()
        m_f = af.mark()
        Wk = abf.get(8 * 2560, [8, 2560])
        Wq = abf.get(8 * 1536, [8, 1536])
        stage = [af.get(2048), af.get(2048)]
        stage_ring = Ring(P, "wstage", 2)
        load_weight(Wk, wk_d, D, 2560, 0, stage, stage_ring)
        load_weight(Wq, wq_d, D, 1536, 0, stage, stage_ring)

        junk = abf.get(1024)
        xs = None
        ssr = small[:, 8:9]
        rrr = small[:, 9:10]
        B_scr = P.buf("normscr")
        scr = (junk, ssr, rrr, xs, B_scr)
        xnT = [abf.get(8 * 512, [8, 512]) for _ in range(2)]
        xnT_ring = Ring(P, "xnT", 2)
        posi = abf.get(512, None, I32)
        ki_t = abf.get(512, None, I32)
        kf_t = af.get(512)
        B_posi = P.buf("posi")
        posf = af.get(512)
        tang = af.get(512)
        tang2 = af.get(512)
        B_rope = P.buf("rope")
        B_tab = P.buf("ropetab")
        t1 = af.get(512)
        t2 = af.get(512)
        B_t12 = P.buf("t12")
        kst = [abf.get(8 * 512, [8, 512]) for _ in range(2)]
        kst_ring = Ring(P, "kst", 2)
        vst = [abf.get(8 * 512, [8, 4, 128]) for _ in range(2)]
        vst_ring = Ring(P, "vst", 2)
        B_ps = [P.buf(f"ps{i}") for i in range(4)]
        B_psT = B_ps[3]
        B_pq = [P.buf(f"pq{i}") for i in range(2)]

        junk_p1 = junk
        xs4 = [abf.get(1024) for _ in range(4)]
        B_xs4 = [P.buf(f"xs4_{i}") for i in range(4)]
        xt3 = [af.get(1024) for _ in range(4)]
        xt3_ring = Ring(P, "xt3", 4)
        tabs = [(af.get(512), af.get(512)) for _ in range(2)]
        B_tabs = [P.buf("tab0"), P.buf("tab1")]
        t12 = [(t1, t2), (af.get(512), af.get(512))]
        B_t12s = [B_t12, P.buf("t12b")]

        def prep_stage_a(xsrc, ci):
            info = []
            for t in range(4):
                ss_ = small[:, 32 + t:33 + t]
                P.op("pool", lambda e, ss_=ss_: e.memset(ss_, 0.0), writes=[B_xs4[t]])
            for t in range(4):
                xi, xb = xt3_ring.next()
                r0 = ci * 512 + t * 128
                P.dma("sp", lambda e, xi=xi, r0=r0: e.dma_start(out=xt3[xi], in_=xsrc[r0:r0 + 128, :]), xb,
                      writes=[xb])
                ss_ = small[:, 32 + t:33 + t]
                rr_ = small[:, 36 + t:37 + t]
                bx = B_xs4[t]
                chain(P, "act", [
                    lambda e, xi=xi, ss_=ss_: e.activation(out=junk_p1, in_=xt3[xi], func=AF.Square, accum_out=ss_),
                    lambda e, ss_=ss_, rr_=rr_: e.activation(out=rr_, in_=ss_, func=AF.Ln, bias=epsc, scale=1.0 / D),
                    lambda e, rr_=rr_: e.activation(out=rr_, in_=rr_, func=AF.Exp, scale=-0.5),
                ], [xb, B_small], [B_scr], bx)
                info.append((xi, xb, rr_, bx))
            return info

        def prep_stage_b(info, t):
            xi, xb, rr_, bx = info[t]
            P.op("dve", lambda e: e.tensor_scalar(out=xs4[t], in0=xt3[xi], scalar1=rr_, scalar2=None, op0=ALU.mult),
                 reads=[xb], writes=[bx])

        def transposes(si, sb):
            for t in range(4):
                pt, ptb = (psT, B_ps[3]) if t % 2 == 0 else (psT2, B_ps[2])

                def ft(e, t=t, pt=pt):
                    ins = None
                    for k in range(8):
                        ins = e.transpose(out=pt[:, k, :], in_=xs4[t][:, k * 128:(k + 1) * 128], identity=ident)
                    return ins
                P.op("pe", ft, reads=[B_xs4[t], B_const], writes=[ptb])
                P.op("act", lambda e, t=t, pt=pt: e.activation(out=xnT[si][:, :, t * 128:(t + 1) * 128], in_=pt,
                                                               func=AF.Copy), reads=[ptb], writes=[sb])

        INV2PI = float(1.0 / TWO_PI)

        def rope_ops(possrc, ci, tslot):
            ctab_, stab_ = tabs[tslot]

            def reduce_(dst, shift):
                return [
                    lambda e: e.tensor_scalar(out=dst, in0=posf, scalar1=ropec[:, 0:1], scalar2=shift,
                                              op0=ALU.mult, op1=ALU.add),
                    lambda e: e.tensor_scalar(out=ki_t, in0=dst, scalar1=INV2PI, scalar2=None, op0=ALU.mult),
                    lambda e: e.tensor_copy(out=kf_t, in_=ki_t),
                    lambda e: e.scalar_tensor_tensor(out=dst, in0=kf_t, scalar=-TWO_PI, in1=dst,
                                                     op0=ALU.mult, op1=ALU.add),
                    lambda e: e.tensor_scalar(out=kf_t, in0=dst, scalar1=float(math.pi), scalar2=None, op0=ALU.is_gt),
                    lambda e: e.scalar_tensor_tensor(out=dst, in0=kf_t, scalar=-TWO_PI, in1=dst,
                                                     op0=ALU.mult, op1=ALU.add),
                    lambda e: e.tensor_scalar(out=dst, in0=dst, scalar1=float(-math.pi), scalar2=float(math.pi),
                                              op0=ALU.max, op1=ALU.min),
                ]
            ops = [lambda e: e.tensor_copy(out=posf, in_=posi)] + reduce_(tang, ropec[:, 1:2]) \
                + reduce_(tang2, float(0.5 * math.pi))

            def chain_all():
                P.dma("sp", lambda e: e.dma_start(
                    out=posi, in_=possrc[0:1, ci * 512:(ci + 1) * 512].partition_broadcast(128)),
                    B_posi, writes=[B_posi])
                chain(P, "dve", ops, [B_posi, B_const], [], B_rope)

            def sin_():
                def g(e):
                    e.activation(out=stab_, in_=tang, func=AF.Sin)
                    return e.activation(out=ctab_, in_=tang2, func=AF.Sin)
                P.op("act", g, reads=[B_rope], writes=[B_tabs[tslot]])
            return chain_all, sin_

        B_pq0a = P.buf("pq0a")
        B_pq0b = P.buf("pq0b")
        B_pq1a = P.buf("pq1a")
        B_pq1b = P.buf("pq1b")

        def proj_qk(W, si, sb, kslot, kb, tslot, hooks, mid):
            xT = xnT[si]
            ctab_, stab_ = tabs[tslot]
            for h in range(4):
                t1_, t2_ = t12[h % 2]
                bt = B_t12s[h % 2]
                if h % 2 == 0:
                    pa_ap, pb_ap, bufs = psum[0], psum[1], [B_ps[0], B_ps[1]]
                else:
                    pa_ap, pb_ap, bufs = psum2[0][:, 0:512], psum2[0][:, 512:1024], [B_pq0a, B_pq0b]

                def fm(e, h=h, pa_ap=pa_ap, pb_ap=pb_ap):
                    ins = None
                    for k in range(8):
                        ins = e.matmul(pa_ap, lhsT=W[:, k, h * 128:(h + 1) * 128], rhs=xT[:, k, :],
                                       start=(k == 0), stop=(k == 7))
                    for k in range(8):
                        ins = e.matmul(pb_ap, lhsT=W[:, k, 512 + h * 128:512 + (h + 1) * 128], rhs=xT[:, k, :],
                                       start=(k == 0), stop=(k == 7))
                    return ins
                P.op("pe", fm, reads=[sb, B_w], writes=bufs)

                def fr(e, pa_ap=pa_ap, pb_ap=pb_ap, t1_=t1_, t2_=t2_):
                    e.tensor_tensor(out=t1_, in0=pa_ap, in1=ctab_, op=ALU.mult)
                    return e.tensor_tensor(out=t2_, in0=pb_ap, in1=stab_, op=ALU.mult)
                P.op("dve", fr, reads=bufs + [B_tabs[tslot]], writes=[bt])
                P.op("pool", lambda e, h=h, t1_=t1_, t2_=t2_: e.tensor_tensor(out=kst[kslot][:, h, :], in0=t1_, in1=t2_,
                                                                            op=ALU.add),
                     reads=[bt], writes=[kb])
                if hooks:
                    hooks[h]()
            if mid:
                mid()
            sbk = [(psum[0], B_ps[0]), (psum[1], B_ps[1]), (psum2[0][:, 0:512], B_pq0a),
                   (psum2[0][:, 512:1024], B_pq0b)]
            for u in range(4):
                pz, pzb = sbk[u]

                def fm(e, u=u, pz=pz):
                    ins = None
                    for k in range(8):
                        ins = e.matmul(pz, lhsT=W[:, k, 1024 + u * 128:1024 + (u + 1) * 128], rhs=xT[:, k, :],
                                       start=(k == 0), stop=(k == 7))
                    return ins
                P.op("pe", fm, reads=[sb, B_w], writes=[pzb])
                P.op("act", lambda e, u=u, pz=pz: e.activation(out=kst[kslot][:, 4 + u, :], in_=pz, func=AF.Copy),
                     reads=[pzb], writes=[kb])

        def proj_v(si, sb, vslot, vb):
            xT = xnT[si]
            for t in range(4):
                for hf in range(2):
                    pv, pvbuf = (psum2[1][:, 0:512], B_pq1a) if hf == 0 else (psum2[1][:, 512:1024], B_pq1b)

                    def fm(e, t=t, hf=hf, pv=pv):
                        ins = None
                        for k in range(8):
                            ins = e.matmul(pv, lhsT=xT[:, k, t * 128:(t + 1) * 128],
                                           rhs=Wk[:, k, 1536 + hf * 512:1536 + (hf + 1) * 512],
                                           start=(k == 0), stop=(k == 7))
                        return ins
                    P.op("pe", fm, reads=[sb, B_w], writes=[pvbuf])
                    if hf == 0:
                        P.op("act", lambda e, t=t, hf=hf, pv=pv: e.activation(
                            out=vst[vslot][:, hf * 4:(hf + 1) * 4, t, :],
                            in_=pv.rearrange("p (u d) -> p u d", u=4), func=AF.Copy),
                            reads=[pvbuf], writes=[vb])
                    else:
                        P.op("dve", lambda e, t=t, hf=hf, pv=pv: e.tensor_copy(
                            out=vst[vslot][:, hf * 4:(hf + 1) * 4, t, :],
                            in_=pv.rearrange("p (u d) -> p u d", u=4)),
                            reads=[pvbuf], writes=[vb])

        def store_units(stage_ap, sbuf_, dst, dbuf, ci):
            P.dma("pool", lambda e: e.dma_start(
                out=dst[:, :, ci * 512:(ci + 1) * 512].rearrange("u p t -> p u t"), in_=stage_ap),
                sbuf_, reads=[sbuf_], writes=[dbuf])

        NKV = int(os.environ.get('KNKV', str(NCH_ALL)))
        jobs = [("kv", x_all, pos_all, ci) for ci in range(NKV)] + \
               [("q", x_own, pos_own, ci) for ci in range(NCH_OWN)]
        if MAXPH < 1:
            jobs = jobs[:0]
        if jobs:
            mode0, xsrc0, possrc0, ci0 = jobs[0]
            info0 = prep_stage_a(xsrc0, ci0)
            for t in range(4):
                prep_stage_b(info0, t)
            rc, rs_ = rope_ops(possrc0, ci0, 0)
            rc()
            rs_()
            nslot = xnT_ring.next()
            transposes(*nslot)
        for ji, (mode, xsrc, possrc, ci) in enumerate(jobs):
            si, sb = nslot
            tslot = ji % 2
            has_next = ji + 1 < len(jobs)
            hooks = None
            if has_next:
                nmode, nxsrc, npossrc, nci = jobs[ji + 1]
                ninfo = prep_stage_a(nxsrc, nci)
                hooks = [(lambda h=h, ninfo=ninfo: prep_stage_b(ninfo, h)) for h in range(4)]
                rc, rs_ = rope_ops(npossrc, nci, (ji + 1) % 2)
            ks, kb = kst_ring.next()
            W = Wk if mode == "kv" else Wq
            proj_qk(W, si, sb, ks, kb, tslot, hooks, (rc if has_next else None))
            store_units(kst[ks], kb, kT_scr if mode == "kv" else q_scr, B_kT if mode == "kv" else B_q, ci)
            if has_next:
                nslot = xnT_ring.next()
                transposes(*nslot)
            if mode == "kv":
                vs, vb = vst_ring.next()
                proj_v(si, sb, vs, vb)
            if has_next:
                rs_()
            if mode == "kv":
                store_units(vst[vs].rearrange("p u t d -> p u (t d)"), vb, v_scr, B_v, ci)

        P.barrier()
        abf.reset(m_bf)
        af.reset(m_f)

        KT = [abf.get(S) for _ in range(2)]
        VV = [abf.get(128 * 130, [128, 130]) for _ in range(2)]
        QT = [abf.get(OWN) for _ in range(2)]
        B_KT = [P.buf("KT0"), P.buf("KT1")]
        B_VV = [P.buf("VV0"), P.buf("VV1")]
        B_QT = [P.buf("QT0"), P.buf("QT1")]
        for s_ in range(2):
            P.op("pool", lambda e, s_=s_: e.memset(VV[s_][:, :, 128:130], 1.0), writes=[B_VV[s_]])
        e_t = [af.get(1024, [2, 512]) for _ in range(3)]
        e_ring = Ring(P, "e", 3)
        L_t = [abf.get(1024, [2, 512]) for _ in range(2)]
        L_ring = Ring(P, "L", 2)
        g_t = [abf.get(1024, [2, 512]) for _ in range(2)]
        g_ring = Ring(P, "g", 2)
        a_t = [abf.get(1024, [2, 512]) for _ in range(3)]
        a_ring = Ring(P, "a", 3)
        yst = [abf.get(512) for _ in range(2)]
        yst_ring = Ring(P, "yst", 2)
        accs = af.get(8 * 129, [8, 129])
        B_accs = P.buf("accs")
        o_t = af.get(128)
        t1d = af.get(128)
        ya = abf.get(128)
        junk2 = abf.get(128)
        B_ep = P.buf("ep")
        onec = small[:, 16:17]
        P.op("pool", lambda e: e.memset(onec, 1.0), writes=[B_small])

        bg_sf = [af.get(1024), af.get(1024)]
        bg_sb = [abf.get(1024), abf.get(1024)]
        B_bgf = [P.buf("bgf0"), P.buf("bgf1")]
        B_bgb = [P.buf("bgb0"), P.buf("bgb1")]
        bg_pieces = []
        for name, src, K, N, gc in (("wo", wout_d, D, 1024, None), ("wxq", wxq_d, D, 1024, 8),
                                    ("wxo", wxo_d, D, 1024, None), ("wkv", wxkv_d, D, 2048, 24),
                                    ("wup", wup_d, D, 4096, 16), ("wdn", wdn_d, 4 * D, 1024, None)):
            for k in range(K // 128):
                for n0 in range(0, N, 1024):
                    bg_pieces.append((name, src, k, n0, gc))
        bg_state = {"i": 0}

        def bg_load(i):
            name, src, k, n0, gc = bg_pieces[i]
            P.dma("sp", lambda e: e.dma_start(out=bg_sf[i % 2], in_=src[k * 128:(k + 1) * 128, n0:n0 + 1024]),
                  B_bgf[i % 2], writes=[B_bgf[i % 2]])

        def bg_conv(j):
            name, src, k, n0, gc = bg_pieces[j]
            if gc is None:
                P.op("dve", lambda e: e.tensor_copy(out=bg_sb[j % 2], in_=bg_sf[j % 2]),
                     reads=[B_bgf[j % 2]], writes=[B_bgb[j % 2]])
            else:
                P.op("dve", lambda e: e.tensor_scalar(out=bg_sb[j % 2], in0=bg_sf[j % 2],
                                                      scalar1=gcols[:, gc + k:gc + k + 1], scalar2=None,
                                                      op0=ALU.mult),
                     reads=[B_bgf[j % 2], B_const], writes=[B_bgb[j % 2]])
            P.dma("pool", lambda e: e.dma_start(out=wbf[name][:, k, n0:n0 + 1024], in_=bg_sb[j % 2]),
                  B_bgb[j % 2], reads=[B_bgb[j % 2]], writes=[B_wbf])

        def bg_step():
            i = bg_state["i"]
            if i > len(bg_pieces):
                return
            bg_state["i"] = i + 1
            if i >= 1:
                bg_conv(i - 1)
            if i < len(bg_pieces):
                bg_load(i)

        Zs = [psum2[i][:].rearrange("p (h t) -> p h t", h=2) for i in range(2)]
        B_Z = B_pq

        def load_unit(u, slot):
            for i in range(4):
                P.dma("sp", lambda e, i=i: e.dma_start(out=KT[slot][:, i * 4096:(i + 1) * 4096],
                                                       in_=kT_scr[u, :, i * 4096:(i + 1) * 4096]),
                      B_KT[slot], reads=[B_kT], writes=[B_KT[slot]])
            for i in range(4):
                P.dma("sp", lambda e, i=i: e.dma_start(
                    out=VV[slot][:, i * 32:(i + 1) * 32, 0:128],
                    in_=v_scr[u, :, i * 4096:(i + 1) * 4096].rearrange("p (t d) -> p t d", d=128)),
                    B_VV[slot], reads=[B_v], writes=[B_VV[slot]])
            P.dma("sp", lambda e: e.dma_start(out=QT[slot], in_=q_scr[u, :, :]),
                  B_QT[slot], reads=[B_q], writes=[B_QT[slot]])

        def tile_list(order, fine=False):
            tl = []
            for mc in range(4):
                kts = list(range(32 * mc + 32))
                if order == "bwd":
                    kts = kts[::-1]
                for j, kt in enumerate(kts):
                    d = kt - 32 * mc
                    c0 = 128 * (d // 8) if d >= 0 else 0
                    cf = 16 * d if d >= 0 else 0
                    if fine:
                        c0 = cf
                    tl.append(dict(mc=mc, kt=kt, d=d, c0=c0, cf=cf, first=(j == 0), last=(j == len(kts) - 1)))
            return tl

        def score_mm(slot, tinfo, Zt, Zb):
            mc, kt, c0 = tinfo["mc"], tinfo["kt"], tinfo["cf"]

            def f(e):
                e.matmul(Zt[:, 0, c0:512], lhsT=KT[slot][0:64, kt * 128:(kt + 1) * 128],
                         rhs=QT[slot][0:64, mc * 512 + c0:(mc + 1) * 512], start=True, stop=True,
                         skip_group_check=True)
                return e.matmul(Zt[:, 1, c0:512], lhsT=KT[slot][64:128, kt * 128:(kt + 1) * 128],
                                rhs=QT[slot][64:128, mc * 512 + c0:(mc + 1) * 512], start=True, stop=True,
                                skip_group_check=True)
            P.op("pe", f, reads=[B_KT[slot], B_QT[slot]], writes=[Zb])

        def diag_mask(tinfo, buf_ap, bbuf, mask):
            d, c0 = tinfo["d"], tinfo["c0"]
            if d < 0:
                return

            def f(e):
                if 16 * d > c0:
                    e.memset(buf_ap[:, :, c0:16 * d], 0.0)
                e.tensor_tensor(out=buf_ap[:, 0, 16 * d:16 * d + 16], in0=buf_ap[:, 0, 16 * d:16 * d + 16],
                                in1=mask, op=ALU.mult)
                return e.tensor_tensor(out=buf_ap[:, 1, 16 * d:16 * d + 16], in0=buf_ap[:, 1, 16 * d:16 * d + 16],
                                       in1=mask, op=ALU.mult)
            P.op("pool", f, reads=[B_const], writes=[bbuf])

        def acc_ap(j):
            return psum[j // 3][:, (j % 3) * 129:(j % 3) * 129 + 129]
        B_acc = None
        BACC = [B_ps[0], B_ps[1], B_ps[2]]

        def da_unit(u, slot):
            tl = tile_list("fwd")
            n = len(tl)
            pslots = {}

            def S_(i):
                score_mm(slot, tl[i], Zs[i % 2], B_Z[i % 2])

            def P_(i):
                ti = tl[i]
                c0 = ti["cf"]
                pi, pb = a_ring.next()
                pslots[i] = (pi, pb)
                P.op("act", lambda e: e.activation(out=a_t[pi][:, :, c0:512], in_=Zs[i % 2][:, :, c0:512],
                                                   func=AF.Exp, scale=0.125),
                     reads=[B_Z[i % 2]], writes=[pb])
                diag_mask(ti, a_t[pi], pb, mask_da)

            def PV_(i):
                ti = tl[i]
                pi, pb = pslots[i]
                c0, kt = ti["c0"], ti["kt"]

                def f(e):
                    ins = None
                    for r in range(c0 // 128, 4):
                        for comp in range(2):
                            j = r * 2 + comp
                            st = ti["first"] and (j % 3 == 0)
                            ins = e.matmul(acc_ap(j), lhsT=a_t[pi][:, comp, r * 128:(r + 1) * 128],
                                           rhs=VV[slot][:, kt, 0:129], start=st, stop=False,
                                           skip_group_check=True)
                    return ins
                P.op("pe", f, reads=[pb, B_VV[slot]], writes=BACC)
                if ti["last"]:
                    da_epilogue(u, ti["mc"])

            S_(0)
            for i in range(n):
                if i + 1 < n:
                    S_(i + 1)
                P_(i)
                PV_(i)
                if i % 20 == 10:
                    bg_step()

        def da_epilogue(u, mc):
            def fc(e):
                ins = None
                for b in range(3):
                    nacc = 3 if b < 2 else 2
                    ins = e.tensor_copy(out=accs[:, b * 3:b * 3 + nacc, :],
                                        in_=psum[b][:, 0:nacc * 129].rearrange("p (a c) -> p a c", c=129))
                return ins
            P.op("dve", fc, reads=BACC, writes=[B_accs])
            yi, yb = yst_ring.next()
            for r in range(4):
                a1 = accs[:, 2 * r, :]
                a2 = accs[:, 2 * r + 1, :]
                chain(P, "dve", [
                    lambda e, a1=a1: e.reciprocal(out=small[:, 20:21], in_=a1[:, 128:129]),
                    lambda e, a2=a2: e.reciprocal(out=small[:, 21:22], in_=a2[:, 128:129]),
                    lambda e: e.tensor_tensor(out=small[:, 21:22], in0=small[:, 21:22], in1=nlam, op=ALU.mult),
                    lambda e, a1=a1: e.tensor_scalar(out=t1d, in0=a1[:, 0:128], scalar1=small[:, 20:21], scalar2=None,
                                                    op0=ALU.mult),
                    lambda e, a2=a2: e.scalar_tensor_tensor(out=o_t, in0=a2[:, 0:128], scalar=small[:, 21:22], in1=t1d,
                                                           op0=ALU.mult, op1=ALU.add),
                    lambda e: e.memset(small[:, 22:23], 0.0),
                ], [B_accs, B_small], [], B_ep)
                chain(P, "act", [
                    lambda e: e.activation(out=junk2, in_=o_t, func=AF.Square, accum_out=small[:, 22:23]),
                    lambda e: e.activation(out=small[:, 23:24], in_=small[:, 22:23], func=AF.Ln, bias=epsc,
                                           scale=1.0 / 128.0),
                    lambda e: e.activation(out=small[:, 23:24], in_=small[:, 23:24], func=AF.Exp, scale=-0.5),
                ], [B_small], [], B_ep)
                P.op("dve", lambda e: e.scalar_tensor_tensor(out=ya, in0=o_t, scalar=small[:, 23:24], in1=gsub,
                                                             op0=ALU.mult, op1=ALU.mult),
                     reads=[B_ep, B_const], writes=[B_ep])
                P.op("pe", lambda e: e.transpose(out=psT[:, 0, :], in_=ya, identity=ident),
                     reads=[B_ep, B_const], writes=[B_psT])
                P.op("act", lambda e, r=r: e.activation(out=yst[yi][:, r * 128:(r + 1) * 128], in_=psT[:, 0, :],
                                                        func=AF.Copy), reads=[B_psT], writes=[yb])
            P.dma("pool", lambda e: e.dma_start(out=y_scr[u, :, mc * 512:(mc + 1) * 512], in_=yst[yi]),
                  yb, reads=[yb], writes=[B_y])

        Xb = [psum[0], psum[1]]
        Ob = [psum[2], psum[3]]
        BX = [B_ps[0], B_ps[1]]
        BO = [B_ps[2], B_ps[3]]
        Xpair = pA[:, :].rearrange("p (h t) -> p h t", h=2)

        def sb_unit(u, slot):
            tl = tile_list("bwd", fine=True)
            n = len(tl)
            es_, Ls_, gs_, as_ = {}, {}, {}, {}

            def Z_(i):
                score_mm(slot, tl[i], Zs[i % 2], B_Z[i % 2])

            def E_(i):
                ti = tl[i]
                c0 = ti["c0"]
                k, b = e_ring.next()
                es_[i] = (k, b)
                P.op("act", lambda e: e.activation(out=e_t[k][:, :, c0:512], in_=Zs[i % 2][:, :, c0:512],
                                                   func=AF.Exp, scale=0.125),
                     reads=[B_Z[i % 2]], writes=[b])
                diag_mask(ti, e_t[k], b, mask_sb)

            def L_(i):
                c0 = tl[i]["c0"]
                k, b = es_[i]
                lk, lb = L_ring.next()
                Ls_[i] = (lk, lb)
                P.op("act", lambda e: e.activation(out=L_t[lk][:, :, c0:512], in_=e_t[k][:, :, c0:512],
                                                   func=AF.Ln, bias=onec, scale=1.0),
                     reads=[b, B_small], writes=[lb])

            def TRI_(i, mat):
                ti = tl[i]
                c0 = ti["c0"]
                lk, lb = Ls_[i]
                st = ti["first"] and (mat is tri)

                def f(e):
                    e.matmul(Xb[0][:, c0:512], lhsT=mat, rhs=L_t[lk][:, 0, c0:512], start=st, stop=False,
                             skip_group_check=True)
                    return e.matmul(Xb[1][:, c0:512], lhsT=mat, rhs=L_t[lk][:, 1, c0:512], start=st, stop=False,
                                    skip_group_check=True)
                P.op("pe", f, reads=[lb, B_const], writes=BX)

            def G_(i):
                c0 = tl[i]["c0"]
                gk, gb = g_ring.next()
                gs_[i] = (gk, gb)

                P.op("act", lambda e: e.activation(out=g_t[gk][:, :, c0:512], in_=Xpair[:, :, c0:512],
                                                   func=AF.Exp, scale=-1.0), reads=BX, writes=[gb])

            def A_(i):
                c0 = tl[i]["c0"]
                k, b = es_[i]
                gk, gb = gs_[i]
                ak, ab = a_ring.next()
                as_[i] = (ak, ab)
                P.op("dve", lambda e: e.tensor_tensor(out=a_t[ak][:, :, c0:512], in0=e_t[k][:, :, c0:512],
                                                      in1=g_t[gk][:, :, c0:512], op=ALU.mult),
                     reads=[b, gb], writes=[ab])

            def AV_(i):
                ti = tl[i]
                c0, kt = ti["c0"], ti["kt"]
                ak, ab = as_[i]
                st = ti["first"]

                def f(e):
                    e.matmul(Ob[0][:, c0:512], lhsT=VV[slot][:, kt, 0:128], rhs=a_t[ak][:, 0, c0:512],
                             start=st, stop=False, skip_group_check=True)
                    return e.matmul(Ob[1][:, c0:512], lhsT=VV[slot][:, kt, 0:128], rhs=a_t[ak][:, 1, c0:512],
                                    start=st, stop=False, skip_group_check=True)
                P.op("pe", f, reads=[ab, B_VV[slot]], writes=BO)
                if ti["last"]:
                    yi, yb = yst_ring.next()
                    mc = ti["mc"]
                    P.op("act", lambda e: e.activation(out=yst[yi][0:64, :], in_=Ob[0][0:64, :], func=AF.Copy),
                         reads=BO, writes=[yb])
                    P.op("dve", lambda e: e.tensor_copy(out=yst[yi][64:128, :], in_=Ob[1][64:128, :]),
                         reads=BO, writes=[yb])
                    P.dma("pool", lambda e: e.dma_start(out=y_scr[u, :, mc * 512:(mc + 1) * 512], in_=yst[yi]),
                          yb, reads=[yb], writes=[B_y])

            Z_(0)
            if n > 1:
                Z_(1)
            E_(0)
            for i in range(n):
                L_(i)
                TRI_(i, tri)
                if i + 2 < n:
                    Z_(i + 2)
                if i + 1 < n:
                    E_(i + 1)
                if i - 1 >= 0:
                    AV_(i - 1)
                G_(i)
                TRI_(i, slt)
                A_(i)
                if i % 20 == 10:
                    bg_step()
            AV_(n - 1)

        unit_order = [4, 0, 5, 1, 6, 2, 7, 3][:NUNITS] if MAXPH >= 2 else []
        if unit_order:
            load_unit(unit_order[0], 0)
        for ui, u in enumerate(unit_order):
            slot = ui % 2
            if ui + 1 < len(unit_order):
                load_unit(unit_order[ui + 1], (ui + 1) % 2)
            if u < 4:
                da_unit(u, slot)
            else:
                sb_unit(u, slot)
        while bg_state["i"] <= len(bg_pieces):
            bg_step()

        P.barrier()
        abf.reset(m_bf)
        af.reset(m_f)
        if debug:
            for u in range(8):
                P.dma("sp", lambda e, u=u: e.dma_start(out=dbg["kT"][u, :, :], in_=kT_scr[u, :, 0:1024]),
                      B_const, reads=[B_kT])
                P.dma("sp", lambda e, u=u: e.dma_start(out=dbg["v"][u, :, :], in_=v_scr[u, :, 0:1024]),
                      B_const, reads=[B_v])
                P.dma("sp", lambda e, u=u: e.dma_start(out=dbg["q"][u, :, :], in_=q_scr[u, :, :]),
                      B_const, reads=[B_q])
                P.dma("sp", lambda e, u=u: e.dma_start(out=dbg["y"][u, :, :], in_=y_scr[u, :, :]),
                      B_const, reads=[B_y])

        Wo = abf.get(8 * 1024, [8, 1024])
        Wxq = abf.get(8 * 1024, [8, 1024])
        Wxo = abf.get(8 * 1024, [8, 1024])
        Wkv = abf.get(8 * 2048, [8, 2048])
        B_wl = P.buf("wload")
        qrr = {"i": 0}

        B_wo, B_wxq, B_wxo, B_wkv, B_wup, B_wdn = [P.buf(n_, multi=True) for n_ in ("wo", "wxq", "wxo", "wkv", "wup", "wdn")]

        def load_bf(dst, name, nk, N, wb_):
            flat = dst.rearrange("p k n -> p (k n)")
            srcf = wbf[name].rearrange("p k n -> p (k n)")
            tot = nk * N
            step = 4096
            for o in range(0, tot, step):
                q = ("sp", "act", "pool")[qrr["i"] % 3]
                qrr["i"] += 1
                if not hasattr(wb_, "ld_owner"):
                    wb_.ld_owner = P.buf("wl_" + name)
                P.dma(q, lambda e, o=o: e.dma_start(out=flat[:, o:o + step], in_=srcf[:, o:o + step]),
                      wb_.ld_owner, reads=[B_wbf], writes=[wb_])
        load_bf(Wkv, "wkv", 8, 2048, B_wkv)
        load_bf(Wo, "wo", 8, 1024, B_wo)
        load_bf(Wxq, "wxq", 8, 1024, B_wxq)
        load_bf(Wxo, "wxo", 8, 1024, B_wxo)

        memT = abf.get(8 * 256, [8, 256])
        B_memT = P.buf("memT")
        kxT = abf.get(8 * 256, [8, 256])
        vx = abf.get(2 * 1024, [2, 1024])
        B_kx = P.buf("kx", multi=True)
        junk = abf.get(1024)
        xs = abf.get(1024)
        scr = (junk, ssr, rrr, xs, B_scr)
        hbuf = [af.get(4 * 1024, [4, 1024]) for _ in range(1)]
        B_hb = P.buf("hbuf", multi=True)
        xt_ring2 = Ring(P, "xtm", 2)
        mt = [hbuf[0][:, 0, :], hbuf[0][:, 1, :]]
        B_hc = P.buf("hchunk")
        for t in range(2):
            P.dma("sp", lambda e, t=t: e.dma_start(out=mt[t], in_=mem_d[t * 128:(t + 1) * 128, :]), B_hc,
                  writes=[B_hc])
            norm_transpose(mt[t], B_hc, memT, B_memT, t * 128, scr, psT, B_psT, "mem")
        for ct in range(8):
            def fm(e, ct=ct):
                ins = None
                for k in range(8):
                    ins = e.matmul(psum[0][:, 0:256], lhsT=Wkv[:, k, ct * 128:(ct + 1) * 128], rhs=memT[:, k, :],
                                   start=(k == 0), stop=(k == 7))
                return ins
            P.op("pe", fm, reads=[B_memT, B_wkv], writes=[B_ps[0]])
            P.op("act", lambda e, ct=ct: e.activation(out=kxT[:, ct, :], in_=psum[0][:, 0:256], func=AF.Copy),
                 reads=[B_ps[0]], writes=[B_kx])
        for t in range(2):
            for hf in range(2):
                def fm(e, t=t, hf=hf):
                    ins = None
                    for k in range(8):
                        ins = e.matmul(psum[1], lhsT=memT[:, k, t * 128:(t + 1) * 128],
                                       rhs=Wkv[:, k, 1024 + hf * 512:1024 + (hf + 1) * 512],
                                       start=(k == 0), stop=(k == 7))
                    return ins
                P.op("pe", fm, reads=[B_memT, B_wkv], writes=[B_ps[1]])
                P.op("act", lambda e, t=t, hf=hf: e.activation(out=vx[:, t, hf * 512:(hf + 1) * 512], in_=psum[1],
                                                               func=AF.Copy), reads=[B_ps[1]], writes=[B_kx])

        yT = abf.get(8 * 512, [8, 512])
        B_yT = P.buf("yT")
        hnT = abf.get(8 * 512, [8, 512])
        B_hnT = P.buf("hnT")
        qxT = abf.get(8 * 512, [8, 512])
        B_qxT = P.buf("qxT")
        pT = abf.get(1024, [2, 512])
        B_pT = P.buf("pT")
        rb = af.get(512)
        B_rb = P.buf("rb")
        oT = abf.get(8 * 512, [8, 512])
        B_oT = P.buf("oT")
        hb = hbuf[0]

        for mc in range(4 if MAXPH >= 3 else 0):
            P.dma("sp", lambda e, mc=mc: e.dma_start(
                out=yT, in_=y_scr[:, :, mc * 512:(mc + 1) * 512].rearrange("u p t -> p u t")),
                B_yT, reads=[B_y], writes=[B_yT])
            P.dma("sp", lambda e, mc=mc: e.dma_start(
                out=hb, in_=x_own[mc * 512:(mc + 1) * 512, :].rearrange("(t p) d -> p t d", p=128)),
                B_hc, writes=[B_hc])
            for t in range(4):
                for hf in range(2):
                    pv, pvb = (psum[2], B_ps[2]) if hf == 0 else (psum[3], B_ps[3])

                    def fm(e, t=t, hf=hf, pv=pv):
                        ins = None
                        for k in range(8):
                            ins = e.matmul(pv, lhsT=yT[:, k, t * 128:(t + 1) * 128],
                                           rhs=Wo[:, k, hf * 512:(hf + 1) * 512], start=(k == 0), stop=(k == 7))
                        return ins
                    P.op("pe", fm, reads=[B_yT, B_wo], writes=[pvb])
                    P.op("dve", lambda e, t=t, hf=hf, pv=pv: e.tensor_tensor(
                        out=hb[:, t, hf * 512:(hf + 1) * 512], in0=pv, in1=hb[:, t, hf * 512:(hf + 1) * 512],
                        op=ALU.add), reads=[pvb], writes=[B_hc])
            for t in range(4):
                norm_transpose(hb[:, t, :], B_hc, hnT, B_hnT, t * 128, scr, psT, B_psT, "c")
            for ct in range(8):
                pv, pvb = (psum[0], B_ps[0]) if ct % 2 == 0 else (psum[1], B_ps[1])

                def fm(e, ct=ct, pv=pv):
                    ins = None
                    for k in range(8):
                        ins = e.matmul(pv, lhsT=Wxq[:, k, ct * 128:(ct + 1) * 128], rhs=hnT[:, k, :],
                                       start=(k == 0), stop=(k == 7))
                    return ins
                P.op("pe", fm, reads=[B_hnT, B_wxq], writes=[pvb])
                P.op("act", lambda e, ct=ct, pv=pv: e.activation(out=qxT[:, ct, :], in_=pv, func=AF.Copy),
                     reads=[pvb], writes=[B_qxT])
            for hh in range(4):
                Zt = Zs[hh % 2]

                def fs(e, hh=hh, Zt=Zt):
                    ins = None
                    for kt in range(2):
                        for j in range(2):
                            ins = e.matmul(Zt[:, kt, :], lhsT=kxT[:, 2 * hh + j, kt * 128:(kt + 1) * 128],
                                           rhs=qxT[:, 2 * hh + j, :], start=(j == 0), stop=(j == 1))
                    return ins
                P.op("pe", fs, reads=[B_kx, B_qxT], writes=[B_Z[hh % 2]])
                P.op("act", lambda e, Zt=Zt: e.activation(out=pT, in_=Zt, func=AF.Exp, scale=1.0 / 16.0),
                     reads=[B_Z[hh % 2]], writes=[B_pT])

                def fl(e):
                    e.matmul(psum[2], lhsT=ones_bf, rhs=pT[:, 0, :], start=True, stop=False)
                    return e.matmul(psum[2], lhsT=ones_bf, rhs=pT[:, 1, :], start=False, stop=True)
                P.op("pe", fl, reads=[B_pT, B_const], writes=[B_ps[2]])
                P.op("dve", lambda e: e.reciprocal(out=rb, in_=psum[2]), reads=[B_ps[2]], writes=[B_rb])
                for j in range(2):
                    pv, pvb = (psum[3], B_ps[3]) if j == 0 else (psum[0], B_ps[0])

                    def fo(e, hh=hh, j=j, pv=pv):
                        c = (2 * hh + j) * 128
                        e.matmul(pv, lhsT=vx[:, 0, c:c + 128], rhs=pT[:, 0, :], start=True, stop=False)
                        return e.matmul(pv, lhsT=vx[:, 1, c:c + 128], rhs=pT[:, 1, :], start=False, stop=True)
                    P.op("pe", fo, reads=[B_pT, B_kx], writes=[pvb])
                    P.op("dve", lambda e, hh=hh, j=j, pv=pv: e.tensor_tensor(out=oT[:, 2 * hh + j, :], in0=pv, in1=rb,
                                                                             op=ALU.mult),
                         reads=[pvb, B_rb], writes=[B_oT])
            for t in range(4):
                for hf in range(2):
                    pv, pvb = (psum[1], B_ps[1]) if hf == 0 else (psum[2], B_ps[2])

                    def fm(e, t=t, hf=hf, pv=pv):
                        ins = None
                        for k in range(8):
                            ins = e.matmul(pv, lhsT=oT[:, k, t * 128:(t + 1) * 128],
                                           rhs=Wxo[:, k, hf * 512:(hf + 1) * 512], start=(k == 0), stop=(k == 7))
                        return ins
                    P.op("pe", fm, reads=[B_oT, B_wxo], writes=[pvb])
                    P.op("dve", lambda e, t=t, hf=hf, pv=pv: e.tensor_tensor(
                        out=hb[:, t, hf * 512:(hf + 1) * 512], in0=pv, in1=hb[:, t, hf * 512:(hf + 1) * 512],
                        op=ALU.add), reads=[pvb], writes=[B_hc])
            P.dma("pool", lambda e, mc=mc: e.dma_start(
                out=h_scr[mc * 512:(mc + 1) * 512, :].rearrange("(t p) d -> p t d", p=128), in_=hb),
                B_hc, reads=[B_hc], writes=[B_h])

        P.barrier()
        abf.reset(m_bf)
        af.reset(m_f)
        if debug:
            P.dma("sp", lambda e: e.dma_start(out=dbg["h"][:, :], in_=h_scr[:, :]), B_const, reads=[B_h])

        Wup = abf.get(8 * 4096, [8, 4096])
        Wdn = abf.get(32 * 1024, [32, 1024])
        load_bf(Wup, "wup", 8, 4096, B_wup)
        load_bf(Wdn, "wdn", 32, 1024, B_wdn)
        gfin = af.get(1024)
        ld_const(gfin, gfin_d[:, :])
        junk = abf.get(1024)
        xs = abf.get(1024)
        scr = (junk, ssr, rrr, xs, B_scr)
        hb2 = [af.get(2 * 1024, [2, 1024]) for _ in range(2)]
        hb2_ring = Ring(P, "hb2", 2)
        hn2 = [abf.get(8 * 256, [8, 256]) for _ in range(2)]
        hn2_ring = Ring(P, "hn2", 2)
        aT = abf.get(32 * 256, [32, 256])
        B_aT = P.buf("aT", multi=True)
        rl = [af.get(256) for _ in range(2)]
        rl_ring = Ring(P, "rl", 2)
        B_fin = P.buf("fin")

        for c2 in range(OWN // 256 if MAXPH >= 4 else 0):
            hi, hbb = hb2_ring.next()
            hbc = hb2[hi]
            P.dma("sp", lambda e, c2=c2, hbc=hbc: e.dma_start(
                out=hbc, in_=h_scr[c2 * 256:(c2 + 1) * 256, :].rearrange("(t p) d -> p t d", p=128)),
                hbb, reads=[B_h], writes=[hbb])
            ni, nb = hn2_ring.next()
            for t in range(2):
                norm_transpose(hbc[:, t, :], hbb, hn2[ni], nb, t * 128, scr, psT, B_psT, "m")
            for fc in range(32):
                pv, pvb = (psum[fc % 2][:, 0:256], B_ps[fc % 2])

                def fm(e, fc=fc, pv=pv, ni=ni):
                    ins = None
                    for k in range(8):
                        ins = e.matmul(pv, lhsT=Wup[:, k, fc * 128:(fc + 1) * 128], rhs=hn2[ni][:, k, :],
                                       start=(k == 0), stop=(k == 7))
                    return ins
                P.op("pe", fm, reads=[nb, B_wup], writes=[pvb])
                ri, rbuf = rl_ring.next()
                P.op("act", lambda e, pv=pv, ri=ri: e.activation(out=rl[ri], in_=pv, func=AF.Relu),
                     reads=[pvb], writes=[rbuf])
                P.op("pool", lambda e, fc=fc, ri=ri: e.tensor_tensor(out=aT[:, fc, :], in0=rl[ri], in1=rl[ri],
                                                                     op=ALU.mult),
                     reads=[rbuf], writes=[B_aT])
            for t in range(2):
                for hf in range(2):
                    pv, pvb = (psum[2], B_ps[2]) if hf == 0 else (psum[3], B_ps[3])

                    def fm(e, t=t, hf=hf, pv=pv):
                        ins = None
                        for fc in range(32):
                            ins = e.matmul(pv, lhsT=aT[:, fc, t * 128:(t + 1) * 128],
                                           rhs=Wdn[:, fc, hf * 512:(hf + 1) * 512], start=(fc == 0), stop=(fc == 31))
                        return ins
                    P.op("pe", fm, reads=[B_aT, B_wdn], writes=[pvb])
                    P.op("dve", lambda e, t=t, hf=hf, pv=pv, hbc=hbc: e.tensor_tensor(
                        out=hbc[:, t, hf * 512:(hf + 1) * 512], in0=pv, in1=hbc[:, t, hf * 512:(hf + 1) * 512],
                        op=ALU.add), reads=[pvb], writes=[hbb])
                P.op("pool", lambda e: e.memset(small[:, 30:31], 0.0), writes=[B_fin])
                chain(P, "act", [
                    lambda e, t=t, hbc=hbc: e.activation(out=junk, in_=hbc[:, t, :], func=AF.Square,
                                                         accum_out=small[:, 30:31]),
                    lambda e: e.activation(out=small[:, 31:32], in_=small[:, 30:31], func=AF.Ln, bias=epsc,
                                           scale=1.0 / D),
                    lambda e: e.activation(out=small[:, 31:32], in_=small[:, 31:32], func=AF.Exp, scale=-0.5),
                ], [hbb, B_small], [], B_fin)
                P.op("dve", lambda e, t=t, hbc=hbc: e.scalar_tensor_tensor(
                    out=hbc[:, t, :], in0=hbc[:, t, :], scalar=small[:, 31:32], in1=gfin, op0=ALU.mult, op1=ALU.mult),
                    reads=[B_fin, B_const], writes=[hbb, B_fin])
            P.dma("pool", lambda e, c2=c2, hbc=hbc: e.dma_start(
                out=out_d[c2 * 256:(c2 + 1) * 256, :].rearrange("(t p) d -> p t d", p=128), in_=hbc),
                hbb, reads=[hbb], writes=[B_out])

        P.barrier()
        block = es.enter_context(nc.Block())
        P.emit(nc, block, es)
    return nc


def _swap_cols(w):
    n = w.shape[1] // 64
    idx = np.arange(w.shape[1]).reshape(n, 64)
    idx2 = idx.copy()
    idx2[:, 0:8] = idx[:, 8:16]
    idx2[:, 8:16] = idx[:, 0:8]
    return w[:, idx2.reshape(-1)]


_NC_CACHE = {}


def make_in_maps(inputs):
    f32 = np.float32
    x = np.ascontiguousarray(np.asarray(inputs["x"], dtype=f32)[0])
    pos = np.ascontiguousarray(np.asarray(inputs["positions"]).astype(np.int32))
    w_in = np.asarray(inputs["w_in"], dtype=f32)[0]
    qa, ka, va = w_in[:, 0:512], w_in[:, 512:1024], w_in[:, 1024:1536]
    qs, ksb, vs = w_in[:, 1536:2048], w_in[:, 2048:2560], w_in[:, 2560:3072]
    wk = np.ascontiguousarray(np.concatenate([ka, _swap_cols(ka), ksb, va, vs], axis=1))
    wq = np.ascontiguousarray(np.concatenate([qa, _swap_cols(qa), qs], axis=1))

    def gcol(g):
        return np.asarray(g, dtype=f32).reshape(8, 128).T

    gcols = np.ascontiguousarray(np.concatenate(
        [gcol(inputs["g_mix"][0]), gcol(inputs["g_cross"][0]), gcol(inputs["g_mlp"][0]), gcol(inputs["g_mem"][0])],
        axis=1))
    gfin = np.ascontiguousarray(np.broadcast_to(np.asarray(inputs["g_final"], dtype=f32)[None, :], (128, D)))
    gsub = np.ascontiguousarray(np.broadcast_to(np.asarray(inputs["g_subln"], dtype=f32)[0][None, :], (128, 128)))
    lam = np.concatenate([np.asarray(inputs[k], dtype=f32)[0] for k in
                          ("lambda_q1", "lambda_k1", "lambda_q2", "lambda_k2")])
    lam = np.ascontiguousarray(np.broadcast_to(lam[None, :], (128, 256)))
    inv_freq = (500000.0 ** (-np.arange(0, 16, 2, dtype=np.float32) / 16.0)).astype(f32)
    ropec = np.zeros((128, 2), f32)
    for p in range(128):
        d = p % 64
        if d < 8:
            ropec[p, 0] = inv_freq[d]
            ropec[p, 1] = math.pi
        elif d < 16:
            ropec[p, 0] = inv_freq[d - 8]
            ropec[p, 1] = 0.0
        else:
            ropec[p, 0] = 0.0
            ropec[p, 1] = 0.0
    k_ = np.arange(128)[:, None]
    cbf = np.zeros((128, 512), f32)
    cbf[:, 0:128] = np.eye(128)
    cbf[:, 128:256] = (k_ >= np.arange(128)[None, :])
    cbf[:, 256:384] = (k_ < np.arange(128)[None, :])
    cbf[:, 384:512] = 1.0
    cbf = cbf.astype(ml_dtypes.bfloat16)
    common = dict(
        x_all=x, pos_all=pos, wk=wk, wq=wq,
        w_out=np.ascontiguousarray(np.asarray(inputs["w_out"], dtype=f32)[0]),
        w_xq=np.ascontiguousarray(np.asarray(inputs["w_xq"], dtype=f32)[0]),
        w_xkv=np.ascontiguousarray(np.asarray(inputs["w_xkv"], dtype=f32)[0]),
        w_xo=np.ascontiguousarray(np.asarray(inputs["w_xo"], dtype=f32)[0]),
        w_up=np.ascontiguousarray(np.asarray(inputs["w_up"], dtype=f32)[0]),
        w_down=np.ascontiguousarray(np.asarray(inputs["w_down"], dtype=f32)[0]),
        mem=np.ascontiguousarray(np.asarray(inputs["mem"], dtype=f32)[0]),
        gcols=gcols, gfin=gfin, gsub=gsub, lam=lam, ropec=ropec, cbf=cbf,
    )
    in_maps = []
    i_ = np.arange(16)[None, :]
    for c in range(NCORES):
        masks = np.zeros((128, 32), f32)
        masks[:, 0:16] = (k_ < 8 * i_ + c)
        masks[:, 16:32] = (k_ <= 8 * i_ + c)
        m = dict(common)
        m["x_own"] = np.ascontiguousarray(x[c::NCORES])
        m["pos_own"] = np.ascontiguousarray(pos[:, c::NCORES])
        m["masks"] = masks
        in_maps.append(m)
    return in_maps


def kernel(**inputs):
    in_maps = make_in_maps(inputs)
    if "nc" not in _NC_CACHE:
        _NC_CACHE["nc"] = build_program(debug=DEBUG)
    nc = _NC_CACHE["nc"]
    res = run_bass_kernel_spmd(nc, in_maps, core_ids=list(range(NCORES)))
    out = np.empty((1, S, D), np.float32)
    for c in range(NCORES):
        out[0, c::NCORES, :] = res.results[c]["out"]
    if DEBUG:
        kernel.last = res
    return out
```

```python
import math
import numpy as np
import ml_dtypes
import concourse.bass as bass
import concourse.mybir as mybir
from concourse.bass_utils import run_bass_kernel_spmd

F32 = mybir.dt.float32
BF16 = mybir.dt.bfloat16
I32 = mybir.dt.int32
AF = mybir.ActivationFunctionType
ALU = mybir.AluOpType
AX = mybir.AxisListType

NCORES = 8
S = 16384
D = 1024
OWN = S // NCORES
NCH_ALL = S // 512
NCH_OWN = OWN // 512
EPS = 1e-6
TWO_PI = 2.0 * math.pi

DEBUG = False
import os
MAXPH = int(os.environ.get('KMAXPH', '4'))
NUNITS = int(os.environ.get('KNUNITS', '8'))


class Buf:
    def __init__(self, name, multi=False):
        self.name = name
        self.multi = multi
        self.writes = {}
        self.reads = {}
        self.sem = None
        self.cnt = 0


class Prog:
    COMPUTE = ("pe", "act", "dve", "pool")
    ALL = ("sp", "pe", "act", "dve", "pool")

    def __init__(self):
        self.ops = {e: [] for e in self.ALL}
        self.seen = {e: {} for e in self.ALL}
        self.bufs = []
        self.dma_owners = []

    def buf(self, name, multi=False):
        b = Buf(name, multi)
        self.bufs.append(b)
        return b

    @staticmethod
    def _merge(dst, src):
        for k, v in src.items():
            if dst.get(k, -1) < v:
                dst[k] = v

    def _waits(self, eng, reads, writes):
        need = {}
        for b in reads:
            self._merge(need, b.writes)
        for b in writes:
            self._merge(need, b.reads)
            if not b.multi:
                self._merge(need, b.writes)
        out = []
        seen = self.seen[eng]
        for k, v in need.items():
            if k[0] == "c" and k[1] == eng and eng in ("pe", "sp"):
                continue
            if seen.get(k, -1) >= v:
                continue
            seen[k] = v
            out.append((k, v))
        return out

    def _commit(self, tokkey, tokval, reads, writes):
        for b in reads:
            if b.reads.get(tokkey, -1) < tokval:
                b.reads[tokkey] = tokval
        for b in writes:
            if b.multi:
                if b.writes.get(tokkey, -1) < tokval:
                    b.writes[tokkey] = tokval
            else:
                b.reads = {}
                b.writes = {tokkey: tokval}

    def op(self, eng, fn, reads=(), writes=()):
        waits = self._waits(eng, reads, writes)
        idx = len(self.ops[eng])
        self.ops[eng].append(dict(fn=fn, waits=waits, dma=None))
        self._commit(("c", eng), idx, reads, writes)

    def dma(self, q, fn, owner, reads=(), writes=()):
        if q == "pool":
            if getattr(owner, "twin", None) is None:
                owner.twin = Buf(owner.name + "_sw")
            owner = owner.twin
        waits = self._waits(q, reads, writes)
        if owner.cnt == 0 and owner not in self.dma_owners:
            self.dma_owners.append(owner)
        owner.cnt += 16
        self.ops[q].append(dict(fn=fn, waits=waits, dma=owner))
        self._commit(("d", id(owner), owner), owner.cnt, reads, writes)

    def barrier(self):
        toks = {}
        for e in self.COMPUTE:
            real = [i for i, o in enumerate(self.ops[e]) if o["fn"] is not None and o["dma"] is None]
            if real:
                toks[("c", e)] = real[-1]
        for b in self.dma_owners:
            toks[("d", id(b), b)] = b.cnt
        for e in self.ALL:
            waits = []
            seen = self.seen[e]
            for k, v in toks.items():
                if k[0] == "c" and k[1] == e and e in ("pe", "sp"):
                    continue
                if seen.get(k, -1) >= v:
                    continue
                seen[k] = v
                waits.append((k, v))
            if waits:
                self.ops[e].append(dict(fn=None, waits=waits, dma=None))
        for b in self.bufs:
            b.reads = {}
            b.writes = {}

    def emit(self, nc, block, es):
        need_inc = {e: set() for e in self.COMPUTE}
        for e in self.ALL:
            for o in self.ops[e]:
                for k, v in o["waits"]:
                    if k[0] == "c":
                        need_inc[k[1]].add(v)
        semval = {}
        for e in self.COMPUTE:
            cnt = 0
            vals = []
            for i in range(len(self.ops[e])):
                if i in need_inc[e]:
                    cnt += 1
                vals.append(cnt)
            semval[e] = vals
        sems = {e: es.enter_context(nc.semaphore("sem_" + e)) for e in self.COMPUTE}
        print("n dma semaphores", len(self.dma_owners))
        for b in self.dma_owners:
            b.sem = es.enter_context(nc.semaphore("dsem_" + b.name))

        def run(ename):
            def body(eng):
                for i, o in enumerate(self.ops[ename]):
                    for k, v in o["waits"]:
                        if k[0] == "c":
                            eng.wait_ge(sems[k[1]], semval[k[1]][v])
                        else:
                            eng.wait_ge(k[2].sem, v)
                    if o["fn"] is None:
                        continue
                    ins = o["fn"](eng)
                    if o["dma"] is not None:
                        ins.then_inc(o["dma"].sem, 16)
                    elif i in need_inc.get(ename, ()):
                        ins.then_inc(sems[ename], 1)
            return body

        block.sync(run("sp"))
        block.tensor(run("pe"))
        block.scalar(run("act"))
        block.vector(run("dve"))
        block.gpsimd(run("pool"))


def chain(P, eng, fns, reads, writes, link):
    for f in fns:
        P.op(eng, f, reads=list(reads) + [link], writes=list(writes) + [link])


class Ring:
    def __init__(self, P, name, n):
        self.bufs = [P.buf(f"{name}{i}") for i in range(n)]
        self.n = n
        self.i = 0

    def next(self):
        k = self.i % self.n
        self.i += 1
        return k, self.bufs[k]


def build_program(debug=False):
    nc = bass.Bass("TRN2", target_bir_lowering=False)
    P = Prog()

    def din(name, shape, dt=F32):
        return nc.dram_tensor(name, list(shape), dt, kind="ExternalInput").ap()

    x_all = din("x_all", [S, D])
    x_own = din("x_own", [OWN, D])
    pos_all = din("pos_all", [1, S], I32)
    pos_own = din("pos_own", [1, OWN], I32)
    wk_d = din("wk", [D, 2560])
    wq_d = din("wq", [D, 1536])
    wout_d = din("w_out", [D, D])
    wxq_d = din("w_xq", [D, D])
    wxkv_d = din("w_xkv", [D, 2 * D])
    wxo_d = din("w_xo", [D, D])
    wup_d = din("w_up", [D, 4 * D])
    wdn_d = din("w_down", [4 * D, D])
    mem_d = din("mem", [256, D])
    gcols_d = din("gcols", [128, 32])
    gfin_d = din("gfin", [128, D])
    gsub_d = din("gsub", [128, 128])
    lam_d = din("lam", [128, 256])
    ropec_d = din("ropec", [128, 2])
    masks_d = din("masks", [128, 32])
    cbf_d = din("cbf", [128, 512], BF16)
    out_d = nc.dram_tensor("out", [OWN, D], F32, kind="ExternalOutput").ap()

    kT_scr = nc.dram_tensor("kT_scr", [8, 128, S], BF16).ap()
    v_scr = nc.dram_tensor("v_scr", [8, 128, S], BF16).ap()
    q_scr = nc.dram_tensor("q_scr", [8, 128, OWN], BF16).ap()
    y_scr = nc.dram_tensor("y_scr", [8, 128, OWN], BF16).ap()
    h_scr = nc.dram_tensor("h_scr", [OWN, D], F32).ap()
    wbf = {
        "wo": nc.dram_tensor("wbf_wo", [128, 8, 1024], BF16).ap(),
        "wxq": nc.dram_tensor("wbf_wxq", [128, 8, 1024], BF16).ap(),
        "wxo": nc.dram_tensor("wbf_wxo", [128, 8, 1024], BF16).ap(),
        "wkv": nc.dram_tensor("wbf_wkv", [128, 8, 2048], BF16).ap(),
        "wup": nc.dram_tensor("wbf_wup", [128, 8, 4096], BF16).ap(),
        "wdn": nc.dram_tensor("wbf_wdn", [128, 32, 1024], BF16).ap(),
    }
    dbg = {}
    if debug:
        dbg["kT"] = nc.dram_tensor("dbg_kT", [8, 128, 1024], BF16, kind="ExternalOutput").ap()
        dbg["v"] = nc.dram_tensor("dbg_v", [8, 128, 1024], BF16, kind="ExternalOutput").ap()
        dbg["q"] = nc.dram_tensor("dbg_q", [8, 128, OWN], BF16, kind="ExternalOutput").ap()
        dbg["y"] = nc.dram_tensor("dbg_y", [8, 128, OWN], BF16, kind="ExternalOutput").ap()
        dbg["h"] = nc.dram_tensor("dbg_h", [OWN, D], F32, kind="ExternalOutput").ap()

    B_kT = P.buf("kT_scr", multi=True)
    B_v = P.buf("v_scr", multi=True)
    B_q = P.buf("q_scr", multi=True)
    B_y = P.buf("y_scr", multi=True)
    B_h = P.buf("h_scr", multi=True)
    B_out = P.buf("out", multi=True)
    B_wbf = P.buf("wbf", multi=True)

    import contextlib
    es = contextlib.ExitStack()
    with es:
        NB = 95 * 1024
        big = es.enter_context(nc.sbuf_tensor("big", [128, NB], BF16))
        pA = es.enter_context(nc.psum_tensor("pA", [128, 1024], F32))
        pB = es.enter_context(nc.psum_tensor("pB", [128, 1024], F32))
        psum = [pA[:, 0:512], pA[:, 512:1024], pB[:, 0:512], pB[:, 512:1024]]
        psum2 = [es.enter_context(nc.psum_tensor(f"pq{i}", [128, 1024], F32)) for i in range(2)]
        psT = pB[:, 512:1024].bitcast(BF16).rearrange("p (k t) -> p k t", k=8)
        psT2 = pB[:, 0:512].bitcast(BF16).rearrange("p (k t) -> p k t", k=8)

        class Alloc:
            def __init__(self):
                self.o = 0

            def get(self, size, shape=None, dt=BF16):
                mul = 1 if dt == BF16 else 2
                n = size * mul
                self.o = (self.o + 1) // 2 * 2
                assert self.o + n <= NB, ("sbuf overflow", self.o, n, NB)
                ap = big[:, self.o:self.o + n]
                self.o += n
                if dt != BF16:
                    ap = ap.bitcast(dt)
                if shape is not None:
                    names = " ".join(f"d{i}" for i in range(len(shape)))
                    kw = {f"d{i}": shape[i] for i in range(len(shape))}
                    ap = ap.rearrange(f"p ({names}) -> p {names}", **kw)
                return ap

            def mark(self):
                return self.o

            def reset(self, o):
                self.o = o

        abf = Alloc()

        class AF32:
            def get(self, size, shape=None):
                return abf.get(size, shape, F32)

            def mark(self):
                return 0

            def reset(self, o):
                pass
        af = AF32()

        cbf = abf.get(512)
        ident = cbf[:, 0:128]
        tri = cbf[:, 128:256]
        slt = cbf[:, 256:384]
        ones_bf = cbf[:, 384:512]
        gcols = af.get(32)
        gsub = af.get(128)
        lamt = af.get(256)
        ropec = af.get(2)
        masks = af.get(32)
        small = af.get(64)
        nlam = small[:, 0:1]
        epsc = small[:, 17:18]
        B_const = P.buf("const")
        B_small = P.buf("small")

        def ld_const(dst, src):
            P.dma("sp", lambda e, dst=dst, src=src: e.dma_start(out=dst, in_=src), B_const, writes=[B_const])

        P.op("pool", lambda e: e.memset(epsc, EPS), writes=[B_small])
        ld_const(cbf, cbf_d[:, :])
        ld_const(gcols, gcols_d[:, :])
        ld_const(gsub, gsub_d[:, :])
        ld_const(lamt, lam_d[:, :])
        ld_const(ropec, ropec_d[:, :])
        ld_const(masks, masks_d[:, :])
        mask_sb = masks[:, 0:16]
        mask_da = masks[:, 16:32]

        lam_init = 0.8 - 0.6 * math.exp(-0.3 * 0)
        tmp64 = af.get(128)

        chain(P, "dve", [
            lambda e: e.tensor_tensor(out=tmp64[:, 0:64], in0=lamt[:, 0:64], in1=lamt[:, 64:128], op=ALU.mult),
            lambda e: e.tensor_tensor(out=tmp64[:, 64:128], in0=lamt[:, 128:192], in1=lamt[:, 192:256], op=ALU.mult),
            lambda e: e.tensor_reduce(out=small[:, 1:2], in_=tmp64[:, 0:64], axis=AX.X, op=ALU.add),
            lambda e: e.tensor_reduce(out=small[:, 2:3], in_=tmp64[:, 64:128], axis=AX.X, op=ALU.add),
        ], [B_const], [], B_small)
        P.op("act", lambda e: e.activation(out=small[:, 3:5], in_=small[:, 1:3], func=AF.Exp),
             reads=[B_small], writes=[B_small])
        chain(P, "dve", [
            lambda e: e.tensor_tensor(out=small[:, 5:6], in0=small[:, 4:5], in1=small[:, 3:4], op=ALU.subtract),
            lambda e: e.tensor_scalar(out=nlam, in0=small[:, 5:6], scalar1=-lam_init, scalar2=None, op0=ALU.add),
            lambda e: e.tensor_scalar(out=gsub, in0=gsub, scalar1=(1.0 - lam_init), scalar2=None, op0=ALU.mult),
        ], [], [B_const], B_small)

        def load_weight(dst, src, K, N, gcol0, stage, stage_ring):
            nk = K // 128
            t = 0
            PW = stage[0].shape[1]
            for k in range(nk):
                for n0 in range(0, N, PW):
                    n1 = min(N, n0 + PW)
                    si, sb = stage_ring.next()
                    st = stage[si][:, 0:n1 - n0]
                    P.dma("sp", lambda e, st=st, k=k, n0=n0, n1=n1: e.dma_start(
                        out=st, in_=src[k * 128:(k + 1) * 128, n0:n1]), sb, writes=[sb])
                    eng = "dve" if (t % 2 == 0) else "pool"
                    t += 1
                    if gcol0 is None:
                        P.op(eng, lambda e, st=st, k=k, n0=n0, n1=n1: e.tensor_copy(out=dst[:, k, n0:n1], in_=st),
                             reads=[sb], writes=[B_w])
                    else:
                        P.op(eng, lambda e, st=st, k=k, n0=n0, n1=n1: e.tensor_scalar(
                            out=dst[:, k, n0:n1], in0=st, scalar1=gcols[:, gcol0 + k:gcol0 + k + 1],
                            scalar2=None, op0=ALU.mult), reads=[sb, B_const], writes=[B_w])

        B_w = P.buf("weights", multi=True)

        def norm_transpose(xt_ap, xt_buf, dstT, dst_buf, col0, scr, ps_t, ps_buf, tag):
            junk, ss, rr, xs, B_s = scr
            P.op("pool", lambda e: e.memset(ss, 0.0), writes=[B_s])
            chain(P, "act", [
                lambda e: e.activation(out=junk, in_=xt_ap, func=AF.Square, accum_out=ss),
                lambda e: e.activation(out=rr, in_=ss, func=AF.Ln, bias=epsc, scale=1.0 / D),
                lambda e: e.activation(out=rr, in_=rr, func=AF.Exp, scale=-0.5),
            ], [xt_buf, B_small], [], B_s)
            P.op("dve", lambda e: e.tensor_scalar(out=xs, in0=xt_ap, scalar1=rr, scalar2=None, op0=ALU.mult),
                 reads=[xt_buf, B_s], writes=[B_s])

            def ft(e):
                ins = None
                for k in range(8):
                    ins = e.transpose(out=ps_t[:, k, :], in_=xs[:, k * 128:(k + 1) * 128], identity=ident)
                return ins
            P.op("pe", ft, reads=[B_s, B_const], writes=[ps_buf])
            P.op("act", lambda e: e.activation(out=dstT[:, :, col0:col0 + 128], in_=ps_t, func=AF.Copy),
                 reads=[ps_buf], writes=[dst_buf])


        m_bf = abf.mark()
        m_f = af.mark()
        Wk = abf.get(8 * 2560, [8, 2560])
        Wq = abf.get(8 * 1536, [8, 1536])
        stage = [af.get(2048), af.get(2048)]
        stage_ring = Ring(P, "wstage", 2)
        load_weight(Wk, wk_d, D, 2560, 0, stage, stage_ring)
        load_weight(Wq, wq_d, D, 1536, 0, stage, stage_ring)

        junk = abf.get(1024)
        xs = None
        ssr = small[:, 8:9]
        rrr = small[:, 9:10]
        B_scr = P.buf("normscr")
        scr = (junk, ssr, rrr, xs, B_scr)
        xnT = [abf.get(8 * 512, [8, 512]) for _ in range(2)]
        xnT_ring = Ring(P, "xnT", 2)
        posi = abf.get(512, None, I32)
        ki_t = abf.get(512, None, I32)
        kf_t = af.get(512)
        B_posi = P.buf("posi")
        posf = af.get(512)
        tang = af.get(512)
        tang2 = af.get(512)
        B_rope = P.buf("rope")
        B_tab = P.buf("ropetab")
        t1 = af.get(512)
        t2 = af.get(512)
        B_t12 = P.buf("t12")
        kst = [abf.get(8 * 512, [8, 512]) for _ in range(2)]
        kst_ring = Ring(P, "kst", 2)
        vst = [abf.get(8 * 512, [8, 4, 128]) for _ in range(2)]
        vst_ring = Ring(P, "vst", 2)
        B_ps = [P.buf(f"ps{i}") for i in range(4)]
        B_psT = B_ps[3]
        B_pq = [P.buf(f"pq{i}") for i in range(2)]

        junk_p1 = junk
        xs4 = [abf.get(1024) for _ in range(4)]
        B_xs4 = [P.buf(f"xs4_{i}") for i in range(4)]
        xt3 = [af.get(1024) for _ in range(4)]
        xt3_ring = Ring(P, "xt3", 4)
        tabs = [(af.get(512), af.get(512)) for _ in range(2)]
        B_tabs = [P.buf("tab0"), P.buf("tab1")]
        t12 = [(t1, t2), (af.get(512), af.get(512))]
        B_t12s = [B_t12, P.buf("t12b")]

        def prep_stage_a(xsrc, ci):
            info = []
            for t in range(4):
                ss_ = small[:, 32 + t:33 + t]
                P.op("pool", lambda e, ss_=ss_: e.memset(ss_, 0.0), writes=[B_xs4[t]])
            for t in range(4):
                xi, xb = xt3_ring.next()
                r0 = ci * 512 + t * 128
                P.dma("sp", lambda e, xi=xi, r0=r0: e.dma_start(out=xt3[xi], in_=xsrc[r0:r0 + 128, :]), xb,
                      writes=[xb])
                ss_ = small[:, 32 + t:33 + t]
                rr_ = small[:, 36 + t:37 + t]
                bx = B_xs4[t]
                chain(P, "act", [
                    lambda e, xi=xi, ss_=ss_: e.activation(out=junk_p1, in_=xt3[xi], func=AF.Square, accum_out=ss_),
                    lambda e, ss_=ss_, rr_=rr_: e.activation(out=rr_, in_=ss_, func=AF.Ln, bias=epsc, scale=1.0 / D),
                    lambda e, rr_=rr_: e.activation(out=rr_, in_=rr_, func=AF.Exp, scale=-0.5),
                ], [xb, B_small], [B_scr], bx)
                info.append((xi, xb, rr_, bx))
            return info

        def prep_stage_b(info, t):
            xi, xb, rr_, bx = info[t]
            P.op("dve", lambda e: e.tensor_scalar(out=xs4[t], in0=xt3[xi], scalar1=rr_, scalar2=None, op0=ALU.mult),
                 reads=[xb], writes=[bx])

        def transposes(si, sb):
            for t in range(4):
                pt, ptb = (psT, B_ps[3]) if t % 2 == 0 else (psT2, B_ps[2])

                def ft(e, t=t, pt=pt):
                    ins = None
                    for k in range(8):
                        ins = e.transpose(out=pt[:, k, :], in_=xs4[t][:, k * 128:(k + 1) * 128], identity=ident)
                    return ins
                P.op("pe", ft, reads=[B_xs4[t], B_const], writes=[ptb])
                P.op("act", lambda e, t=t, pt=pt: e.activation(out=xnT[si][:, :, t * 128:(t + 1) * 128], in_=pt,
                                                               func=AF.Copy), reads=[ptb], writes=[sb])

        INV2PI = float(1.0 / TWO_PI)

        def rope_ops(possrc, ci, tslot):
            ctab_, stab_ = tabs[tslot]

            def reduce_(dst, shift):
                return [
                    lambda e: e.tensor_scalar(out=dst, in0=posf, scalar1=ropec[:, 0:1], scalar2=shift,
                                              op0=ALU.mult, op1=ALU.add),
                    lambda e: e.tensor_scalar(out=ki_t, in0=dst, scalar1=INV2PI, scalar2=None, op0=ALU.mult),
                    lambda e: e.tensor_copy(out=kf_t, in_=ki_t),
                    lambda e: e.scalar_tensor_tensor(out=dst, in0=kf_t, scalar=-TWO_PI, in1=dst,
                                                     op0=ALU.mult, op1=ALU.add),
                    lambda e: e.tensor_scalar(out=kf_t, in0=dst, scalar1=float(math.pi), scalar2=None, op0=ALU.is_gt),
                    lambda e: e.scalar_tensor_tensor(out=dst, in0=kf_t, scalar=-TWO_PI, in1=dst,
                                                     op0=ALU.mult, op1=ALU.add),
                    lambda e: e.tensor_scalar(out=dst, in0=dst, scalar1=float(-math.pi), scalar2=float(math.pi),
                                              op0=ALU.max, op1=ALU.min),
                ]
            ops = [lambda e: e.tensor_copy(out=posf, in_=posi)] + reduce_(tang, ropec[:, 1:2]) \
                + reduce_(tang2, float(0.5 * math.pi))

            def chain_all():
                P.dma("sp", lambda e: e.dma_start(
                    out=posi, in_=possrc[0:1, ci * 512:(ci + 1) * 512].partition_broadcast(128)),
                    B_posi, writes=[B_posi])
                chain(P, "dve", ops, [B_posi, B_const], [], B_rope)

            def sin_():
                def g(e):
                    e.activation(out=stab_, in_=tang, func=AF.Sin)
                    return e.activation(out=ctab_, in_=tang2, func=AF.Sin)
                P.op("act", g, reads=[B_rope], writes=[B_tabs[tslot]])
            return chain_all, sin_

        B_pq0a = P.buf("pq0a")
        B_pq0b = P.buf("pq0b")
        B_pq1a = P.buf("pq1a")
        B_pq1b = P.buf("pq1b")

        def proj_qk(W, si, sb, kslot, kb, tslot, hooks, mid):
            xT = xnT[si]
            ctab_, stab_ = tabs[tslot]
            for h in range(4):
                t1_, t2_ = t12[h % 2]
                bt = B_t12s[h % 2]
                if h % 2 == 0:
                    pa_ap, pb_ap, bufs = psum[0], psum[1], [B_ps[0], B_ps[1]]
                else:
                    pa_ap, pb_ap, bufs = psum2[0][:, 0:512], psum2[0][:, 512:1024], [B_pq0a, B_pq0b]

                def fm(e, h=h, pa_ap=pa_ap, pb_ap=pb_ap):
                    ins = None
                    for k in range(8):
                        ins = e.matmul(pa_ap, lhsT=W[:, k, h * 128:(h + 1) * 128], rhs=xT[:, k, :],
                                       start=(k == 0), stop=(k == 7))
                    for k in range(8):
                        ins = e.matmul(pb_ap, lhsT=W[:, k, 512 + h * 128:512 + (h + 1) * 128], rhs=xT[:, k, :],
                                       start=(k == 0), stop=(k == 7))
                    return ins
                P.op("pe", fm, reads=[sb, B_w], writes=bufs)

                def fr(e, pa_ap=pa_ap, pb_ap=pb_ap, t1_=t1_, t2_=t2_):
                    e.tensor_tensor(out=t1_, in0=pa_ap, in1=ctab_, op=ALU.mult)
                    return e.tensor_tensor(out=t2_, in0=pb_ap, in1=stab_, op=ALU.mult)
                P.op("dve", fr, reads=bufs + [B_tabs[tslot]], writes=[bt])
                P.op("pool", lambda e, h=h, t1_=t1_, t2_=t2_: e.tensor_tensor(out=kst[kslot][:, h, :], in0=t1_, in1=t2_,
                                                                            op=ALU.add),
                     reads=[bt], writes=[kb])
                if hooks:
                    hooks[h]()
            if mid:
                mid()
            sbk = [(psum[0], B_ps[0]), (psum[1], B_ps[1]), (psum2[0][:, 0:512], B_pq0a),
                   (psum2[0][:, 512:1024], B_pq0b)]
            for u in range(4):
                pz, pzb = sbk[u]

                def fm(e, u=u, pz=pz):
                    ins = None
                    for k in range(8):
                        ins = e.matmul(pz, lhsT=W[:, k, 1024 + u * 128:1024 + (u + 1) * 128], rhs=xT[:, k, :],
                                       start=(k == 0), stop=(k == 7))
                    return ins
                P.op("pe", fm, reads=[sb, B_w], writes=[pzb])
                P.op("act", lambda e, u=u, pz=pz: e.activation(out=kst[kslot][:, 4 + u, :], in_=pz, func=AF.Copy),
                     reads=[pzb], writes=[kb])

        def proj_v(si, sb, vslot, vb):
            xT = xnT[si]
            for t in range(4):
                for hf in range(2):
                    pv, pvbuf = (psum2[1][:, 0:512], B_pq1a) if hf == 0 else (psum2[1][:, 512:1024], B_pq1b)

                    def fm(e, t=t, hf=hf, pv=pv):
                        ins = None
                        for k in range(8):
                            ins = e.matmul(pv, lhsT=xT[:, k, t * 128:(t + 1) * 128],
                                           rhs=Wk[:, k, 1536 + hf * 512:1536 + (hf + 1) * 512],
                                           start=(k == 0), stop=(k == 7))
                        return ins
                    P.op("pe", fm, reads=[sb, B_w], writes=[pvbuf])
                    if hf == 0:
                        P.op("act", lambda e, t=t, hf=hf, pv=pv: e.activation(
                            out=vst[vslot][:, hf * 4:(hf + 1) * 4, t, :],
                            in_=pv.rearrange("p (u d) -> p u d", u=4), func=AF.Copy),
                            reads=[pvbuf], writes=[vb])
                    else:
                        P.op("dve", lambda e, t=t, hf=hf, pv=pv: e.tensor_copy(
                            out=vst[vslot][:, hf * 4:(hf + 1) * 4, t, :],
                            in_=pv.rearrange("p (u d) -> p u d", u=4)),
                            reads=[pvbuf], writes=[vb])

        def store_units(stage_ap, sbuf_, dst, dbuf, ci):
            P.dma("pool", lambda e: e.dma_start(
                out=dst[:, :, ci * 512:(ci + 1) * 512].rearrange("u p t -> p u t"), in_=stage_ap),
                sbuf_, reads=[sbuf_], writes=[dbuf])

        NKV = int(os.environ.get('KNKV', str(NCH_ALL)))
        jobs = [("kv", x_all, pos_all, ci) for ci in range(NKV)] + \
               [("q", x_own, pos_own, ci) for ci in range(NCH_OWN)]
        if MAXPH < 1:
            jobs = jobs[:0]
        if jobs:
            mode0, xsrc0, possrc0, ci0 = jobs[0]
            info0 = prep_stage_a(xsrc0, ci0)
            for t in range(4):
                prep_stage_b(info0, t)
            rc, rs_ = rope_ops(possrc0, ci0, 0)
            rc()
            rs_()
            nslot = xnT_ring.next()
            transposes(*nslot)
        for ji, (mode, xsrc, possrc, ci) in enumerate(jobs):
            si, sb = nslot
            tslot = ji % 2
            has_next = ji + 1 < len(jobs)
            hooks = None
            if has_next:
                nmode, nxsrc, npossrc, nci = jobs[ji + 1]
                ninfo = prep_stage_a(nxsrc, nci)
                hooks = [(lambda h=h, ninfo=ninfo: prep_stage_b(ninfo, h)) for h in range(4)]
                rc, rs_ = rope_ops(npossrc, nci, (ji + 1) % 2)
            ks, kb = kst_ring.next()
            W = Wk if mode == "kv" else Wq
            proj_qk(W, si, sb, ks, kb, tslot, hooks, (rc if has_next else None))
            store_units(kst[ks], kb, kT_scr if mode == "kv" else q_scr, B_kT if mode == "kv" else B_q, ci)
            if has_next:
                nslot = xnT_ring.next()
                transposes(*nslot)
            if mode == "kv":
                vs, vb = vst_ring.next()
                proj_v(si, sb, vs, vb)
            if has_next:
                rs_()
            if mode == "kv":
                store_units(vst[vs].rearrange("p u t d -> p u (t d)"), vb, v_scr, B_v, ci)

        P.barrier()
        abf.reset(m_bf)
        af.reset(m_f)

        KT = [abf.get(S) for _ in range(2)]
        VV = [abf.get(128 * 130, [128, 130]) for _ in range(2)]
        QT = [abf.get(OWN) for _ in range(2)]
        B_KT = [P.buf("KT0"), P.buf("KT1")]
        B_VV = [P.buf("VV0"), P.buf("VV1")]
        B_QT = [P.buf("QT0"), P.buf("QT1")]
        for s_ in range(2):
            P.op("pool", lambda e, s_=s_: e.memset(VV[s_][:, :, 128:130], 1.0), writes=[B_VV[s_]])
        e_t = [af.get(1024, [2, 512]) for _ in range(3)]
        e_ring = Ring(P, "e", 3)
        L_t = [abf.get(1024, [2, 512]) for _ in range(2)]
        L_ring = Ring(P, "L", 2)
        g_t = [abf.get(1024, [2, 512]) for _ in range(2)]
        g_ring = Ring(P, "g", 2)
        a_t = [abf.get(1024, [2, 512]) for _ in range(3)]
        a_ring = Ring(P, "a", 3)
        yst = [abf.get(512) for _ in range(2)]
        yst_ring = Ring(P, "yst", 2)
        accs = af.get(8 * 129, [8, 129])
        B_accs = P.buf("accs")
        o_t = af.get(128)
        t1d = af.get(128)
        ya = abf.get(128)
        junk2 = abf.get(128)
        B_ep = P.buf("ep")
        onec = small[:, 16:17]
        P.op("pool", lambda e: e.memset(onec, 1.0), writes=[B_small])

        bg_sf = [af.get(1024), af.get(1024)]
        bg_sb = [abf.get(1024), abf.get(1024)]
        B_bgf = [P.buf("bgf0"), P.buf("bgf1")]
        B_bgb = [P.buf("bgb0"), P.buf("bgb1")]
        bg_pieces = []
        for name, src, K, N, gc in (("wo", wout_d, D, 1024, None), ("wxq", wxq_d, D, 1024, 8),
                                    ("wxo", wxo_d, D, 1024, None), ("wkv", wxkv_d, D, 2048, 24),
                                    ("wup", wup_d, D, 4096, 16), ("wdn", wdn_d, 4 * D, 1024, None)):
            for k in range(K // 128):
                for n0 in range(0, N, 1024):
                    bg_pieces.append((name, src, k, n0, gc))
        bg_state = {"i": 0}

        def bg_load(i):
            name, src, k, n0, gc = bg_pieces[i]
            P.dma("sp", lambda e: e.dma_start(out=bg_sf[i % 2], in_=src[k * 128:(k + 1) * 128, n0:n0 + 1024]),
                  B_bgf[i % 2], writes=[B_bgf[i % 2]])

        def bg_conv(j):
            name, src, k, n0, gc = bg_pieces[j]
            if gc is None:
                P.op("dve", lambda e: e.tensor_copy(out=bg_sb[j % 2], in_=bg_sf[j % 2]),
                     reads=[B_bgf[j % 2]], writes=[B_bgb[j % 2]])
            else:
                P.op("dve", lambda e: e.tensor_scalar(out=bg_sb[j % 2], in0=bg_sf[j % 2],
                                                      scalar1=gcols[:, gc + k:gc + k + 1], scalar2=None,
                                                      op0=ALU.mult),
                     reads=[B_bgf[j % 2], B_const], writes=[B_bgb[j % 2]])
            P.dma("pool", lambda e: e.dma_start(out=wbf[name][:, k, n0:n0 + 1024], in_=bg_sb[j % 2]),
                  B_bgb[j % 2], reads=[B_bgb[j % 2]], writes=[B_wbf])

        def bg_step():
            i = bg_state["i"]
            if i > len(bg_pieces):
                return
            bg_state["i"] = i + 1
            if i >= 1:
                bg_conv(i - 1)
            if i < len(bg_pieces):
                bg_load(i)

        Zs = [psum2[i][:].rearrange("p (h t) -> p h t", h=2) for i in range(2)]
        B_Z = B_pq

        def load_unit(u, slot):
            for i in range(4):
                P.dma("sp", lambda e, i=i: e.dma_start(out=KT[slot][:, i * 4096:(i + 1) * 4096],
                                                       in_=kT_scr[u, :, i * 4096:(i + 1) * 4096]),
                      B_KT[slot], reads=[B_kT], writes=[B_KT[slot]])
            for i in range(4):
                P.dma("sp", lambda e, i=i: e.dma_start(
                    out=VV[slot][:, i * 32:(i + 1) * 32, 0:128],
                    in_=v_scr[u, :, i * 4096:(i + 1) * 4096].rearrange("p (t d) -> p t d", d=128)),
                    B_VV[slot], reads=[B_v], writes=[B_VV[slot]])
            P.dma("sp", lambda e: e.dma_start(out=QT[slot], in_=q_scr[u, :, :]),
                  B_QT[slot], reads=[B_q], writes=[B_QT[slot]])

        def tile_list(order, fine=False):
            tl = []
            for mc in range(4):
                kts = list(range(32 * mc + 32))
                if order == "bwd":
                    kts = kts[::-1]
                for j, kt in enumerate(kts):
                    d = kt - 32 * mc
                    c0 = 128 * (d // 8) if d >= 0 else 0
                    cf = 16 * d if d >= 0 else 0
                    if fine:
                        c0 = cf
                    tl.append(dict(mc=mc, kt=kt, d=d, c0=c0, cf=cf, first=(j == 0), last=(j == len(kts) - 1)))
            return tl

        def score_mm(slot, tinfo, Zt, Zb):
            mc, kt, c0 = tinfo["mc"], tinfo["kt"], tinfo["cf"]

            def f(e):
                e.matmul(Zt[:, 0, c0:512], lhsT=KT[slot][0:64, kt * 128:(kt + 1) * 128],
                         rhs=QT[slot][0:64, mc * 512 + c0:(mc + 1) * 512], start=True, stop=True,
                         skip_group_check=True)
                return e.matmul(Zt[:, 1, c0:512], lhsT=KT[slot][64:128, kt * 128:(kt + 1) * 128],
                                rhs=QT[slot][64:128, mc * 512 + c0:(mc + 1) * 512], start=True, stop=True,
                                skip_group_check=True)
            P.op("pe", f, reads=[B_KT[slot], B_QT[slot]], writes=[Zb])

        def diag_mask(tinfo, buf_ap, bbuf, mask):
            d, c0 = tinfo["d"], tinfo["c0"]
            if d < 0:
                return

            def f(e):
                if 16 * d > c0:
                    e.memset(buf_ap[:, :, c0:16 * d], 0.0)
                e.tensor_tensor(out=buf_ap[:, 0, 16 * d:16 * d + 16], in0=buf_ap[:, 0, 16 * d:16 * d + 16],
                                in1=mask, op=ALU.mult)
                return e.tensor_tensor(out=buf_ap[:, 1, 16 * d:16 * d + 16], in0=buf_ap[:, 1, 16 * d:16 * d + 16],
                                       in1=mask, op=ALU.mult)
            P.op("pool", f, reads=[B_const], writes=[bbuf])

        def acc_ap(j):
            return psum[j // 3][:, (j % 3) * 129:(j % 3) * 129 + 129]
        B_acc = None
        BACC = [B_ps[0], B_ps[1], B_ps[2]]

        def da_unit(u, slot):
            tl = tile_list("fwd")
            n = len(tl)
            pslots = {}

            def S_(i):
                score_mm(slot, tl[i], Zs[i % 2], B_Z[i % 2])

            def P_(i):
                ti = tl[i]
                c0 = ti["cf"]
                pi, pb = a_ring.next()
                pslots[i] = (pi, pb)
                P.op("act", lambda e: e.activation(out=a_t[pi][:, :, c0:512], in_=Zs[i % 2][:, :, c0:512],
                                                   func=AF.Exp, scale=0.125),
                     reads=[B_Z[i % 2]], writes=[pb])
                diag_mask(ti, a_t[pi], pb, mask_da)

            def PV_(i):
                ti = tl[i]
                pi, pb = pslots[i]
                c0, kt = ti["c0"], ti["kt"]

                def f(e):
                    ins = None
                    for r in range(c0 // 128, 4):
                        for comp in range(2):
                            j = r * 2 + comp
                            st = ti["first"] and (j % 3 == 0)
                            ins = e.matmul(acc_ap(j), lhsT=a_t[pi][:, comp, r * 128:(r + 1) * 128],
                                           rhs=VV[slot][:, kt, 0:129], start=st, stop=False,
                                           skip_group_check=True)
                    return ins
                P.op("pe", f, reads=[pb, B_VV[slot]], writes=BACC)
                if ti["last"]:
                    da_epilogue(u, ti["mc"])

            S_(0)
            for i in range(n):
                if i + 1 < n:
                    S_(i + 1)
                P_(i)
                PV_(i)
                if i % 10 == 5:
                    bg_step()

        def da_epilogue(u, mc):
            def fc(e):
                ins = None
                for b in range(3):
                    nacc = 3 if b < 2 else 2
                    ins = e.tensor_copy(out=accs[:, b * 3:b * 3 + nacc, :],
                                        in_=psum[b][:, 0:nacc * 129].rearrange("p (a c) -> p a c", c=129))
                return ins
            P.op("dve", fc, reads=BACC, writes=[B_accs])
            yi, yb = yst_ring.next()
            for r in range(4):
                a1 = accs[:, 2 * r, :]
                a2 = accs[:, 2 * r + 1, :]
                chain(P, "dve", [
                    lambda e, a1=a1: e.reciprocal(out=small[:, 20:21], in_=a1[:, 128:129]),
                    lambda e, a2=a2: e.reciprocal(out=small[:, 21:22], in_=a2[:, 128:129]),
                    lambda e: e.tensor_tensor(out=small[:, 21:22], in0=small[:, 21:22], in1=nlam, op=ALU.mult),
                    lambda e, a1=a1: e.tensor_scalar(out=t1d, in0=a1[:, 0:128], scalar1=small[:, 20:21], scalar2=None,
                                                    op0=ALU.mult),
                    lambda e, a2=a2: e.scalar_tensor_tensor(out=o_t, in0=a2[:, 0:128], scalar=small[:, 21:22], in1=t1d,
                                                           op0=ALU.mult, op1=ALU.add),
                    lambda e: e.memset(small[:, 22:23], 0.0),
                ], [B_accs, B_small], [], B_ep)
                chain(P, "act", [
                    lambda e: e.activation(out=junk2, in_=o_t, func=AF.Square, accum_out=small[:, 22:23]),
                    lambda e: e.activation(out=small[:, 23:24], in_=small[:, 22:23], func=AF.Ln, bias=epsc,
                                           scale=1.0 / 128.0),
                    lambda e: e.activation(out=small[:, 23:24], in_=small[:, 23:24], func=AF.Exp, scale=-0.5),
                ], [B_small], [], B_ep)
                P.op("dve", lambda e: e.scalar_tensor_tensor(out=ya, in0=o_t, scalar=small[:, 23:24], in1=gsub,
                                                             op0=ALU.mult, op1=ALU.mult),
                     reads=[B_ep, B_const], writes=[B_ep])
                P.op("pe", lambda e: e.transpose(out=psT[:, 0, :], in_=ya, identity=ident),
                     reads=[B_ep, B_const], writes=[B_psT])
                P.op("act", lambda e, r=r: e.activation(out=yst[yi][:, r * 128:(r + 1) * 128], in_=psT[:, 0, :],
                                                        func=AF.Copy), reads=[B_psT], writes=[yb])
            P.dma("pool", lambda e: e.dma_start(out=y_scr[u, :, mc * 512:(mc + 1) * 512], in_=yst[yi]),
                  yb, reads=[yb], writes=[B_y])

        Xb = [psum[0], psum[1]]
        Ob = [psum[2], psum[3]]
        BX = [B_ps[0], B_ps[1]]
        BO = [B_ps[2], B_ps[3]]
        Xpair = pA[:, :].rearrange("p (h t) -> p h t", h=2)

        def sb_unit(u, slot):
            tl = tile_list("bwd", fine=True)
            n = len(tl)
            es_, Ls_, gs_, as_ = {}, {}, {}, {}

            def Z_(i):
                score_mm(slot, tl[i], Zs[i % 2], B_Z[i % 2])

            def E_(i):
                ti = tl[i]
                c0 = ti["c0"]
                k, b = e_ring.next()
                es_[i] = (k, b)
                P.op("act", lambda e: e.activation(out=e_t[k][:, :, c0:512], in_=Zs[i % 2][:, :, c0:512],
                                                   func=AF.Exp, scale=0.125),
                     reads=[B_Z[i % 2]], writes=[b])
                diag_mask(ti, e_t[k], b, mask_sb)

            def L_(i):
                c0 = tl[i]["c0"]
                k, b = es_[i]
                lk, lb = L_ring.next()
                Ls_[i] = (lk, lb)
                P.op("act", lambda e: e.activation(out=L_t[lk][:, :, c0:512], in_=e_t[k][:, :, c0:512],
                                                   func=AF.Ln, bias=onec, scale=1.0),
                     reads=[b, B_small], writes=[lb])

            def TRI_(i, mat):
                ti = tl[i]
                c0 = ti["c0"]
                lk, lb = Ls_[i]
                st = ti["first"] and (mat is tri)

                def f(e):
                    e.matmul(Xb[0][:, c0:512], lhsT=mat, rhs=L_t[lk][:, 0, c0:512], start=st, stop=False,
                             skip_group_check=True)
                    return e.matmul(Xb[1][:, c0:512], lhsT=mat, rhs=L_t[lk][:, 1, c0:512], start=st, stop=False,
                                    skip_group_check=True)
                P.op("pe", f, reads=[lb, B_const], writes=BX)

            def G_(i):
                c0 = tl[i]["c0"]
                gk, gb = g_ring.next()
                gs_[i] = (gk, gb)

                P.op("act", lambda e: e.activation(out=g_t[gk][:, :, c0:512], in_=Xpair[:, :, c0:512],
                                                   func=AF.Exp, scale=-1.0), reads=BX, writes=[gb])

            def A_(i):
                c0 = tl[i]["c0"]
                k, b = es_[i]
                gk, gb = gs_[i]
                ak, ab = a_ring.next()
                as_[i] = (ak, ab)
                P.op("dve", lambda e: e.tensor_tensor(out=a_t[ak][:, :, c0:512], in0=e_t[k][:, :, c0:512],
                                                      in1=g_t[gk][:, :, c0:512], op=ALU.mult),
                     reads=[b, gb], writes=[ab])

            def AV_(i):
                ti = tl[i]
                c0, kt = ti["c0"], ti["kt"]
                ak, ab = as_[i]
                st = ti["first"]

                def f(e):
                    e.matmul(Ob[0][:, c0:512], lhsT=VV[slot][:, kt, 0:128], rhs=a_t[ak][:, 0, c0:512],
                             start=st, stop=False, skip_group_check=True)
                    return e.matmul(Ob[1][:, c0:512], lhsT=VV[slot][:, kt, 0:128], rhs=a_t[ak][:, 1, c0:512],
                                    start=st, stop=False, skip_group_check=True)
                P.op("pe", f, reads=[ab, B_VV[slot]], writes=BO)
                if ti["last"]:
                    yi, yb = yst_ring.next()
                    mc = ti["mc"]
                    P.op("act", lambda e: e.activation(out=yst[yi][0:64, :], in_=Ob[0][0:64, :], func=AF.Copy),
                         reads=BO, writes=[yb])
                    P.op("dve", lambda e: e.tensor_copy(out=yst[yi][64:128, :], in_=Ob[1][64:128, :]),
                         reads=BO, writes=[yb])
                    P.dma("pool", lambda e: e.dma_start(out=y_scr[u, :, mc * 512:(mc + 1) * 512], in_=yst[yi]),
                          yb, reads=[yb], writes=[B_y])

            Z_(0)
            if n > 1:
                Z_(1)
            E_(0)
            for i in range(n):
                L_(i)
                TRI_(i, tri)
                if i + 2 < n:
                    Z_(i + 2)
                if i + 1 < n:
                    E_(i + 1)
                if i - 1 >= 0:
                    AV_(i - 1)
                G_(i)
                TRI_(i, slt)
                A_(i)
            AV_(n - 1)

        unit_order = [4, 0, 5, 1, 6, 2, 7, 3][:NUNITS] if MAXPH >= 2 else []
        if unit_order:
            load_unit(unit_order[0], 0)
        for ui, u in enumerate(unit_order):
            slot = ui % 2
            if ui + 1 < len(unit_order):
                load_unit(unit_order[ui + 1], (ui + 1) % 2)
            if u < 4:
                da_unit(u, slot)
            else:
                sb_unit(u, slot)
        while bg_state["i"] <= len(bg_pieces):
            bg_step()

        P.barrier()
        abf.reset(m_bf)
        af.reset(m_f)
        if debug:
            for u in range(8):
                P.dma("sp", lambda e, u=u: e.dma_start(out=dbg["kT"][u, :, :], in_=kT_scr[u, :, 0:1024]),
                      B_const, reads=[B_kT])
                P.dma("sp", lambda e, u=u: e.dma_start(out=dbg["v"][u, :, :], in_=v_scr[u, :, 0:1024]),
                      B_const, reads=[B_v])
                P.dma("sp", lambda e, u=u: e.dma_start(out=dbg["q"][u, :, :], in_=q_scr[u, :, :]),
                      B_const, reads=[B_q])
                P.dma("sp", lambda e, u=u: e.dma_start(out=dbg["y"][u, :, :], in_=y_scr[u, :, :]),
                      B_const, reads=[B_y])

        Wo = abf.get(8 * 1024, [8, 1024])
        Wxq = abf.get(8 * 1024, [8, 1024])
        Wxo = abf.get(8 * 1024, [8, 1024])
        Wkv = abf.get(8 * 2048, [8, 2048])
        B_wl = P.buf("wload")
        qrr = {"i": 0}

        B_wo, B_wxq, B_wxo, B_wkv, B_wup, B_wdn = [P.buf(n_, multi=True) for n_ in ("wo", "wxq", "wxo", "wkv", "wup", "wdn")]

        def load_bf(dst, name, nk, N, wb_):
            flat = dst.rearrange("p k n -> p (k n)")
            srcf = wbf[name].rearrange("p k n -> p (k n)")
            tot = nk * N
            step = 4096
            for o in range(0, tot, step):
                q = ("sp", "act", "pool")[qrr["i"] % 3]
                qrr["i"] += 1
                if not hasattr(wb_, "ld_owner"):
                    wb_.ld_owner = P.buf("wl_" + name)
                P.dma(q, lambda e, o=o: e.dma_start(out=flat[:, o:o + step], in_=srcf[:, o:o + step]),
                      wb_.ld_owner, reads=[B_wbf], writes=[wb_])
        load_bf(Wkv, "wkv", 8, 2048, B_wkv)
        load_bf(Wo, "wo", 8, 1024, B_wo)
        load_bf(Wxq, "wxq", 8, 1024, B_wxq)
        load_bf(Wxo, "wxo", 8, 1024, B_wxo)

        memT = abf.get(8 * 256, [8, 256])
        B_memT = P.buf("memT")
        kxT = abf.get(8 * 256, [8, 256])
        vx = abf.get(2 * 1024, [2, 1024])
        B_kx = P.buf("kx", multi=True)
        junk = abf.get(1024)
        xs = abf.get(1024)
        scr = (junk, ssr, rrr, xs, B_scr)
        hbuf = [af.get(4 * 1024, [4, 1024]) for _ in range(1)]
        B_hb = P.buf("hbuf", multi=True)
        xt_ring2 = Ring(P, "xtm", 2)
        mt = [hbuf[0][:, 0, :], hbuf[0][:, 1, :]]
        B_hc = P.buf("hchunk")
        for t in range(2):
            P.dma("sp", lambda e, t=t: e.dma_start(out=mt[t], in_=mem_d[t * 128:(t + 1) * 128, :]), B_hc,
                  writes=[B_hc])
            norm_transpose(mt[t], B_hc, memT, B_memT, t * 128, scr, psT, B_psT, "mem")
        for ct in range(8):
            def fm(e, ct=ct):
                ins = None
                for k in range(8):
                    ins = e.matmul(psum[0][:, 0:256], lhsT=Wkv[:, k, ct * 128:(ct + 1) * 128], rhs=memT[:, k, :],
                                   start=(k == 0), stop=(k == 7))
                return ins
            P.op("pe", fm, reads=[B_memT, B_wkv], writes=[B_ps[0]])
            P.op("act", lambda e, ct=ct: e.activation(out=kxT[:, ct, :], in_=psum[0][:, 0:256], func=AF.Copy),
                 reads=[B_ps[0]], writes=[B_kx])
        for t in range(2):
            for hf in range(2):
                def fm(e, t=t, hf=hf):
                    ins = None
                    for k in range(8):
                        ins = e.matmul(psum[1], lhsT=memT[:, k, t * 128:(t + 1) * 128],
                                       rhs=Wkv[:, k, 1024 + hf * 512:1024 + (hf + 1) * 512],
                                       start=(k == 0), stop=(k == 7))
                    return ins
                P.op("pe", fm, reads=[B_memT, B_wkv], writes=[B_ps[1]])
                P.op("act", lambda e, t=t, hf=hf: e.activation(out=vx[:, t, hf * 512:(hf + 1) * 512], in_=psum[1],
                                                               func=AF.Copy), reads=[B_ps[1]], writes=[B_kx])

        yT = abf.get(8 * 512, [8, 512])
        B_yT = P.buf("yT")
        hnT = abf.get(8 * 512, [8, 512])
        B_hnT = P.buf("hnT")
        qxT = abf.get(8 * 512, [8, 512])
        B_qxT = P.buf("qxT")
        pT = abf.get(1024, [2, 512])
        B_pT = P.buf("pT")
        rb = af.get(512)
        B_rb = P.buf("rb")
        oT = abf.get(8 * 512, [8, 512])
        B_oT = P.buf("oT")
        hb = hbuf[0]

        for mc in range(4 if MAXPH >= 3 else 0):
            P.dma("sp", lambda e, mc=mc: e.dma_start(
                out=yT, in_=y_scr[:, :, mc * 512:(mc + 1) * 512].rearrange("u p t -> p u t")),
                B_yT, reads=[B_y], writes=[B_yT])
            P.dma("sp", lambda e, mc=mc: e.dma_start(
                out=hb, in_=x_own[mc * 512:(mc + 1) * 512, :].rearrange("(t p) d -> p t d", p=128)),
                B_hc, writes=[B_hc])
            for t in range(4):
                for hf in range(2):
                    pv, pvb = (psum[2], B_ps[2]) if hf == 0 else (psum[3], B_ps[3])

                    def fm(e, t=t, hf=hf, pv=pv):
                        ins = None
                        for k in range(8):
                            ins = e.matmul(pv, lhsT=yT[:, k, t * 128:(t + 1) * 128],
                                           rhs=Wo[:, k, hf * 512:(hf + 1) * 512], start=(k == 0), stop=(k == 7))
                        return ins
                    P.op("pe", fm, reads=[B_yT, B_wo], writes=[pvb])
                    P.op("dve", lambda e, t=t, hf=hf, pv=pv: e.tensor_tensor(
                        out=hb[:, t, hf * 512:(hf + 1) * 512], in0=pv, in1=hb[:, t, hf * 512:(hf + 1) * 512],
                        op=ALU.add), reads=[pvb], writes=[B_hc])
            for t in range(4):
                norm_transpose(hb[:, t, :], B_hc, hnT, B_hnT, t * 128, scr, psT, B_psT, "c")
            for ct in range(8):
                pv, pvb = (psum[0], B_ps[0]) if ct % 2 == 0 else (psum[1], B_ps[1])

                def fm(e, ct=ct, pv=pv):
                    ins = None
                    for k in range(8):
                        ins = e.matmul(pv, lhsT=Wxq[:, k, ct * 128:(ct + 1) * 128], rhs=hnT[:, k, :],
                                       start=(k == 0), stop=(k == 7))
                    return ins
                P.op("pe", fm, reads=[B_hnT, B_wxq], writes=[pvb])
                P.op("act", lambda e, ct=ct, pv=pv: e.activation(out=qxT[:, ct, :], in_=pv, func=AF.Copy),
                     reads=[pvb], writes=[B_qxT])
            for hh in range(4):
                Zt = Zs[hh % 2]

                def fs(e, hh=hh, Zt=Zt):
                    ins = None
                    for kt in range(2):
                        for j in range(2):
                            ins = e.matmul(Zt[:, kt, :], lhsT=kxT[:, 2 * hh + j, kt * 128:(kt + 1) * 128],
                                           rhs=qxT[:, 2 * hh + j, :], start=(j == 0), stop=(j == 1))
                    return ins
                P.op("pe", fs, reads=[B_kx, B_qxT], writes=[B_Z[hh % 2]])
                P.op("act", lambda e, Zt=Zt: e.activation(out=pT, in_=Zt, func=AF.Exp, scale=1.0 / 16.0),
                     reads=[B_Z[hh % 2]], writes=[B_pT])

                def fl(e):
                    e.matmul(psum[2], lhsT=ones_bf, rhs=pT[:, 0, :], start=True, stop=False)
                    return e.matmul(psum[2], lhsT=ones_bf, rhs=pT[:, 1, :], start=False, stop=True)
                P.op("pe", fl, reads=[B_pT, B_const], writes=[B_ps[2]])
                P.op("dve", lambda e: e.reciprocal(out=rb, in_=psum[2]), reads=[B_ps[2]], writes=[B_rb])
                for j in range(2):
                    pv, pvb = (psum[3], B_ps[3]) if j == 0 else (psum[0], B_ps[0])

                    def fo(e, hh=hh, j=j, pv=pv):
                        c = (2 * hh + j) * 128
                        e.matmul(pv, lhsT=vx[:, 0, c:c + 128], rhs=pT[:, 0, :], start=True, stop=False)
                        return e.matmul(pv, lhsT=vx[:, 1, c:c + 128], rhs=pT[:, 1, :], start=False, stop=True)
                    P.op("pe", fo, reads=[B_pT, B_kx], writes=[pvb])
                    P.op("dve", lambda e, hh=hh, j=j, pv=pv: e.tensor_tensor(out=oT[:, 2 * hh + j, :], in0=pv, in1=rb,
                                                                             op=ALU.mult),
                         reads=[pvb, B_rb], writes=[B_oT])
            for t in range(4):
                for hf in range(2):
                    pv, pvb = (psum[1], B_ps[1]) if hf == 0 else (psum[2], B_ps[2])

                    def fm(e, t=t, hf=hf, pv=pv):
                        ins = None
                        for k in range(8):
                            ins = e.matmul(pv, lhsT=oT[:, k, t * 128:(t + 1) * 128],
                                           rhs=Wxo[:, k, hf * 512:(hf + 1) * 512], start=(k == 0), stop=(k == 7))
                        return ins
                    P.op("pe", fm, reads=[B_oT, B_wxo], writes=[pvb])
                    P.op("dve", lambda e, t=t, hf=hf, pv=pv: e.tensor_tensor(
                        out=hb[:, t, hf * 512:(hf + 1) * 512], in0=pv, in1=hb[:, t, hf * 512:(hf + 1) * 512],
                        op=ALU.add), reads=[pvb], writes=[B_hc])
            P.dma("pool", lambda e, mc=mc: e.dma_start(
                out=h_scr[mc * 512:(mc + 1) * 512, :].rearrange("(t p) d -> p t d", p=128), in_=hb),
                B_hc, reads=[B_hc], writes=[B_h])

        P.barrier()
        abf.reset(m_bf)
        af.reset(m_f)
        if debug:
            P.dma("sp", lambda e: e.dma_start(out=dbg["h"][:, :], in_=h_scr[:, :]), B_const, reads=[B_h])

        Wup = abf.get(8 * 4096, [8, 4096])
        Wdn = abf.get(32 * 1024, [32, 1024])
        load_bf(Wup, "wup", 8, 4096, B_wup)
        load_bf(Wdn, "wdn", 32, 1024, B_wdn)
        gfin = af.get(1024)
        ld_const(gfin, gfin_d[:, :])
        junk = abf.get(1024)
        xs = abf.get(1024)
        scr = (junk, ssr, rrr, xs, B_scr)
        hb2 = [af.get(2 * 1024, [2, 1024]) for _ in range(2)]
        hb2_ring = Ring(P, "hb2", 2)
        hn2 = [abf.get(8 * 256, [8, 256]) for _ in range(1)]
        hn2_ring = Ring(P, "hn2", 1)
        aT = abf.get(32 * 256, [32, 256])
        B_aT = P.buf("aT", multi=True)
        rl = [af.get(256) for _ in range(2)]
        rl_ring = Ring(P, "rl", 2)
        B_fin = P.buf("fin")

        for c2 in range(OWN // 256 if MAXPH >= 4 else 0):
            hi, hbb = hb2_ring.next()
            hbc = hb2[hi]
            P.dma("sp", lambda e, c2=c2, hbc=hbc: e.dma_start(
                out=hbc, in_=h_scr[c2 * 256:(c2 + 1) * 256, :].rearrange("(t p) d -> p t d", p=128)),
                hbb, reads=[B_h], writes=[hbb])
            ni, nb = hn2_ring.next()
            for t in range(2):
                norm_transpose(hbc[:, t, :], hbb, hn2[ni], nb, t * 128, scr, psT, B_psT, "m")
            for fc in range(32):
                pv, pvb = (psum[fc % 2][:, 0:256], B_ps[fc % 2])

                def fm(e, fc=fc, pv=pv, ni=ni):
                    ins = None
                    for k in range(8):
                        ins = e.matmul(pv, lhsT=Wup[:, k, fc * 128:(fc + 1) * 128], rhs=hn2[ni][:, k, :],
                                       start=(k == 0), stop=(k == 7))
                    return ins
                P.op("pe", fm, reads=[nb, B_wup], writes=[pvb])
                ri, rbuf = rl_ring.next()
                P.op("act", lambda e, pv=pv, ri=ri: e.activation(out=rl[ri], in_=pv, func=AF.Relu),
                     reads=[pvb], writes=[rbuf])
                P.op("pool", lambda e, fc=fc, ri=ri: e.tensor_tensor(out=aT[:, fc, :], in0=rl[ri], in1=rl[ri],
                                                                     op=ALU.mult),
                     reads=[rbuf], writes=[B_aT])
            for t in range(2):
                for hf in range(2):
                    pv, pvb = (psum[2], B_ps[2]) if hf == 0 else (psum[3], B_ps[3])

                    def fm(e, t=t, hf=hf, pv=pv):
                        ins = None
                        for fc in range(32):
                            ins = e.matmul(pv, lhsT=aT[:, fc, t * 128:(t + 1) * 128],
                                           rhs=Wdn[:, fc, hf * 512:(hf + 1) * 512], start=(fc == 0), stop=(fc == 31))
                        return ins
                    P.op("pe", fm, reads=[B_aT, B_wdn], writes=[pvb])
                    P.op("dve", lambda e, t=t, hf=hf, pv=pv, hbc=hbc: e.tensor_tensor(
                        out=hbc[:, t, hf * 512:(hf + 1) * 512], in0=pv, in1=hbc[:, t, hf * 512:(hf + 1) * 512],
                        op=ALU.add), reads=[pvb], writes=[hbb])
                P.op("pool", lambda e: e.memset(small[:, 30:31], 0.0), writes=[B_fin])
                chain(P, "act", [
                    lambda e, t=t, hbc=hbc: e.activation(out=junk, in_=hbc[:, t, :], func=AF.Square,
                                                         accum_out=small[:, 30:31]),
                    lambda e: e.activation(out=small[:, 31:32], in_=small[:, 30:31], func=AF.Ln, bias=epsc,
                                           scale=1.0 / D),
                    lambda e: e.activation(out=small[:, 31:32], in_=small[:, 31:32], func=AF.Exp, scale=-0.5),
                ], [hbb, B_small], [], B_fin)
                P.op("dve", lambda e, t=t, hbc=hbc: e.scalar_tensor_tensor(
                    out=hbc[:, t, :], in0=hbc[:, t, :], scalar=small[:, 31:32], in1=gfin, op0=ALU.mult, op1=ALU.mult),
                    reads=[B_fin, B_const], writes=[hbb, B_fin])
            P.dma("pool", lambda e, c2=c2, hbc=hbc: e.dma_start(
                out=out_d[c2 * 256:(c2 + 1) * 256, :].rearrange("(t p) d -> p t d", p=128), in_=hbc),
                hbb, reads=[hbb], writes=[B_out])

        P.barrier()
        block = es.enter_context(nc.Block())
        P.emit(nc, block, es)
    return nc


def _swap_cols(w):
    n = w.shape[1] // 64
    idx = np.arange(w.shape[1]).reshape(n, 64)
    idx2 = idx.copy()
    idx2[:, 0:8] = idx[:, 8:16]
    idx2[:, 8:16] = idx[:, 0:8]
    return w[:, idx2.reshape(-1)]


_NC_CACHE = {}


def make_in_maps(inputs):
    f32 = np.float32
    x = np.ascontiguousarray(np.asarray(inputs["x"], dtype=f32)[0])
    pos = np.ascontiguousarray(np.asarray(inputs["positions"]).astype(np.int32))
    w_in = np.asarray(inputs["w_in"], dtype=f32)[0]
    qa, ka, va = w_in[:, 0:512], w_in[:, 512:1024], w_in[:, 1024:1536]
    qs, ksb, vs = w_in[:, 1536:2048], w_in[:, 2048:2560], w_in[:, 2560:3072]
    wk = np.ascontiguousarray(np.concatenate([ka, _swap_cols(ka), ksb, va, vs], axis=1))
    wq = np.ascontiguousarray(np.concatenate([qa, _swap_cols(qa), qs], axis=1))

    def gcol(g):
        return np.asarray(g, dtype=f32).reshape(8, 128).T

    gcols = np.ascontiguousarray(np.concatenate(
        [gcol(inputs["g_mix"][0]), gcol(inputs["g_cross"][0]), gcol(inputs["g_mlp"][0]), gcol(inputs["g_mem"][0])],
        axis=1))
    gfin = np.ascontiguousarray(np.broadcast_to(np.asarray(inputs["g_final"], dtype=f32)[None, :], (128, D)))
    gsub = np.ascontiguousarray(np.broadcast_to(np.asarray(inputs["g_subln"], dtype=f32)[0][None, :], (128, 128)))
    lam = np.concatenate([np.asarray(inputs[k], dtype=f32)[0] for k in
                          ("lambda_q1", "lambda_k1", "lambda_q2", "lambda_k2")])
    lam = np.ascontiguousarray(np.broadcast_to(lam[None, :], (128, 256)))
    inv_freq = (500000.0 ** (-np.arange(0, 16, 2, dtype=np.float32) / 16.0)).astype(f32)
    ropec = np.zeros((128, 2), f32)
    for p in range(128):
        d = p % 64
        if d < 8:
            ropec[p, 0] = inv_freq[d]
            ropec[p, 1] = math.pi
        elif d < 16:
            ropec[p, 0] = inv_freq[d - 8]
            ropec[p, 1] = 0.0
        else:
            ropec[p, 0] = 0.0
            ropec[p, 1] = 0.0
    k_ = np.arange(128)[:, None]
    cbf = np.zeros((128, 512), f32)
    cbf[:, 0:128] = np.eye(128)
    cbf[:, 128:256] = (k_ >= np.arange(128)[None, :])
    cbf[:, 256:384] = (k_ < np.arange(128)[None, :])
    cbf[:, 384:512] = 1.0
    cbf = cbf.astype(ml_dtypes.bfloat16)
    common = dict(
        x_all=x, pos_all=pos, wk=wk, wq=wq,
        w_out=np.ascontiguousarray(np.asarray(inputs["w_out"], dtype=f32)[0]),
        w_xq=np.ascontiguousarray(np.asarray(inputs["w_xq"], dtype=f32)[0]),
        w_xkv=np.ascontiguousarray(np.asarray(inputs["w_xkv"], dtype=f32)[0]),
        w_xo=np.ascontiguousarray(np.asarray(inputs["w_xo"], dtype=f32)[0]),
        w_up=np.ascontiguousarray(np.asarray(inputs["w_up"], dtype=f32)[0]),
        w_down=np.ascontiguousarray(np.asarray(inputs["w_down"], dtype=f32)[0]),
        mem=np.ascontiguousarray(np.asarray(inputs["mem"], dtype=f32)[0]),
        gcols=gcols, gfin=gfin, gsub=gsub, lam=lam, ropec=ropec, cbf=cbf,
    )
    in_maps = []
    i_ = np.arange(16)[None, :]
    for c in range(NCORES):
        masks = np.zeros((128, 32), f32)
        masks[:, 0:16] = (k_ < 8 * i_ + c)
        masks[:, 16:32] = (k_ <= 8 * i_ + c)
        m = dict(common)
        m["x_own"] = np.ascontiguousarray(x[c::NCORES])
        m["pos_own"] = np.ascontiguousarray(pos[:, c::NCORES])
        m["masks"] = masks
        in_maps.append(m)
    return in_maps


def kernel(**inputs):
    in_maps = make_in_maps(inputs)
    if "nc" not in _NC_CACHE:
        _NC_CACHE["nc"] = build_program(debug=DEBUG)
    nc = _NC_CACHE["nc"]
    res = run_bass_kernel_spmd(nc, in_maps, core_ids=list(range(NCORES)))
    out = np.empty((1, S, D), np.float32)
    for c in range(NCORES):
        out[0, c::NCORES, :] = res.results[c]["out"]
    if DEBUG:
        kernel.last = res
    return out
```

```python
import math
import numpy as np
import ml_dtypes
import concourse.bass as bass
import concourse.mybir as mybir
from concourse.bass_utils import run_bass_kernel_spmd

F32 = mybir.dt.float32
BF16 = mybir.dt.bfloat16
I32 = mybir.dt.int32
AF = mybir.ActivationFunctionType
ALU = mybir.AluOpType
AX = mybir.AxisListType

NCORES = 8
S = 16384
D = 1024
OWN = S // NCORES
NCH_ALL = S // 512
NCH_OWN = OWN // 512
EPS = 1e-6
TWO_PI = 2.0 * math.pi

DEBUG = False
import os
MAXPH = int(os.environ.get('KMAXPH', '4'))
NUNITS = int(os.environ.get('KNUNITS', '8'))


class Buf:
    def __init__(self, name, multi=False):
        self.name = name
        self.multi = multi
        self.writes = {}
        self.reads = {}
        self.sem = None
        self.cnt = 0


class Prog:
    COMPUTE = ("pe", "act", "dve", "pool")
    ALL = ("sp", "pe", "act", "dve", "pool")

    def __init__(self):
        self.ops = {e: [] for e in self.ALL}
        self.seen = {e: {} for e in self.ALL}
        self.bufs = []
        self.dma_owners = []

    def buf(self, name, multi=False):
        b = Buf(name, multi)
        self.bufs.append(b)
        return b

    @staticmethod
    def _merge(dst, src):
        for k, v in src.items():
            if dst.get(k, -1) < v:
                dst[k] = v

    def _waits(self, eng, reads, writes):
        need = {}
        for b in reads:
            self._merge(need, b.writes)
        for b in writes:
            self._merge(need, b.reads)
            if not b.multi:
                self._merge(need, b.writes)
        out = []
        seen = self.seen[eng]
        for k, v in need.items():
            if k[0] == "c" and k[1] == eng and eng in ("pe", "sp"):
                continue
            if seen.get(k, -1) >= v:
                continue
            seen[k] = v
            out.append((k, v))
        return out

    def _commit(self, tokkey, tokval, reads, writes):
        for b in reads:
            if b.reads.get(tokkey, -1) < tokval:
                b.reads[tokkey] = tokval
        for b in writes:
            if b.multi:
                if b.writes.get(tokkey, -1) < tokval:
                    b.writes[tokkey] = tokval
            else:
                b.reads = {}
                b.writes = {tokkey: tokval}

    def op(self, eng, fn, reads=(), writes=()):
        waits = self._waits(eng, reads, writes)
        idx = len(self.ops[eng])
        self.ops[eng].append(dict(fn=fn, waits=waits, dma=None))
        self._commit(("c", eng), idx, reads, writes)

    def dma(self, q, fn, owner, reads=(), writes=()):
        if q == "pool":
            if getattr(owner, "twin", None) is None:
                owner.twin = Buf(owner.name + "_sw")
            owner = owner.twin
        waits = self._waits(q, reads, writes)
        if owner.cnt == 0 and owner not in self.dma_owners:
            self.dma_owners.append(owner)
        owner.cnt += 16
        self.ops[q].append(dict(fn=fn, waits=waits, dma=owner))
        self._commit(("d", id(owner), owner), owner.cnt, reads, writes)

    def barrier(self):
        toks = {}
        for e in self.COMPUTE:
            real = [i for i, o in enumerate(self.ops[e]) if o["fn"] is not None and o["dma"] is None]
            if real:
                toks[("c", e)] = real[-1]
        for b in self.dma_owners:
            toks[("d", id(b), b)] = b.cnt
        for e in self.ALL:
            waits = []
            seen = self.seen[e]
            for k, v in toks.items():
                if k[0] == "c" and k[1] == e and e in ("pe", "sp"):
                    continue
                if seen.get(k, -1) >= v:
                    continue
                seen[k] = v
                waits.append((k, v))
            if waits:
                self.ops[e].append(dict(fn=None, waits=waits, dma=None))
        for b in self.bufs:
            b.reads = {}
            b.writes = {}

    def emit(self, nc, block, es):
        need_inc = {e: set() for e in self.COMPUTE}
        for e in self.ALL:
            for o in self.ops[e]:
                for k, v in o["waits"]:
                    if k[0] == "c":
                        need_inc[k[1]].add(v)
        semval = {}
        for e in self.COMPUTE:
            cnt = 0
            vals = []
            for i in range(len(self.ops[e])):
                if i in need_inc[e]:
                    cnt += 1
                vals.append(cnt)
            semval[e] = vals
        sems = {e: es.enter_context(nc.semaphore("sem_" + e)) for e in self.COMPUTE}
        print("n dma semaphores", len(self.dma_owners))
        for b in self.dma_owners:
            b.sem = es.enter_context(nc.semaphore("dsem_" + b.name))

        def run(ename):
            def body(eng):
                for i, o in enumerate(self.ops[ename]):
                    for k, v in o["waits"]:
                        if k[0] == "c":
                            eng.wait_ge(sems[k[1]], semval[k[1]][v])
                        else:
                            eng.wait_ge(k[2].sem, v)
                    if o["fn"] is None:
                        continue
                    ins = o["fn"](eng)
                    if o["dma"] is not None:
                        ins.then_inc(o["dma"].sem, 16)
                    elif i in need_inc.get(ename, ()):
                        ins.then_inc(sems[ename], 1)
            return body

        block.sync(run("sp"))
        block.tensor(run("pe"))
        block.scalar(run("act"))
        block.vector(run("dve"))
        block.gpsimd(run("pool"))


def chain(P, eng, fns, reads, writes, link):
    for f in fns:
        P.op(eng, f, reads=list(reads) + [link], writes=list(writes) + [link])


class Ring:
    def __init__(self, P, name, n):
        self.bufs = [P.buf(f"{name}{i}") for i in range(n)]
        self.n = n
        self.i = 0

    def next(self):
        k = self.i % self.n
        self.i += 1
        return k, self.bufs[k]


def build_program(debug=False):
    nc = bass.Bass("TRN2", target_bir_lowering=False)
    P = Prog()

    def din(name, shape, dt=F32):
        return nc.dram_tensor(name, list(shape), dt, kind="ExternalInput").ap()

    x_all = din("x_all", [S, D])
    x_own = din("x_own", [OWN, D])
    pos_all = din("pos_all", [1, S], I32)
    pos_own = din("pos_own", [1, OWN], I32)
    wk_d = din("wk", [D, 2560])
    wq_d = din("wq", [D, 1536])
    wout_d = din("w_out", [D, D])
    wxq_d = din("w_xq", [D, D])
    wxkv_d = din("w_xkv", [D, 2 * D])
    wxo_d = din("w_xo", [D, D])
    wup_d = din("w_up", [D, 4 * D])
    wdn_d = din("w_down", [4 * D, D])
    mem_d = din("mem", [256, D])
    gcols_d = din("gcols", [128, 32])
    gfin_d = din("gfin", [128, D])
    gsub_d = din("gsub", [128, 128])
    lam_d = din("lam", [128, 256])
    ropec_d = din("ropec", [128, 2])
    masks_d = din("masks", [128, 32])
    cbf_d = din("cbf", [128, 512], BF16)
    out_d = nc.dram_tensor("out", [OWN, D], F32, kind="ExternalOutput").ap()

    kT_scr = nc.dram_tensor("kT_scr", [8, 128, S], BF16).ap()
    v_scr = nc.dram_tensor("v_scr", [8, 128, S], BF16).ap()
    q_scr = nc.dram_tensor("q_scr", [8, 128, OWN], BF16).ap()
    y_scr = nc.dram_tensor("y_scr", [8, 128, OWN], BF16).ap()
    h_scr = nc.dram_tensor("h_scr", [OWN, D], F32).ap()
    wbf = {
        "wo": nc.dram_tensor("wbf_wo", [128, 8, 1024], BF16).ap(),
        "wxq": nc.dram_tensor("wbf_wxq", [128, 8, 1024], BF16).ap(),
        "wxo": nc.dram_tensor("wbf_wxo", [128, 8, 1024], BF16).ap(),
        "wkv": nc.dram_tensor("wbf_wkv", [128, 8, 2048], BF16).ap(),
        "wup": nc.dram_tensor("wbf_wup", [128, 8, 4096], BF16).ap(),
        "wdn": nc.dram_tensor("wbf_wdn", [128, 32, 1024], BF16).ap(),
    }
    dbg = {}
    if debug:
        dbg["kT"] = nc.dram_tensor("dbg_kT", [8, 128, 1024], BF16, kind="ExternalOutput").ap()
        dbg["v"] = nc.dram_tensor("dbg_v", [8, 128, 1024], BF16, kind="ExternalOutput").ap()
        dbg["q"] = nc.dram_tensor("dbg_q", [8, 128, OWN], BF16, kind="ExternalOutput").ap()
        dbg["y"] = nc.dram_tensor("dbg_y", [8, 128, OWN], BF16, kind="ExternalOutput").ap()
        dbg["h"] = nc.dram_tensor("dbg_h", [OWN, D], F32, kind="ExternalOutput").ap()

    B_kT = P.buf("kT_scr", multi=True)
    B_v = P.buf("v_scr", multi=True)
    B_q = P.buf("q_scr", multi=True)
    B_y = P.buf("y_scr", multi=True)
    B_h = P.buf("h_scr", multi=True)
    B_out = P.buf("out", multi=True)
    B_wbf = P.buf("wbf", multi=True)

    import contextlib
    es = contextlib.ExitStack()
    with es:
        NB = 95 * 1024
        big = es.enter_context(nc.sbuf_tensor("big", [128, NB], BF16))
        pA = es.enter_context(nc.psum_tensor("pA", [128, 1024], F32))
        pB = es.enter_context(nc.psum_tensor("pB", [128, 1024], F32))
        psum = [pA[:, 0:512], pA[:, 512:1024], pB[:, 0:512], pB[:, 512:1024]]
        psum2 = [es.enter_context(nc.psum_tensor(f"pq{i}", [128, 1024], F32)) for i in range(2)]
        psT = pB[:, 512:1024].bitcast(BF16).rearrange("p (k t) -> p k t", k=8)
        psT2 = pB[:, 0:512].bitcast(BF16).rearrange("p (k t) -> p k t", k=8)

        class Alloc:
            def __init__(self):
                self.o = 0

            def get(self, size, shape=None, dt=BF16):
                mul = 1 if dt == BF16 else 2
                n = size * mul
                self.o = (self.o + 1) // 2 * 2
                assert self.o + n <= NB, ("sbuf overflow", self.o, n, NB)
                ap = big[:, self.o:self.o + n]
                self.o += n
                if dt != BF16:
                    ap = ap.bitcast(dt)
                if shape is not None:
                    names = " ".join(f"d{i}" for i in range(len(shape)))
                    kw = {f"d{i}": shape[i] for i in range(len(shape))}
                    ap = ap.rearrange(f"p ({names}) -> p {names}", **kw)
                return ap

            def mark(self):
                return self.o

            def reset(self, o):
                self.o = o

        abf = Alloc()

        class AF32:
            def get(self, size, shape=None):
                return abf.get(size, shape, F32)

            def mark(self):
                return 0

            def reset(self, o):
                pass
        af = AF32()

        cbf = abf.get(512)
        ident = cbf[:, 0:128]
        tri = cbf[:, 128:256]
        slt = cbf[:, 256:384]
        ones_bf = cbf[:, 384:512]
        gcols = af.get(32)
        gsub = af.get(128)
        lamt = af.get(256)
        ropec = af.get(2)
        masks = af.get(32)
        small = af.get(64)
        nlam = small[:, 0:1]
        epsc = small[:, 17:18]
        B_const = P.buf("const")
        B_small = P.buf("small")

        def ld_const(dst, src):
            P.dma("sp", lambda e, dst=dst, src=src: e.dma_start(out=dst, in_=src), B_const, writes=[B_const])

        P.op("pool", lambda e: e.memset(epsc, EPS), writes=[B_small])
        ld_const(cbf, cbf_d[:, :])
        ld_const(gcols, gcols_d[:, :])
        ld_const(gsub, gsub_d[:, :])
        ld_const(lamt, lam_d[:, :])
        ld_const(ropec, ropec_d[:, :])
        ld_const(masks, masks_d[:, :])
        mask_sb = masks[:, 0:16]
        mask_da = masks[:, 16:32]

        lam_init = 0.8 - 0.6 * math.exp(-0.3 * 0)
        tmp64 = af.get(128)

        chain(P, "dve", [
            lambda e: e.tensor_tensor(out=tmp64[:, 0:64], in0=lamt[:, 0:64], in1=lamt[:, 64:128], op=ALU.mult),
            lambda e: e.tensor_tensor(out=tmp64[:, 64:128], in0=lamt[:, 128:192], in1=lamt[:, 192:256], op=ALU.mult),
            lambda e: e.tensor_reduce(out=small[:, 1:2], in_=tmp64[:, 0:64], axis=AX.X, op=ALU.add),
            lambda e: e.tensor_reduce(out=small[:, 2:3], in_=tmp64[:, 64:128], axis=AX.X, op=ALU.add),
        ], [B_const], [], B_small)
        P.op("act", lambda e: e.activation(out=small[:, 3:5], in_=small[:, 1:3], func=AF.Exp),
             reads=[B_small], writes=[B_small])
        chain(P, "dve", [
            lambda e: e.tensor_tensor(out=small[:, 5:6], in0=small[:, 4:5], in1=small[:, 3:4], op=ALU.subtract),
            lambda e: e.tensor_scalar(out=nlam, in0=small[:, 5:6], scalar1=-lam_init, scalar2=None, op0=ALU.add),
            lambda e: e.tensor_scalar(out=gsub, in0=gsub, scalar1=(1.0 - lam_init), scalar2=None, op0=ALU.mult),
        ], [], [B_const], B_small)

        def load_weight(dst, src, K, N, gcol0, stage, stage_ring):
            nk = K // 128
            t = 0
            PW = stage[0].shape[1]
            for k in range(nk):
                for n0 in range(0, N, PW):
                    n1 = min(N, n0 + PW)
                    si, sb = stage_ring.next()
                    st = stage[si][:, 0:n1 - n0]
                    P.dma("sp", lambda e, st=st, k=k, n0=n0, n1=n1: e.dma_start(
                        out=st, in_=src[k * 128:(k + 1) * 128, n0:n1]), sb, writes=[sb])
                    eng = "dve" if (t % 2 == 0) else "pool"
                    t += 1
                    if gcol0 is None:
                        P.op(eng, lambda e, st=st, k=k, n0=n0, n1=n1: e.tensor_copy(out=dst[:, k, n0:n1], in_=st),
                             reads=[sb], writes=[B_w])
                    else:
                        P.op(eng, lambda e, st=st, k=k, n0=n0, n1=n1: e.tensor_scalar(
                            out=dst[:, k, n0:n1], in0=st, scalar1=gcols[:, gcol0 + k:gcol0 + k + 1],
                            scalar2=None, op0=ALU.mult), reads=[sb, B_const], writes=[B_w])

        B_w = P.buf("weights", multi=True)

        def norm_transpose(xt_ap, xt_buf, dstT, dst_buf, col0, scr, ps_t, ps_buf, tag):
            junk, ss, rr, xs, B_s = scr
            P.op("pool", lambda e: e.memset(ss, 0.0), writes=[B_s])
            chain(P, "act", [
                lambda e: e.activation(out=junk, in_=xt_ap, func=AF.Square, accum_out=ss),
                lambda e: e.activation(out=rr, in_=ss, func=AF.Ln, bias=epsc, scale=1.0 / D),
                lambda e: e.activation(out=rr, in_=rr, func=AF.Exp, scale=-0.5),
            ], [xt_buf, B_small], [], B_s)
            P.op("dve", lambda e: e.tensor_scalar(out=xs, in0=xt_ap, scalar1=rr, scalar2=None, op0=ALU.mult),
                 reads=[xt_buf, B_s], writes=[B_s])

            def ft(e):
                ins = None
                for k in range(8):
                    ins = e.transpose(out=ps_t[:, k, :], in_=xs[:, k * 128:(k + 1) * 128], identity=ident)
                return ins
            P.op("pe", ft, reads=[B_s, B_const], writes=[ps_buf])
            P.op("act", lambda e: e.activation(out=dstT[:, :, col0:col0 + 128], in_=ps_t, func=AF.Copy),
                 reads=[ps_buf], writes=[dst_buf])


        m_bf = abf.mark()
        m_f = af.mark()
        Wk = abf.get(8 * 2560, [8, 2560])
        Wq = abf.get(8 * 1536, [8, 1536])
        stage = [af.get(2048), af.get(2048)]
        stage_ring = Ring(P, "wstage", 2)
        load_weight(Wk, wk_d, D, 2560, 0, stage, stage_ring)
        load_weight(Wq, wq_d, D, 1536, 0, stage, stage_ring)

        junk = abf.get(1024)
        xs = None
        ssr = small[:, 8:9]
        rrr = small[:, 9:10]
        B_scr = P.buf("normscr")
        scr = (junk, ssr, rrr, xs, B_scr)
        xnT = [abf.get(8 * 512, [8, 512]) for _ in range(2)]
        xnT_ring = Ring(P, "xnT", 2)
        posi = abf.get(512, None, I32)
        ki_t = abf.get(512, None, I32)
        kf_t = af.get(512)
        B_posi = P.buf("posi")
        posf = af.get(512)
        tang = af.get(512)
        tang2 = af.get(512)
        B_rope = P.buf("rope")
        B_tab = P.buf("ropetab")
        t1 = af.get(512)
        t2 = af.get(512)
        B_t12 = P.buf("t12")
        kst = [abf.get(8 * 512, [8, 512]) for _ in range(2)]
        kst_ring = Ring(P, "kst", 2)
        vst = [abf.get(8 * 512, [8, 4, 128]) for _ in range(2)]
        vst_ring = Ring(P, "vst", 2)
        B_ps = [P.buf(f"ps{i}") for i in range(4)]
        B_psT = B_ps[3]
        B_pq = [P.buf(f"pq{i}") for i in range(2)]

        junk_p1 = junk
        xs4 = [abf.get(1024) for _ in range(4)]
        B_xs4 = [P.buf(f"xs4_{i}") for i in range(4)]
        xt3 = [af.get(1024) for _ in range(4)]
        xt3_ring = Ring(P, "xt3", 4)
        tabs = [(af.get(512), af.get(512)) for _ in range(2)]
        B_tabs = [P.buf("tab0"), P.buf("tab1")]
        t12 = [(t1, t2), (af.get(512), af.get(512))]
        B_t12s = [B_t12, P.buf("t12b")]

        def prep_stage_a(xsrc, ci):
            info = []
            for t in range(4):
                ss_ = small[:, 32 + t:33 + t]
                P.op("pool", lambda e, ss_=ss_: e.memset(ss_, 0.0), writes=[B_xs4[t]])
            for t in range(4):
                xi, xb = xt3_ring.next()
                r0 = ci * 512 + t * 128
                P.dma("sp", lambda e, xi=xi, r0=r0: e.dma_start(out=xt3[xi], in_=xsrc[r0:r0 + 128, :]), xb,
                      writes=[xb])
                ss_ = small[:, 32 + t:33 + t]
                rr_ = small[:, 36 + t:37 + t]
                bx = B_xs4[t]
                chain(P, "act", [
                    lambda e, xi=xi, ss_=ss_: e.activation(out=junk_p1, in_=xt3[xi], func=AF.Square, accum_out=ss_),
                    lambda e, ss_=ss_, rr_=rr_: e.activation(out=rr_, in_=ss_, func=AF.Ln, bias=epsc, scale=1.0 / D),
                    lambda e, rr_=rr_: e.activation(out=rr_, in_=rr_, func=AF.Exp, scale=-0.5),
                ], [xb, B_small], [B_scr], bx)
                info.append((xi, xb, rr_, bx))
            return info

        def prep_stage_b(info, t):
            xi, xb, rr_, bx = info[t]
            P.op("dve", lambda e: e.tensor_scalar(out=xs4[t], in0=xt3[xi], scalar1=rr_, scalar2=None, op0=ALU.mult),
                 reads=[xb], writes=[bx])

        def transposes(si, sb):
            for t in range(4):
                pt, ptb = (psT, B_ps[3]) if t % 2 == 0 else (psT2, B_ps[2])

                def ft(e, t=t, pt=pt):
                    ins = None
                    for k in range(8):
                        ins = e.transpose(out=pt[:, k, :], in_=xs4[t][:, k * 128:(k + 1) * 128], identity=ident)
                    return ins
                P.op("pe", ft, reads=[B_xs4[t], B_const], writes=[ptb])
                P.op("act", lambda e, t=t, pt=pt: e.activation(out=xnT[si][:, :, t * 128:(t + 1) * 128], in_=pt,
                                                               func=AF.Copy), reads=[ptb], writes=[sb])

        INV2PI = float(1.0 / TWO_PI)

        def rope_ops(possrc, ci, tslot):
            ctab_, stab_ = tabs[tslot]

            def reduce_(dst, shift):
                return [
                    lambda e: e.tensor_scalar(out=dst, in0=posf, scalar1=ropec[:, 0:1], scalar2=shift,
                                              op0=ALU.mult, op1=ALU.add),
                    lambda e: e.tensor_scalar(out=ki_t, in0=dst, scalar1=INV2PI, scalar2=None, op0=ALU.mult),
                    lambda e: e.tensor_copy(out=kf_t, in_=ki_t),
                    lambda e: e.scalar_tensor_tensor(out=dst, in0=kf_t, scalar=-TWO_PI, in1=dst,
                                                     op0=ALU.mult, op1=ALU.add),
                    lambda e: e.tensor_scalar(out=kf_t, in0=dst, scalar1=float(math.pi), scalar2=None, op0=ALU.is_gt),
                    lambda e: e.scalar_tensor_tensor(out=dst, in0=kf_t, scalar=-TWO_PI, in1=dst,
                                                     op0=ALU.mult, op1=ALU.add),
                    lambda e: e.tensor_scalar(out=dst, in0=dst, scalar1=float(-math.pi), scalar2=float(math.pi),
                                              op0=ALU.max, op1=ALU.min),
                ]
            ops = [lambda e: e.tensor_copy(out=posf, in_=posi)] + reduce_(tang, ropec[:, 1:2]) \
                + reduce_(tang2, float(0.5 * math.pi))

            def chain_all():
                P.dma("sp", lambda e: e.dma_start(
                    out=posi, in_=possrc[0:1, ci * 512:(ci + 1) * 512].partition_broadcast(128)),
                    B_posi, writes=[B_posi])
                chain(P, "dve", ops, [B_posi, B_const], [], B_rope)

            def sin_():
                def g(e):
                    e.activation(out=stab_, in_=tang, func=AF.Sin)
                    return e.activation(out=ctab_, in_=tang2, func=AF.Sin)
                P.op("act", g, reads=[B_rope], writes=[B_tabs[tslot]])
            return chain_all, sin_

        B_pq0a = P.buf("pq0a")
        B_pq0b = P.buf("pq0b")
        B_pq1a = P.buf("pq1a")
        B_pq1b = P.buf("pq1b")

        def proj_qk(W, si, sb, kslot, kb, tslot, hooks, mid):
            xT = xnT[si]
            ctab_, stab_ = tabs[tslot]
            for h in range(4):
                t1_, t2_ = t12[h % 2]
                bt = B_t12s[h % 2]
                if h % 2 == 0:
                    pa_ap, pb_ap, bufs = psum[0], psum[1], [B_ps[0], B_ps[1]]
                else:
                    pa_ap, pb_ap, bufs = psum2[0][:, 0:512], psum2[0][:, 512:1024], [B_pq0a, B_pq0b]

                def fm(e, h=h, pa_ap=pa_ap, pb_ap=pb_ap):
                    ins = None
                    for k in range(8):
                        ins = e.matmul(pa_ap, lhsT=W[:, k, h * 128:(h + 1) * 128], rhs=xT[:, k, :],
                                       start=(k == 0), stop=(k == 7))
                    for k in range(8):
                        ins = e.matmul(pb_ap, lhsT=W[:, k, 512 + h * 128:512 + (h + 1) * 128], rhs=xT[:, k, :],
                                       start=(k == 0), stop=(k == 7))
                    return ins
                P.op("pe", fm, reads=[sb, B_w], writes=bufs)

                def fr(e, pa_ap=pa_ap, pb_ap=pb_ap, t1_=t1_, t2_=t2_):
                    e.tensor_tensor(out=t1_, in0=pa_ap, in1=ctab_, op=ALU.mult)
                    return e.tensor_tensor(out=t2_, in0=pb_ap, in1=stab_, op=ALU.mult)
                P.op("dve", fr, reads=bufs + [B_tabs[tslot]], writes=[bt])
                P.op("pool", lambda e, h=h, t1_=t1_, t2_=t2_: e.tensor_tensor(out=kst[kslot][:, h, :], in0=t1_, in1=t2_,
                                                                            op=ALU.add),
                     reads=[bt], writes=[kb])
                if hooks:
                    hooks[h]()
            if mid:
                mid()
            sbk = [(psum[0], B_ps[0]), (psum[1], B_ps[1]), (psum2[0][:, 0:512], B_pq0a),
                   (psum2[0][:, 512:1024], B_pq0b)]
            for u in range(4):
                pz, pzb = sbk[u]

                def fm(e, u=u, pz=pz):
                    ins = None
                    for k in range(8):
                        ins = e.matmul(pz, lhsT=W[:, k, 1024 + u * 128:1024 + (u + 1) * 128], rhs=xT[:, k, :],
                                       start=(k == 0), stop=(k == 7))
                    return ins
                P.op("pe", fm, reads=[sb, B_w], writes=[pzb])
                P.op("act", lambda e, u=u, pz=pz: e.activation(out=kst[kslot][:, 4 + u, :], in_=pz, func=AF.Copy),
                     reads=[pzb], writes=[kb])

        def proj_v(si, sb, vslot, vb):
            xT = xnT[si]
            for t in range(4):
                for hf in range(2):
                    pv, pvbuf = (psum2[1][:, 0:512], B_pq1a) if hf == 0 else (psum2[1][:, 512:1024], B_pq1b)

                    def fm(e, t=t, hf=hf, pv=pv):
                        ins = None
                        for k in range(8):
                            ins = e.matmul(pv, lhsT=xT[:, k, t * 128:(t + 1) * 128],
                                           rhs=Wk[:, k, 1536 + hf * 512:1536 + (hf + 1) * 512],
                                           start=(k == 0), stop=(k == 7))
                        return ins
                    P.op("pe", fm, reads=[sb, B_w], writes=[pvbuf])
                    if hf == 0:
                        P.op("act", lambda e, t=t, hf=hf, pv=pv: e.activation(
                            out=vst[vslot][:, hf * 4:(hf + 1) * 4, t, :],
                            in_=pv.rearrange("p (u d) -> p u d", u=4), func=AF.Copy),
                            reads=[pvbuf], writes=[vb])
                    else:
                        P.op("dve", lambda e, t=t, hf=hf, pv=pv: e.tensor_copy(
                            out=vst[vslot][:, hf * 4:(hf + 1) * 4, t, :],
                            in_=pv.rearrange("p (u d) -> p u d", u=4)),
                            reads=[pvbuf], writes=[vb])

        def store_units(stage_ap, sbuf_, dst, dbuf, ci):
            P.dma("pool", lambda e: e.dma_start(
                out=dst[:, :, ci * 512:(ci + 1) * 512].rearrange("u p t -> p u t"), in_=stage_ap),
                sbuf_, reads=[sbuf_], writes=[dbuf])

        NKV = int(os.environ.get('KNKV', str(NCH_ALL)))
        jobs = [("kv", x_all, pos_all, ci) for ci in range(NKV)] + \
               [("q", x_own, pos_own, ci) for ci in range(NCH_OWN)]
        if MAXPH < 1:
            jobs = jobs[:0]
        if jobs:
            mode0, xsrc0, possrc0, ci0 = jobs[0]
            info0 = prep_stage_a(xsrc0, ci0)
            for t in range(4):
                prep_stage_b(info0, t)
            rc, rs_ = rope_ops(possrc0, ci0, 0)
            rc()
            rs_()
            nslot = xnT_ring.next()
            transposes(*nslot)
        for ji, (mode, xsrc, possrc, ci) in enumerate(jobs):
            si, sb = nslot
            tslot = ji % 2
            has_next = ji + 1 < len(jobs)
            hooks = None
            if has_next:
                nmode, nxsrc, npossrc, nci = jobs[ji + 1]
                ninfo = prep_stage_a(nxsrc, nci)
                hooks = [(lambda h=h, ninfo=ninfo: prep_stage_b(ninfo, h)) for h in range(4)]
                rc, rs_ = rope_ops(npossrc, nci, (ji + 1) % 2)
            ks, kb = kst_ring.next()
            W = Wk if mode == "kv" else Wq
            proj_qk(W, si, sb, ks, kb, tslot, hooks, (rc if has_next else None))
            store_units(kst[ks], kb, kT_scr if mode == "kv" else q_scr, B_kT if mode == "kv" else B_q, ci)
            if has_next:
                nslot = xnT_ring.next()
                transposes(*nslot)
            if mode == "kv":
                vs, vb = vst_ring.next()
                proj_v(si, sb, vs, vb)
            if has_next:
                rs_()
            if mode == "kv":
                store_units(vst[vs].rearrange("p u t d -> p u (t d)"), vb, v_scr, B_v, ci)

        P.barrier()
        abf.reset(m_bf)
        af.reset(m_f)

        KT = [abf.get(S) for _ in range(2)]
        VV = [abf.get(128 * 130, [128, 130]) for _ in range(2)]
        QT = [abf.get(OWN) for _ in range(2)]
        B_KT = [P.buf("KT0"), P.buf("KT1")]
        B_VV = [P.buf("VV0"), P.buf("VV1")]
        B_QT = [P.buf("QT0"), P.buf("QT1")]
        for s_ in range(2):
            P.op("pool", lambda e, s_=s_: e.memset(VV[s_][:, :, 128:130], 1.0), writes=[B_VV[s_]])
        e_t = [af.get(1024, [2, 512]) for _ in range(3)]
        e_ring = Ring(P, "e", 3)
        L_t = [abf.get(1024, [2, 512]) for _ in range(2)]
        L_ring = Ring(P, "L", 2)
        g_t = [abf.get(1024, [2, 512]) for _ in range(2)]
        g_ring = Ring(P, "g", 2)
        a_t = [abf.get(1024, [2, 512]) for _ in range(3)]
        a_ring = Ring(P, "a", 3)
        yst = [abf.get(512) for _ in range(2)]
        yst_ring = Ring(P, "yst", 2)
        accs = af.get(8 * 129, [8, 129])
        B_accs = P.buf("accs")
        o_t = af.get(128)
        t1d = af.get(128)
        ya = abf.get(128)
        junk2 = abf.get(128)
        B_ep = P.buf("ep")
        onec = small[:, 16:17]
        P.op("pool", lambda e: e.memset(onec, 1.0), writes=[B_small])

        bg_sf = [af.get(1024), af.get(1024)]
        bg_sb = [abf.get(1024), abf.get(1024)]
        B_bgf = [P.buf("bgf0"), P.buf("bgf1")]
        B_bgb = [P.buf("bgb0"), P.buf("bgb1")]
        bg_pieces = []
        for name, src, K, N, gc in (("wo", wout_d, D, 1024, None), ("wxq", wxq_d, D, 1024, 8),
                                    ("wxo", wxo_d, D, 1024, None), ("wkv", wxkv_d, D, 2048, 24),
                                    ("wup", wup_d, D, 4096, 16), ("wdn", wdn_d, 4 * D, 1024, None)):
            for k in range(K // 128):
                for n0 in range(0, N, 1024):
                    bg_pieces.append((name, src, k, n0, gc))
        bg_state = {"i": 0}

        def bg_load(i):
            name, src, k, n0, gc = bg_pieces[i]
            P.dma("sp", lambda e: e.dma_start(out=bg_sf[i % 2], in_=src[k * 128:(k + 1) * 128, n0:n0 + 1024]),
                  B_bgf[i % 2], writes=[B_bgf[i % 2]])

        def bg_conv(j):
            name, src, k, n0, gc = bg_pieces[j]
            if gc is None:
                P.op("dve", lambda e: e.tensor_copy(out=bg_sb[j % 2], in_=bg_sf[j % 2]),
                     reads=[B_bgf[j % 2]], writes=[B_bgb[j % 2]])
            else:
                P.op("dve", lambda e: e.tensor_scalar(out=bg_sb[j % 2], in0=bg_sf[j % 2],
                                                      scalar1=gcols[:, gc + k:gc + k + 1], scalar2=None,
                                                      op0=ALU.mult),
                     reads=[B_bgf[j % 2], B_const], writes=[B_bgb[j % 2]])
            P.dma("pool", lambda e: e.dma_start(out=wbf[name][:, k, n0:n0 + 1024], in_=bg_sb[j % 2]),
                  B_bgb[j % 2], reads=[B_bgb[j % 2]], writes=[B_wbf])

        def bg_step():
            i = bg_state["i"]
            if i > len(bg_pieces):
                return
            bg_state["i"] = i + 1
            if i >= 1:
                bg_conv(i - 1)
            if i < len(bg_pieces):
                bg_load(i)

        Zs = [psum2[i][:].rearrange("p (h t) -> p h t", h=2) for i in range(2)]
        B_Z = B_pq

        def load_unit(u, slot):
            for i in range(4):
                P.dma("sp", lambda e, i=i: e.dma_start(out=KT[slot][:, i * 4096:(i + 1) * 4096],
                                                       in_=kT_scr[u, :, i * 4096:(i + 1) * 4096]),
                      B_KT[slot], reads=[B_kT], writes=[B_KT[slot]])
            for i in range(4):
                P.dma("sp", lambda e, i=i: e.dma_start(
                    out=VV[slot][:, i * 32:(i + 1) * 32, 0:128],
                    in_=v_scr[u, :, i * 4096:(i + 1) * 4096].rearrange("p (t d) -> p t d", d=128)),
                    B_VV[slot], reads=[B_v], writes=[B_VV[slot]])
            P.dma("sp", lambda e: e.dma_start(out=QT[slot], in_=q_scr[u, :, :]),
                  B_QT[slot], reads=[B_q], writes=[B_QT[slot]])

        def tile_list(order, fine=False):
            tl = []
            for mc in range(4):
                kts = list(range(32 * mc + 32))
                if order == "bwd":
                    kts = kts[::-1]
                for j, kt in enumerate(kts):
                    d = kt - 32 * mc
                    c0 = 128 * (d // 8) if d >= 0 else 0
                    cf = 16 * d if d >= 0 else 0
                    if fine:
                        c0 = cf
                    tl.append(dict(mc=mc, kt=kt, d=d, c0=c0, cf=cf, first=(j == 0), last=(j == len(kts) - 1)))
            return tl

        def score_mm(slot, tinfo, Zt, Zb):
            mc, kt, c0 = tinfo["mc"], tinfo["kt"], tinfo["cf"]

            def f(e):
                e.matmul(Zt[:, 0, c0:512], lhsT=KT[slot][0:64, kt * 128:(kt + 1) * 128],
                         rhs=QT[slot][0:64, mc * 512 + c0:(mc + 1) * 512], start=True, stop=True,
                         skip_group_check=True)
                return e.matmul(Zt[:, 1, c0:512], lhsT=KT[slot][64:128, kt * 128:(kt + 1) * 128],
                                rhs=QT[slot][64:128, mc * 512 + c0:(mc + 1) * 512], start=True, stop=True,
                                skip_group_check=True)
            P.op("pe", f, reads=[B_KT[slot], B_QT[slot]], writes=[Zb])

        def diag_mask(tinfo, buf_ap, bbuf, mask):
            d, c0 = tinfo["d"], tinfo["c0"]
            if d < 0:
                return

            def f(e):
                if 16 * d > c0:
                    e.memset(buf_ap[:, :, c0:16 * d], 0.0)
                e.tensor_tensor(out=buf_ap[:, 0, 16 * d:16 * d + 16], in0=buf_ap[:, 0, 16 * d:16 * d + 16],
                                in1=mask, op=ALU.mult)
                return e.tensor_tensor(out=buf_ap[:, 1, 16 * d:16 * d + 16], in0=buf_ap[:, 1, 16 * d:16 * d + 16],
                                       in1=mask, op=ALU.mult)
            P.op("pool", f, reads=[B_const], writes=[bbuf])

        def acc_ap(j):
            return psum[j // 3][:, (j % 3) * 129:(j % 3) * 129 + 129]
        B_acc = None
        BACC = [B_ps[0], B_ps[1], B_ps[2]]

        def da_unit(u, slot):
            tl = tile_list("fwd")
            n = len(tl)
            pslots = {}

            def S_(i):
                score_mm(slot, tl[i], Zs[i % 2], B_Z[i % 2])

            def P_(i):
                ti = tl[i]
                c0 = ti["cf"]
                pi, pb = a_ring.next()
                pslots[i] = (pi, pb)
                P.op("act", lambda e: e.activation(out=a_t[pi][:, :, c0:512], in_=Zs[i % 2][:, :, c0:512],
                                                   func=AF.Exp, scale=0.125),
                     reads=[B_Z[i % 2]], writes=[pb])
                diag_mask(ti, a_t[pi], pb, mask_da)

            def PV_(i):
                ti = tl[i]
                pi, pb = pslots[i]
                c0, kt = ti["c0"], ti["kt"]

                def f(e):
                    ins = None
                    for r in range(c0 // 128, 4):
                        for comp in range(2):
                            j = r * 2 + comp
                            st = ti["first"] and (j % 3 == 0)
                            ins = e.matmul(acc_ap(j), lhsT=a_t[pi][:, comp, r * 128:(r + 1) * 128],
                                           rhs=VV[slot][:, kt, 0:129], start=st, stop=False,
                                           skip_group_check=True)
                    return ins
                P.op("pe", f, reads=[pb, B_VV[slot]], writes=BACC)
                if ti["last"]:
                    da_epilogue(u, ti["mc"])

            S_(0)
            for i in range(n):
                if i + 1 < n:
                    S_(i + 1)
                P_(i)
                PV_(i)
                if i % 20 == 10:
                    bg_step()

        def da_epilogue(u, mc):
            def fc(e):
                ins = None
                for b in range(3):
                    nacc = 3 if b < 2 else 2
                    ins = e.tensor_copy(out=accs[:, b * 3:b * 3 + nacc, :],
                                        in_=psum[b][:, 0:nacc * 129].rearrange("p (a c) -> p a c", c=129))
                return ins
            P.op("dve", fc, reads=BACC, writes=[B_accs])
            yi, yb = yst_ring.next()
            for r in range(4):
                a1 = accs[:, 2 * r, :]
                a2 = accs[:, 2 * r + 1, :]
                chain(P, "dve", [
                    lambda e, a1=a1: e.reciprocal(out=small[:, 20:21], in_=a1[:, 128:129]),
                    lambda e, a2=a2: e.reciprocal(out=small[:, 21:22], in_=a2[:, 128:129]),
                    lambda e: e.tensor_tensor(out=small[:, 21:22], in0=small[:, 21:22], in1=nlam, op=ALU.mult),
                    lambda e, a1=a1: e.tensor_scalar(out=t1d, in0=a1[:, 0:128], scalar1=small[:, 20:21], scalar2=None,
                                                    op0=ALU.mult),
                    lambda e, a2=a2: e.scalar_tensor_tensor(out=o_t, in0=a2[:, 0:128], scalar=small[:, 21:22], in1=t1d,
                                                           op0=ALU.mult, op1=ALU.add),
                    lambda e: e.memset(small[:, 22:23], 0.0),
                ], [B_accs, B_small], [], B_ep)
                chain(P, "act", [
                    lambda e: e.activation(out=junk2, in_=o_t, func=AF.Square, accum_out=small[:, 22:23]),
                    lambda e: e.activation(out=small[:, 23:24], in_=small[:, 22:23], func=AF.Ln, bias=epsc,
                                           scale=1.0 / 128.0),
                    lambda e: e.activation(out=small[:, 23:24], in_=small[:, 23:24], func=AF.Exp, scale=-0.5),
                ], [B_small], [], B_ep)
                P.op("dve", lambda e: e.scalar_tensor_tensor(out=ya, in0=o_t, scalar=small[:, 23:24], in1=gsub,
                                                             op0=ALU.mult, op1=ALU.mult),
                     reads=[B_ep, B_const], writes=[B_ep])
                P.op("pe", lambda e: e.transpose(out=psT[:, 0, :], in_=ya, identity=ident),
                     reads=[B_ep, B_const], writes=[B_psT])
                P.op("act", lambda e, r=r: e.activation(out=yst[yi][:, r * 128:(r + 1) * 128], in_=psT[:, 0, :],
                                                        func=AF.Copy), reads=[B_psT], writes=[yb])
            P.dma("pool", lambda e: e.dma_start(out=y_scr[u, :, mc * 512:(mc + 1) * 512], in_=yst[yi]),
                  yb, reads=[yb], writes=[B_y])

        Xb = [psum[0], psum[1]]
        Ob = [psum[2], psum[3]]
        BX = [B_ps[0], B_ps[1]]
        BO = [B_ps[2], B_ps[3]]
        Xpair = pA[:, :].rearrange("p (h t) -> p h t", h=2)

        def sb_unit(u, slot):
            tl = tile_list("bwd", fine=True)
            n = len(tl)
            es_, Ls_, gs_, as_ = {}, {}, {}, {}

            def Z_(i):
                score_mm(slot, tl[i], Zs[i % 2], B_Z[i % 2])

            def E_(i):
                ti = tl[i]
                c0 = ti["c0"]
                k, b = e_ring.next()
                es_[i] = (k, b)
                P.op("act", lambda e: e.activation(out=e_t[k][:, :, c0:512], in_=Zs[i % 2][:, :, c0:512],
                                                   func=AF.Exp, scale=0.125),
                     reads=[B_Z[i % 2]], writes=[b])
                diag_mask(ti, e_t[k], b, mask_sb)

            def L_(i):
                c0 = tl[i]["c0"]
                k, b = es_[i]
                lk, lb = L_ring.next()
                Ls_[i] = (lk, lb)
                P.op("act", lambda e: e.activation(out=L_t[lk][:, :, c0:512], in_=e_t[k][:, :, c0:512],
                                                   func=AF.Ln, bias=onec, scale=1.0),
                     reads=[b, B_small], writes=[lb])

            def TRI_(i, mat):
                ti = tl[i]
                c0 = ti["c0"]
                lk, lb = Ls_[i]
                st = ti["first"] and (mat is tri)

                def f(e):
                    e.matmul(Xb[0][:, c0:512], lhsT=mat, rhs=L_t[lk][:, 0, c0:512], start=st, stop=False,
                             skip_group_check=True)
                    return e.matmul(Xb[1][:, c0:512], lhsT=mat, rhs=L_t[lk][:, 1, c0:512], start=st, stop=False,
                                    skip_group_check=True)
                P.op("pe", f, reads=[lb, B_const], writes=BX)

            def G_(i):
                c0 = tl[i]["c0"]
                gk, gb = g_ring.next()
                gs_[i] = (gk, gb)

                P.op("act", lambda e: e.activation(out=g_t[gk][:, :, c0:512], in_=Xpair[:, :, c0:512],
                                                   func=AF.Exp, scale=-1.0), reads=BX, writes=[gb])

            def A_(i):
                c0 = tl[i]["c0"]
                k, b = es_[i]
                gk, gb = gs_[i]
                ak, ab = a_ring.next()
                as_[i] = (ak, ab)
                P.op("dve", lambda e: e.tensor_tensor(out=a_t[ak][:, :, c0:512], in0=e_t[k][:, :, c0:512],
                                                      in1=g_t[gk][:, :, c0:512], op=ALU.mult),
                     reads=[b, gb], writes=[ab])

            def AV_(i):
                ti = tl[i]
                c0, kt = ti["c0"], ti["kt"]
                ak, ab = as_[i]
                st = ti["first"]

                def f(e):
                    e.matmul(Ob[0][:, c0:512], lhsT=VV[slot][:, kt, 0:128], rhs=a_t[ak][:, 0, c0:512],
                             start=st, stop=False, skip_group_check=True)
                    return e.matmul(Ob[1][:, c0:512], lhsT=VV[slot][:, kt, 0:128], rhs=a_t[ak][:, 1, c0:512],
                                    start=st, stop=False, skip_group_check=True)
                P.op("pe", f, reads=[ab, B_VV[slot]], writes=BO)
                if ti["last"]:
                    yi, yb = yst_ring.next()
                    mc = ti["mc"]
                    P.op("act", lambda e: e.activation(out=yst[yi][0:64, :], in_=Ob[0][0:64, :], func=AF.Copy),
                         reads=BO, writes=[yb])
                    P.op("dve", lambda e: e.tensor_copy(out=yst[yi][64:128, :], in_=Ob[1][64:128, :]),
                         reads=BO, writes=[yb])
                    P.dma("pool", lambda e: e.dma_start(out=y_scr[u, :, mc * 512:(mc + 1) * 512], in_=yst[yi]),
                          yb, reads=[yb], writes=[B_y])

            Z_(0)
            if n > 1:
                Z_(1)
            E_(0)
            for i in range(n):
                L_(i)
                TRI_(i, tri)
                if i + 2 < n:
                    Z_(i + 2)
                if i + 1 < n:
                    E_(i + 1)
                if i - 1 >= 0:
                    AV_(i - 1)
                G_(i)
                TRI_(i, slt)
                A_(i)
                if i % 20 == 10:
                    bg_step()
            AV_(n - 1)

        unit_order = [4, 0, 5, 1, 6, 2, 7, 3][:NUNITS] if MAXPH >= 2 else []
        if unit_order:
            load_unit(unit_order[0], 0)
        for ui, u in enumerate(unit_order):
            slot = ui % 2
            if ui + 1 < len(unit_order):
                load_unit(unit_order[ui + 1], (ui + 1) % 2)
            if u < 4:
                da_unit(u, slot)
            else:
                sb_unit(u, slot)
        while bg_state["i"] <= len(bg_pieces):
            bg_step()

        P.barrier()
        abf.reset(m_bf)
        af.reset(m_f)
        if debug:
            for u in range(8):
                P.dma("sp", lambda e, u=u: e.dma_start(out=dbg["kT"][u, :, :], in_=kT_scr[u, :, 0:1024]),
                      B_const, reads=[B_kT])
                P.dma("sp", lambda e, u=u: e.dma_start(out=dbg["v"][u, :, :], in_=v_scr[u, :, 0:1024]),
                      B_const, reads=[B_v])
                P.dma("sp", lambda e, u=u: e.dma_start(out=dbg["q"][u, :, :], in_=q_scr[u, :, :]),
                      B_const, reads=[B_q])
                P.dma("sp", lambda e, u=u: e.dma_start(out=dbg["y"][u, :, :], in_=y_scr[u, :, :]),
                      B_const, reads=[B_y])

        Wo = abf.get(8 * 1024, [8, 1024])
        Wxq = abf.get(8 * 1024, [8, 1024])
        Wxo = abf.get(8 * 1024, [8, 1024])
        Wkv = abf.get(8 * 2048, [8, 2048])
        B_wl = P.buf("wload")
        qrr = {"i": 0}

        B_wo, B_wxq, B_wxo, B_wkv, B_wup, B_wdn = [P.buf(n_, multi=True) for n_ in ("wo", "wxq", "wxo", "wkv", "wup", "wdn")]

        def load_bf(dst, name, nk, N, wb_):
            flat = dst.rearrange("p k n -> p (k n)")
            srcf = wbf[name].rearrange("p k n -> p (k n)")
            tot = nk * N
            step = 4096
            for o in range(0, tot, step):
                q = ("sp", "act", "pool")[qrr["i"] % 3]
                qrr["i"] += 1
                if not hasattr(wb_, "ld_owner"):
                    wb_.ld_owner = P.buf("wl_" + name)
                P.dma(q, lambda e, o=o: e.dma_start(out=flat[:, o:o + step], in_=srcf[:, o:o + step]),
                      wb_.ld_owner, reads=[B_wbf], writes=[wb_])
        load_bf(Wkv, "wkv", 8, 2048, B_wkv)
        load_bf(Wo, "wo", 8, 1024, B_wo)
        load_bf(Wxq, "wxq", 8, 1024, B_wxq)
        load_bf(Wxo, "wxo", 8, 1024, B_wxo)

        memT = abf.get(8 * 256, [8, 256])
        B_memT = P.buf("memT")
        kxT = abf.get(8 * 256, [8, 256])
        vx = abf.get(2 * 1024, [2, 1024])
        B_kx = P.buf("kx", multi=True)
        junk = abf.get(1024)
        xs = abf.get(1024)
        scr = (junk, ssr, rrr, xs, B_scr)
        hbuf = [af.get(4 * 1024, [4, 1024]) for _ in range(1)]
        B_hb = P.buf("hbuf", multi=True)
        xt_ring2 = Ring(P, "xtm", 2)
        mt = [hbuf[0][:, 0, :], hbuf[0][:, 1, :]]
        B_hc = P.buf("hchunk")
        for t in range(2):
            P.dma("sp", lambda e, t=t: e.dma_start(out=mt[t], in_=mem_d[t * 128:(t + 1) * 128, :]), B_hc,
                  writes=[B_hc])
            norm_transpose(mt[t], B_hc, memT, B_memT, t * 128, scr, psT, B_psT, "mem")
        for ct in range(8):
            def fm(e, ct=ct):
                ins = None
                for k in range(8):
                    ins = e.matmul(psum[0][:, 0:256], lhsT=Wkv[:, k, ct * 128:(ct + 1) * 128], rhs=memT[:, k, :],
                                   start=(k == 0), stop=(k == 7))
                return ins
            P.op("pe", fm, reads=[B_memT, B_wkv], writes=[B_ps[0]])
            P.op("act", lambda e, ct=ct: e.activation(out=kxT[:, ct, :], in_=psum[0][:, 0:256], func=AF.Copy),
                 reads=[B_ps[0]], writes=[B_kx])
        for t in range(2):
            for hf in range(2):
                def fm(e, t=t, hf=hf):
                    ins = None
                    for k in range(8):
                        ins = e.matmul(psum[1], lhsT=memT[:, k, t * 128:(t + 1) * 128],
                                       rhs=Wkv[:, k, 1024 + hf * 512:1024 + (hf + 1) * 512],
                                       start=(k == 0), stop=(k == 7))
                    return ins
                P.op("pe", fm, reads=[B_memT, B_wkv], writes=[B_ps[1]])
                P.op("act", lambda e, t=t, hf=hf: e.activation(out=vx[:, t, hf * 512:(hf + 1) * 512], in_=psum[1],
                                                               func=AF.Copy), reads=[B_ps[1]], writes=[B_kx])

        yT = abf.get(8 * 512, [8, 512])
        B_yT = P.buf("yT")
        hnT = abf.get(8 * 512, [8, 512])
        B_hnT = P.buf("hnT")
        qxT = abf.get(8 * 512, [8, 512])
        B_qxT = P.buf("qxT")
        pT_s = [abf.get(1024, [2, 512]) for _ in range(2)]
        B_pT_s = [P.buf("pT0"), P.buf("pT1")]
        rb_s = [af.get(512) for _ in range(2)]
        B_rb_s = [P.buf("rb0"), P.buf("rb1")]
        oT = abf.get(8 * 512, [8, 512])
        B_oT = P.buf("oT")
        hb = hbuf[0]

        for mc in range(4 if MAXPH >= 3 else 0):
            P.dma("sp", lambda e, mc=mc: e.dma_start(
                out=yT, in_=y_scr[:, :, mc * 512:(mc + 1) * 512].rearrange("u p t -> p u t")),
                B_yT, reads=[B_y], writes=[B_yT])
            P.dma("sp", lambda e, mc=mc: e.dma_start(
                out=hb, in_=x_own[mc * 512:(mc + 1) * 512, :].rearrange("(t p) d -> p t d", p=128)),
                B_hc, writes=[B_hc])
            for t in range(4):
                for hf in range(2):
                    pv, pvb = (psum[2], B_ps[2]) if hf == 0 else (psum[3], B_ps[3])

                    def fm(e, t=t, hf=hf, pv=pv):
                        ins = None
                        for k in range(8):
                            ins = e.matmul(pv, lhsT=yT[:, k, t * 128:(t + 1) * 128],
                                           rhs=Wo[:, k, hf * 512:(hf + 1) * 512], start=(k == 0), stop=(k == 7))
                        return ins
                    P.op("pe", fm, reads=[B_yT, B_wo], writes=[pvb])
                    P.op("dve", lambda e, t=t, hf=hf, pv=pv: e.tensor_tensor(
                        out=hb[:, t, hf * 512:(hf + 1) * 512], in0=pv, in1=hb[:, t, hf * 512:(hf + 1) * 512],
                        op=ALU.add), reads=[pvb], writes=[B_hc])
            for t in range(4):
                norm_transpose(hb[:, t, :], B_hc, hnT, B_hnT, t * 128, scr, psT, B_psT, "c")
            for ct in range(8):
                pv, pvb = (psum[0], B_ps[0]) if ct % 2 == 0 else (psum[1], B_ps[1])

                def fm(e, ct=ct, pv=pv):
                    ins = None
                    for k in range(8):
                        ins = e.matmul(pv, lhsT=Wxq[:, k, ct * 128:(ct + 1) * 128], rhs=hnT[:, k, :],
                                       start=(k == 0), stop=(k == 7))
                    return ins
                P.op("pe", fm, reads=[B_hnT, B_wxq], writes=[pvb])
                P.op("act", lambda e, ct=ct, pv=pv: e.activation(out=qxT[:, ct, :], in_=pv, func=AF.Copy),
                     reads=[pvb], writes=[B_qxT])
            for hh in range(4):
                Zt = Zs[hh % 2]
                pT = pT_s[hh % 2]
                B_pT = B_pT_s[hh % 2]
                rb = rb_s[hh % 2]
                B_rb = B_rb_s[hh % 2]

                def fs(e, hh=hh, Zt=Zt):
                    ins = None
                    for kt in range(2):
                        for j in range(2):
                            ins = e.matmul(Zt[:, kt, :], lhsT=kxT[:, 2 * hh + j, kt * 128:(kt + 1) * 128],
                                           rhs=qxT[:, 2 * hh + j, :], start=(j == 0), stop=(j == 1))
                    return ins
                P.op("pe", fs, reads=[B_kx, B_qxT], writes=[B_Z[hh % 2]])
                P.op("act", lambda e, Zt=Zt, pT=pT: e.activation(out=pT, in_=Zt, func=AF.Exp, scale=1.0 / 16.0),
                     reads=[B_Z[hh % 2]], writes=[B_pT])

                def fl(e, pT=pT):
                    e.matmul(psum[2], lhsT=ones_bf, rhs=pT[:, 0, :], start=True, stop=False)
                    return e.matmul(psum[2], lhsT=ones_bf, rhs=pT[:, 1, :], start=False, stop=True)
                P.op("pe", fl, reads=[B_pT, B_const], writes=[B_ps[2]])
                P.op("dve", lambda e, rb=rb: e.reciprocal(out=rb, in_=psum[2]), reads=[B_ps[2]], writes=[B_rb])
                for j in range(2):
                    pv, pvb = (psum[3], B_ps[3]) if j == 0 else (psum[0], B_ps[0])

                    def fo(e, hh=hh, j=j, pv=pv, pT=pT):
                        c = (2 * hh + j) * 128
                        e.matmul(pv, lhsT=vx[:, 0, c:c + 128], rhs=pT[:, 0, :], start=True, stop=False)
                        return e.matmul(pv, lhsT=vx[:, 1, c:c + 128], rhs=pT[:, 1, :], start=False, stop=True)
                    P.op("pe", fo, reads=[B_pT, B_kx], writes=[pvb])
                    P.op("dve", lambda e, hh=hh, j=j, pv=pv, rb=rb: e.tensor_tensor(out=oT[:, 2 * hh + j, :], in0=pv, in1=rb,
                                                                             op=ALU.mult),
                         reads=[pvb, B_rb], writes=[B_oT])
            for t in range(4):
                for hf in range(2):
                    pv, pvb = (psum[1], B_ps[1]) if hf == 0 else (psum[2], B_ps[2])

                    def fm(e, t=t, hf=hf, pv=pv):
                        ins = None
                        for k in range(8):
                            ins = e.matmul(pv, lhsT=oT[:, k, t * 128:(t + 1) * 128],
                                           rhs=Wxo[:, k, hf * 512:(hf + 1) * 512], start=(k == 0), stop=(k == 7))
                        return ins
                    P.op("pe", fm, reads=[B_oT, B_wxo], writes=[pvb])
                    P.op("dve", lambda e, t=t, hf=hf, pv=pv: e.tensor_tensor(
                        out=hb[:, t, hf * 512:(hf + 1) * 512], in0=pv, in1=hb[:, t, hf * 512:(hf + 1) * 512],
                        op=ALU.add), reads=[pvb], writes=[B_hc])
            P.dma("pool", lambda e, mc=mc: e.dma_start(
                out=h_scr[mc * 512:(mc + 1) * 512, :].rearrange("(t p) d -> p t d", p=128), in_=hb),
                B_hc, reads=[B_hc], writes=[B_h])

        P.barrier()
        abf.reset(m_bf)
        af.reset(m_f)
        if debug:
            P.dma("sp", lambda e: e.dma_start(out=dbg["h"][:, :], in_=h_scr[:, :]), B_const, reads=[B_h])

        Wup = abf.get(8 * 4096, [8, 4096])
        Wdn = abf.get(32 * 1024, [32, 1024])
        load_bf(Wup, "wup", 8, 4096, B_wup)
        load_bf(Wdn, "wdn", 32, 1024, B_wdn)
        gfin = af.get(1024)
        ld_const(gfin, gfin_d[:, :])
        junk = abf.get(1024)
        xs = abf.get(1024)
        scr = (junk, ssr, rrr, xs, B_scr)
        hb2 = [af.get(2 * 1024, [2, 1024]) for _ in range(2)]
        hb2_ring = Ring(P, "hb2", 2)
        hn2 = [abf.get(8 * 256, [8, 256]) for _ in range(1)]
        hn2_ring = Ring(P, "hn2", 1)
        aT = abf.get(32 * 256, [32, 256])
        B_aT = P.buf("aT", multi=True)
        rl = [af.get(256) for _ in range(2)]
        rl_ring = Ring(P, "rl", 2)
        B_fin = P.buf("fin")

        for c2 in range(OWN // 256 if MAXPH >= 4 else 0):
            hi, hbb = hb2_ring.next()
            hbc = hb2[hi]
            P.dma("sp", lambda e, c2=c2, hbc=hbc: e.dma_start(
                out=hbc, in_=h_scr[c2 * 256:(c2 + 1) * 256, :].rearrange("(t p) d -> p t d", p=128)),
                hbb, reads=[B_h], writes=[hbb])
            ni, nb = hn2_ring.next()
            for t in range(2):
                norm_transpose(hbc[:, t, :], hbb, hn2[ni], nb, t * 128, scr, psT, B_psT, "m")
            for fc in range(32):
                pv, pvb = (psum[fc % 2][:, 0:256], B_ps[fc % 2])

                def fm(e, fc=fc, pv=pv, ni=ni):
                    ins = None
                    for k in range(8):
                        ins = e.matmul(pv, lhsT=Wup[:, k, fc * 128:(fc + 1) * 128], rhs=hn2[ni][:, k, :],
                                       start=(k == 0), stop=(k == 7))
                    return ins
                P.op("pe", fm, reads=[nb, B_wup], writes=[pvb])
                ri, rbuf = rl_ring.next()
                P.op("act", lambda e, pv=pv, ri=ri: e.activation(out=rl[ri], in_=pv, func=AF.Relu),
                     reads=[pvb], writes=[rbuf])
                P.op("pool", lambda e, fc=fc, ri=ri: e.tensor_tensor(out=aT[:, fc, :], in0=rl[ri], in1=rl[ri],
                                                                     op=ALU.mult),
                     reads=[rbuf], writes=[B_aT])
            for t in range(2):
                for hf in range(2):
                    pv, pvb = (psum[2], B_ps[2]) if hf == 0 else (psum[3], B_ps[3])

                    def fm(e, t=t, hf=hf, pv=pv):
                        ins = None
                        for fc in range(32):
                            ins = e.matmul(pv, lhsT=aT[:, fc, t * 128:(t + 1) * 128],
                                           rhs=Wdn[:, fc, hf * 512:(hf + 1) * 512], start=(fc == 0), stop=(fc == 31))
                        return ins
                    P.op("pe", fm, reads=[B_aT, B_wdn], writes=[pvb])
                    P.op("dve", lambda e, t=t, hf=hf, pv=pv, hbc=hbc: e.tensor_tensor(
                        out=hbc[:, t, hf * 512:(hf + 1) * 512], in0=pv, in1=hbc[:, t, hf * 512:(hf + 1) * 512],
                        op=ALU.add), reads=[pvb], writes=[hbb])
                P.op("pool", lambda e: e.memset(small[:, 30:31], 0.0), writes=[B_fin])
                chain(P, "act", [
                    lambda e, t=t, hbc=hbc: e.activation(out=junk, in_=hbc[:, t, :], func=AF.Square,
                                                         accum_out=small[:, 30:31]),
                    lambda e: e.activation(out=small[:, 31:32], in_=small[:, 30:31], func=AF.Ln, bias=epsc,
                                           scale=1.0 / D),
                    lambda e: e.activation(out=small[:, 31:32], in_=small[:, 31:32], func=AF.Exp, scale=-0.5),
                ], [hbb, B_small], [], B_fin)
                P.op("dve", lambda e, t=t, hbc=hbc: e.scalar_tensor_tensor(
                    out=hbc[:, t, :], in0=hbc[:, t, :], scalar=small[:, 31:32], in1=gfin, op0=ALU.mult, op1=ALU.mult),
                    reads=[B_fin, B_const], writes=[hbb, B_fin])
            P.dma("pool", lambda e, c2=c2, hbc=hbc: e.dma_start(
                out=out_d[c2 * 256:(c2 + 1) * 256, :].rearrange("(t p) d -> p t d", p=128), in_=hbc),
                hbb, reads=[hbb], writes=[B_out])

        P.barrier()
        block = es.enter_context(nc.Block())
        P.emit(nc, block, es)
    return nc


def _swap_cols(w):
    n = w.shape[1] // 64
    idx = np.arange(w.shape[1]).reshape(n, 64)
    idx2 = idx.copy()
    idx2[:, 0:8] = idx[:, 8:16]
    idx2[:, 8:16] = idx[:, 0:8]
    return w[:, idx2.reshape(-1)]


_NC_CACHE = {}


def make_in_maps(inputs):
    f32 = np.float32
    x = np.ascontiguousarray(np.asarray(inputs["x"], dtype=f32)[0])
    pos = np.ascontiguousarray(np.asarray(inputs["positions"]).astype(np.int32))
    w_in = np.asarray(inputs["w_in"], dtype=f32)[0]
    qa, ka, va = w_in[:, 0:512], w_in[:, 512:1024], w_in[:, 1024:1536]
    qs, ksb, vs = w_in[:, 1536:2048], w_in[:, 2048:2560], w_in[:, 2560:3072]
    wk = np.ascontiguousarray(np.concatenate([ka, _swap_cols(ka), ksb, va, vs], axis=1))
    wq = np.ascontiguousarray(np.concatenate([qa, _swap_cols(qa), qs], axis=1))

    def gcol(g):
        return np.asarray(g, dtype=f32).reshape(8, 128).T

    gcols = np.ascontiguousarray(np.concatenate(
        [gcol(inputs["g_mix"][0]), gcol(inputs["g_cross"][0]), gcol(inputs["g_mlp"][0]), gcol(inputs["g_mem"][0])],
        axis=1))
    gfin = np.ascontiguousarray(np.broadcast_to(np.asarray(inputs["g_final"], dtype=f32)[None, :], (128, D)))
    gsub = np.ascontiguousarray(np.broadcast_to(np.asarray(inputs["g_subln"], dtype=f32)[0][None, :], (128, 128)))
    lam = np.concatenate([np.asarray(inputs[k], dtype=f32)[0] for k in
                          ("lambda_q1", "lambda_k1", "lambda_q2", "lambda_k2")])
    lam = np.ascontiguousarray(np.broadcast_to(lam[None, :], (128, 256)))
    inv_freq = (500000.0 ** (-np.arange(0, 16, 2, dtype=np.float32) / 16.0)).astype(f32)
    ropec = np.zeros((128, 2), f32)
    for p in range(128):
        d = p % 64
        if d < 8:
            ropec[p, 0] = inv_freq[d]
            ropec[p, 1] = math.pi
        elif d < 16:
            ropec[p, 0] = inv_freq[d - 8]
            ropec[p, 1] = 0.0
        else:
            ropec[p, 0] = 0.0
            ropec[p, 1] = 0.0
    k_ = np.arange(128)[:, None]
    cbf = np.zeros((128, 512), f32)
    cbf[:, 0:128] = np.eye(128)
    cbf[:, 128:256] = (k_ >= np.arange(128)[None, :])
    cbf[:, 256:384] = (k_ < np.arange(128)[None, :])
    cbf[:, 384:512] = 1.0
    cbf = cbf.astype(ml_dtypes.bfloat16)
    common = dict(
        x_all=x, pos_all=pos, wk=wk, wq=wq,
        w_out=np.ascontiguousarray(np.asarray(inputs["w_out"], dtype=f32)[0]),
        w_xq=np.ascontiguousarray(np.asarray(inputs["w_xq"], dtype=f32)[0]),
        w_xkv=np.ascontiguousarray(np.asarray(inputs["w_xkv"], dtype=f32)[0]),
        w_xo=np.ascontiguousarray(np.asarray(inputs["w_xo"], dtype=f32)[0]),
        w_up=np.ascontiguousarray(np.asarray(inputs["w_up"], dtype=f32)[0]),
        w_down=np.ascontiguousarray(np.asarray(inputs["w_down"], dtype=f32)[0]),
        mem=np.ascontiguousarray(np.asarray(inputs["mem"], dtype=f32)[0]),
        gcols=gcols, gfin=gfin, gsub=gsub, lam=lam, ropec=ropec, cbf=cbf,
    )
    in_maps = []
    i_ = np.arange(16)[None, :]
    for c in range(NCORES):
        masks = np.zeros((128, 32), f32)
        masks[:, 0:16] = (k_ < 8 * i_ + c)
        masks[:, 16:32] = (k_ <= 8 * i_ + c)
        m = dict(common)
        m["x_own"] = np.ascontiguousarray(x[c::NCORES])
        m["pos_own"] = np.ascontiguousarray(pos[:, c::NCORES])
        m["masks"] = masks
        in_maps.append(m)
    return in_maps


def kernel(**inputs):
    in_maps = make_in_maps(inputs)
    if "nc" not in _NC_CACHE:
        _NC_CACHE["nc"] = build_program(debug=DEBUG)
    nc = _NC_CACHE["nc"]
    res = run_bass_kernel_spmd(nc, in_maps, core_ids=list(range(NCORES)))
    out = np.empty((1, S, D), np.float32)
    for c in range(NCORES):
        out[0, c::NCORES, :] = res.results[c]["out"]
    if DEBUG:
        kernel.last = res
    return out
```
